# Optimizing a Trainium2 kernel written in Bass

```python
import math
import jax
import jax.numpy as jnp
from jax import lax
import numpy as np


D_MODEL = 2048
BATCH = 4
SEQ = 4096
DEPTH = 1

HYENA_WIDTH = 1024
HYENA_ORDER = 2
HYENA_EMB = 33
HYENA_FILTER_HIDDEN = 64
HYENA_FAST_DECAY = 0.3
HYENA_SLOW_DECAY = 1.5
HYENA_TARGET = 1e-2
MLSTM_WIDTH = 1024
MLSTM_HEADS = 4
MLSTM_HEAD_DIM = MLSTM_WIDTH // MLSTM_HEADS
MLSTM_CHUNK = 128
N_MLSTM_GATES = 4 * MLSTM_HEADS
SHORT_CONV = 3
SHORT_CONV_CH = 3 * HYENA_WIDTH + 2 * MLSTM_WIDTH
IN_COLS = SHORT_CONV_CH + 2 * MLSTM_WIDTH + N_MLSTM_GATES + 2 * D_MODEL
FFN_HIDDEN = -(-8 * D_MODEL // (3 * 256)) * 256
RMS_EPS = 1e-6

kernel_name = 'hybrid_hyena_mlstm_encoder_block'


def rmsnorm(x, w):
    xf = x.astype(jnp.float32)
    y = xf * lax.rsqrt(jnp.mean(xf * xf, axis=-1, keepdims=True) + RMS_EPS)
    return (y * w.astype(jnp.float32)).astype(x.dtype)


def centred_depthwise_conv(u, w, b):
    K = w.shape[0]
    pad = K // 2
    L = u.shape[1]
    up = jnp.pad(u, ((0, 0), (pad, pad), (0, 0)))
    out = up[:, 0:L] * w[0] + b
    for j in range(1, K):
        out = out + up[:, j:j + L] * w[j]
    return out


def hyena_filters(L, w1, b1, f1, w2, b2, f2, w3):
    f32 = jnp.float32
    t = jnp.linspace(0.0, 1.0, L, dtype=f32)[:, None]
    bands = (HYENA_EMB - 1) // 2
    omega = 2.0 * math.pi * jnp.arange(L, dtype=f32) / L
    freqs = jnp.linspace(1e-4, bands - 1, bands, dtype=f32)
    ang = omega[:, None] * freqs[None, :]
    z = jnp.concatenate([t, jnp.cos(ang), -jnp.sin(ang)], axis=-1)
    h = jnp.sin(f1.astype(f32) * (z @ w1.astype(f32) + b1.astype(f32)))
    h = jnp.sin(f2.astype(f32) * (h @ w2.astype(f32) + b2.astype(f32)))
    h = h @ w3.astype(f32)
    max_decay = math.log(HYENA_TARGET) / HYENA_FAST_DECAY
    min_decay = math.log(HYENA_TARGET) / HYENA_SLOW_DECAY
    deltas = jnp.linspace(min_decay, max_decay, HYENA_WIDTH, dtype=f32)
    window = jnp.exp(-t * jnp.abs(deltas)[None, :])
    return h.reshape(L, HYENA_ORDER, 2, HYENA_WIDTH) * window[:, None, None, :]


def two_sided_spectrum(h):
    h_fwd = h[:, :, 0]
    h_bwd = h[:, :, 1]
    kern = jnp.concatenate([h_fwd, jnp.zeros_like(h_fwd[:1]), h_bwd[:0:-1]], axis=0)
    return jnp.fft.rfft(kern, axis=0)


def long_conv(u, kf, bias):
    L = u.shape[1]
    uf = jnp.fft.rfft(u, n=2 * L, axis=1)
    y = jnp.fft.irfft(uf * kf[None], n=2 * L, axis=1)[:, :L]
    return y + u * bias


def mlstm_direction(q, k, v, i_pre, f_pre):
    B, L, H, dh = q.shape
    nc = L // MLSTM_CHUNK

    def to_chunks(t):
        return t.reshape(B, nc, MLSTM_CHUNK, H, -1).transpose(0, 3, 1, 2, 4)

    qc = to_chunks(q)
    kc = to_chunks(k) * (dh ** -0.5)
    vc = to_chunks(v)
    ig = to_chunks(i_pre[..., None])[..., 0]
    logf = to_chunks(jax.nn.log_sigmoid(f_pre)[..., None])[..., 0]
    b = jnp.cumsum(logf, axis=-1)
    b_last = b[..., -1]
    a = b_last[..., None] - b + ig
    m_loc = jnp.max(a, axis=-1)

    def step(carry, xs):
        C, n, m = carry
        bl, a_c, ml, k_c, v_c = xs
        m_new = jnp.maximum(bl + m, ml)
        keep = jnp.exp(bl + m - m_new)
        w = jnp.exp(a_c - m_new[..., None])
        C_new = keep[..., None, None] * C + jnp.einsum('bhsv,bhsk->bhvk', v_c * w[..., None], k_c)
        n_new = keep[..., None] * n + jnp.einsum('bhs,bhsk->bhk', w, k_c)
        return (C_new, n_new, m_new), (C, n, m)

    xs = tuple(jnp.moveaxis(t, 2, 0) for t in (b_last, a, m_loc, kc, vc))
    init = (jnp.zeros((B, H, dh, dh), jnp.float32),
            jnp.zeros((B, H, dh), jnp.float32),
            jnp.zeros((B, H), jnp.float32))
    _, (C_prev, n_prev, m_prev) = lax.scan(step, init, xs)
    C_prev = jnp.moveaxis(C_prev, 0, 2)
    n_prev = jnp.moveaxis(n_prev, 0, 2)
    m_prev = jnp.moveaxis(m_prev, 0, 2)

    pos = jnp.arange(MLSTM_CHUNK)
    lower = pos[:, None] >= pos[None, :]
    dlog = b[..., :, None] - b[..., None, :] + ig[..., None, :]
    dlog = jnp.where(lower, dlog, -jnp.inf)
    inter = b + m_prev[..., None]
    m_j = jnp.maximum(inter, jnp.max(dlog, axis=-1))
    s = jnp.einsum('bhcjd,bhcsd->bhcjs', qc, kc) * jnp.exp(dlog - m_j[..., None])
    inter_w = jnp.exp(inter - m_j)
    num = (jnp.einsum('bhcjs,bhcsv->bhcjv', s, vc)
           + inter_w[..., None] * jnp.einsum('bhcjk,bhcvk->bhcjv', qc, C_prev))
    den = jnp.sum(s, axis=-1) + inter_w * jnp.einsum('bhcjk,bhck->bhcj', qc, n_prev)
    h = num / jnp.maximum(jnp.abs(den), jnp.exp(-m_j))[..., None]
    return h.transpose(0, 2, 3, 1, 4).reshape(B, L, H, dh)


def setup_inputs(seed: int = 0) -> dict:
    key = jax.random.key(seed)
    ks = jax.random.split(key, 24)
    f32 = jnp.float32

    def nrm(k, shape, scale):
        return jax.random.normal(k, shape, f32) * scale

    Ld = DEPTH
    H = MLSTM_HEADS
    FH = HYENA_FILTER_HIDDEN
    x = nrm(ks[0], (BATCH, SEQ, D_MODEL), 1.0)
    norm1_w = 1.0 + nrm(ks[1], (Ld, D_MODEL), 0.05)
    w_in = nrm(ks[2], (Ld, D_MODEL, IN_COLS), D_MODEL ** -0.5)
    conv_w = nrm(ks[3], (Ld, SHORT_CONV, SHORT_CONV_CH), SHORT_CONV ** -0.5)
    conv_b = nrm(ks[4], (Ld, SHORT_CONV_CH), 0.02)
    filt_w1 = nrm(ks[5], (Ld, HYENA_EMB, FH), HYENA_EMB ** -0.5)
    filt_b1 = nrm(ks[6], (Ld, FH), 0.1)
    filt_freq1 = 1.0 + nrm(ks[7], (Ld, FH), 0.05)
    filt_w2 = nrm(ks[8], (Ld, FH, FH), FH ** -0.5)
    filt_b2 = nrm(ks[9], (Ld, FH), 0.1)
    filt_freq2 = 1.0 + nrm(ks[10], (Ld, FH), 0.05)
    filt_w3 = nrm(ks[11], (Ld, FH, HYENA_ORDER * 2 * HYENA_WIDTH), 0.5 * SEQ ** -0.5)
    hyena_bias = nrm(ks[12], (Ld, HYENA_ORDER, HYENA_WIDTH), 0.5)
    f_base = jnp.linspace(3.0, 6.0, H, dtype=f32)
    zero_h = jnp.zeros((H,), f32)
    gate_base = jnp.stack([zero_h, f_base, zero_h, f_base])
    mlstm_gate_bias = gate_base[None] + nrm(ks[13], (Ld, 4, H), 0.1)
    w_branch_a = nrm(ks[14], (Ld, HYENA_WIDTH, D_MODEL), HYENA_WIDTH ** -0.5)
    w_branch_b = nrm(ks[15], (Ld, MLSTM_WIDTH, D_MODEL), MLSTM_WIDTH ** -0.5)
    w_out = nrm(ks[16], (Ld, D_MODEL, D_MODEL), D_MODEL ** -0.5)
    norm2_w = 1.0 + nrm(ks[17], (Ld, D_MODEL), 0.05)
    w_gate_up = nrm(ks[18], (Ld, D_MODEL, 2 * FFN_HIDDEN), D_MODEL ** -0.5)
    w_down = nrm(ks[19], (Ld, FFN_HIDDEN, D_MODEL), FFN_HIDDEN ** -0.5)
    norm_f_w = 1.0 + nrm(ks[20], (D_MODEL,), 0.05)
    return {'x': x, 'norm1_w': norm1_w, 'w_in': w_in, 'conv_w': conv_w, 'conv_b': conv_b,
            'filt_w1': filt_w1, 'filt_b1': filt_b1, 'filt_freq1': filt_freq1,
            'filt_w2': filt_w2, 'filt_b2': filt_b2, 'filt_freq2': filt_freq2, 'filt_w3': filt_w3,
            'hyena_bias': hyena_bias, 'mlstm_gate_bias': mlstm_gate_bias,
            'w_branch_a': w_branch_a, 'w_branch_b': w_branch_b, 'w_out': w_out,
            'norm2_w': norm2_w, 'w_gate_up': w_gate_up, 'w_down': w_down, 'norm_f_w': norm_f_w}


def reference(x, norm1_w, w_in, conv_w, conv_b, filt_w1, filt_b1, filt_freq1, filt_w2, filt_b2,
              filt_freq2, filt_w3, hyena_bias, mlstm_gate_bias, w_branch_a, w_branch_b, w_out,
              norm2_w, w_gate_up, w_down, norm_f_w):
    B, L, _ = x.shape
    f32 = jnp.float32
    HW, MW, H, dh = HYENA_WIDTH, MLSTM_WIDTH, MLSTM_HEADS, MLSTM_HEAD_DIM
    for l in range(DEPTH):
        hn = rmsnorm(x, norm1_w[l])
        proj = hn @ w_in[l]
        conv_out = centred_depthwise_conv(proj[..., :SHORT_CONV_CH], conv_w[l], conv_b[l])
        rest = proj[..., SHORT_CONV_CH:]

        hy = conv_out[..., :3 * HW].astype(f32)
        hv, hx1, hx2 = hy[..., :HW], hy[..., HW:2 * HW], hy[..., 2 * HW:]
        filt = hyena_filters(L, filt_w1[l], filt_b1[l], filt_freq1[l], filt_w2[l], filt_b2[l],
                             filt_freq2[l], filt_w3[l])
        kf = two_sided_spectrum(filt)
        hb = hyena_bias[l].astype(f32)
        z = hx1 * long_conv(hv, kf[:, 0], hb[0])
        y_a = (hx2 * long_conv(z, kf[:, 1], hb[1])).astype(x.dtype)

        qk = jax.nn.silu(conv_out[..., 3 * HW:].astype(f32))
        q = qk[..., :MW].reshape(B, L, H, dh)
        k = qk[..., MW:].reshape(B, L, H, dh)
        v = rest[..., :MW].astype(f32).reshape(B, L, H, dh)
        o = jax.nn.sigmoid(rest[..., MW:2 * MW].astype(f32))
        g = (rest[..., 2 * MW:2 * MW + N_MLSTM_GATES].astype(f32).reshape(B, L, 4, H)
             + mlstm_gate_bias[l].astype(f32))
        h_fwd = mlstm_direction(q, k, v, g[:, :, 0], g[:, :, 1])
        flip = lambda t: jnp.flip(t, axis=1)
        h_bwd = flip(mlstm_direction(flip(q), flip(k), flip(v), flip(g[:, :, 2]), flip(g[:, :, 3])))
        y_b = (o * (h_fwd + h_bwd).reshape(B, L, MW)).astype(x.dtype)

        merge = rest[..., 2 * MW + N_MLSTM_GATES:]
        gate_a = jax.nn.sigmoid(merge[..., :D_MODEL])
        gate_b = jax.nn.sigmoid(merge[..., D_MODEL:])
        mixed = gate_a * (y_a @ w_branch_a[l]) + gate_b * (y_b @ w_branch_b[l])
        x = x + mixed @ w_out[l]

        hn2 = rmsnorm(x, norm2_w[l])
        gu = hn2 @ w_gate_up[l]
        x = x + (jax.nn.silu(gu[..., :FFN_HIDDEN]) * gu[..., FFN_HIDDEN:]) @ w_down[l]
    return rmsnorm(x, norm_f_w)
```

```python
import numpy as np
import ml_dtypes
from contextlib import ExitStack
import concourse.bass as bass
import concourse.mybir as mybir
from concourse.bass_utils import run_bass_kernel_spmd

F32 = mybir.dt.float32
BF16 = mybir.dt.bfloat16
AF = mybir.ActivationFunctionType
ALU = mybir.AluOpType
AX = mybir.AxisListType

D = 2048
L = 4096
OWN = 2048
CH = 512
FF = 5632
NFT = FF // 128
EPS = 1e-6
NFFT = 8192


class Prog:
    def __init__(self, nc):
        self.nc = nc
        self.q = {k: [] for k in ("pe", "act", "dve", "pool", "sp")}
        self.sems = {}
        self.cnt = {}
        self.waited = {k: {} for k in self.q}
        self._ctx = []
        self.alias = {}
        self.next_phys = 0
        self.next_phys_s = 0

    ENG = ("pe", "act", "dve", "pool")

    def sem(self, name, eng=None):
        if name not in self.ENG and not name.startswith("g#") and not name.startswith("s#"):
            if name not in self.alias:
                if eng == "pool":
                    self.alias[name] = "s#%d" % self.next_phys_s
                    self.next_phys_s += 1
                else:
                    self.alias[name] = "g#%d" % self.next_phys
                    self.next_phys += 1
            name = self.alias[name]
        if name not in self.sems:
            cm = self.nc.semaphore(name)
            s = cm.__enter__()
            self._ctx.append(cm)
            self.sems[name] = s
            self.cnt[name] = 0
        return self.sems[name]

    def op(self, eng, fn, deps=(), sem="auto", inc=1):
        if sem == "auto":
            sem = eng
        waits = []
        for d in deps:
            if d is None:
                continue
            if isinstance(d, list):
                dl = d
            else:
                dl = [d]
            for dd in dl:
                if dd is None:
                    continue
                sn, v = dd
                if self.waited[eng].get(sn, 0) >= v:
                    continue
                self.waited[eng][sn] = v
                waits.append((self.sems[sn], v))
        tok = None
        s = None
        if sem is not None:
            s = self.sem(sem, eng)
            if sem not in self.ENG:
                sem = self.alias.get(sem, sem)
            self.cnt[sem] += inc
            tok = (sem, self.cnt[sem])
        self.q[eng].append((waits, fn, s, inc))
        return tok

    def dma(self, eng, out, in_, deps=(), sem=None):
        return self.op(eng, lambda e: e.dma_start(out=out, in_=in_), deps=deps, sem=sem, inc=16)

    def barrier(self, exclude=()):
        ex = {self.alias.get(n, n) for n in exclude}
        toks = [(n, c) for n, c in self.cnt.items() if c > 0 and n not in ex]
        if ex:
            for eng in self.q:
                self.op(eng, None, deps=toks, sem=None)
            return
        for eng in self.q:
            self.op(eng, None, deps=toks, sem=None)
        self.alias = {}
        self.next_phys = 0
        self.next_phys_s = 0

    def run(self):
        nc = self.nc
        with nc.Block() as block:
            def mk(name):
                def body(e):
                    for waits, fn, s, inc in self.q[name]:
                        for (ws, v) in waits:
                            e.wait_ge(ws, v)
                        if fn is not None:
                            ins = fn(e)
                            if s is not None:
                                ins.then_inc(s, inc)
                return body
            block.tensor(mk("pe"))
            block.scalar(mk("act"))
            block.vector(mk("dve"))
            block.gpsimd(mk("pool"))
            block.sync(mk("sp"))
        for cm in reversed(self._ctx):
            cm.__exit__(None, None, None)


class Ring:
    def __init__(self, bufs):
        self.bufs = list(bufs)
        self.free = [[] for _ in self.bufs]
        self.i = 0

    def next(self):
        idx = self.i % len(self.bufs)
        self.i += 1
        return idx, self.bufs[idx], self.free[idx]

    def release(self, idx, toks):
        self.free[idx] = [t for t in toks if t is not None]


def T(x):
    return x if isinstance(x, list) else [x]


def bf(x):
    return np.ascontiguousarray(x.astype(np.float32)).astype(ml_dtypes.bfloat16)


_CONSTS = None


def host_consts():
    global _CONSTS
    if _CONSTS is not None:
        return _CONSTS
    c = {}
    c["ident"] = bf(np.eye(128))
    c["identf"] = np.eye(128, dtype=np.float32)
    a = np.arange(64)
    p = np.arange(33)
    ang = 2 * np.pi * np.outer(a, p) / 64.0
    c["FA64"] = bf(np.concatenate([np.cos(ang), -np.sin(ang)], 1))
    b = np.arange(128, dtype=np.float64)
    q = np.arange(128, dtype=np.float64)
    th = 2 * np.pi * b[:, None, None] * (p[None, :, None] + 64 * q[None, None, :]) / float(NFFT)
    c["MB"] = bf(np.stack([np.cos(th), np.sin(th), -np.sin(th)], 2))
    thT = th.transpose(2, 1, 0)
    c["IB"] = bf(np.stack([np.cos(thT), np.sin(thT), -np.sin(thT)], 2))
    wp = np.full(33, 2.0)
    wp[0] = 1.0
    wp[32] = 1.0
    a32 = np.arange(32)
    ang2 = 2 * np.pi * np.outer(p, a32) / 64.0
    Hm = np.concatenate([wp[:, None] * np.cos(ang2), -wp[:, None] * np.sin(ang2)], 0) / float(NFFT)
    Hpad = np.zeros((66, 4, 128))
    for bi in range(4):
        Hpad[:, bi, 32 * bi:32 * bi + 32] = Hm
    c["Hpad"] = bf(Hpad)
    t = np.linspace(0.0, 1.0, L, dtype=np.float32)
    bands = 16
    omega = (2.0 * np.pi * np.arange(L, dtype=np.float32) / L).astype(np.float32)
    freqs = np.linspace(1e-4, bands - 1, bands, dtype=np.float32)
    angz = (omega[:, None] * freqs[None, :]).astype(np.float32)
    z = np.concatenate([t[:, None], np.cos(angz), -np.sin(angz)], -1).astype(np.float32)
    zz = np.zeros((NFFT, 33), np.float32)
    zz[:L] = z
    zz[L + 1:] = z[1:][::-1]
    c["zzT"] = np.ascontiguousarray(zz.T)
    tau = np.zeros(NFFT, np.float32)
    tau[:L] = t
    tau[L + 1:] = t[1:][::-1]
    tau[L] = 1e4
    c["negtau"] = np.ascontiguousarray(-tau.reshape(64, 128))
    max_decay = np.log(1e-2) / 0.3
    min_decay = np.log(1e-2) / 1.5
    deltas = np.abs(np.linspace(min_decay, max_decay, 1024, dtype=np.float32))
    c["absdelta"] = deltas
    s_ = np.arange(128)[:, None]
    j_ = np.arange(128)[None, :]
    c["maskf"] = (s_ <= j_).astype(np.float32)
    c["maskb"] = (s_ >= j_).astype(np.float32)
    rm = np.ones((64, L), np.float32)
    rm[0:32, ::128] = 0.0
    rm[32:64, 127::128] = 0.0
    c["resetmask"] = rm
    _CONSTS = c
    return c


class K:
    pass


def build(phases=None, dbg_in=(), dbg_out=()):
    ALL = ["F", "A", "B1", "B2", "H", "M", "X", "C", "D"]
    if phases is None:
        phases = set(ALL)
    nc = bass.Bass("TRN2", target_bir_lowering=False)
    P = Prog(nc)
    k = K()
    k.nc, k.P = nc, P
    k.cache = {}

    def din(name, shape, dt=F32):
        return nc.dram_tensor(name, list(shape), dt, kind="ExternalInput").ap()

    def dscr(name, shape, dt):
        if name in dbg_in:
            return nc.dram_tensor(name, list(shape), dt, kind="ExternalInput").ap()
        if name in dbg_out:
            return nc.dram_tensor(name, list(shape), dt, kind="ExternalOutput").ap()
        return nc.dram_tensor(name, list(shape), dt).ap()

    SHAPES = {
        "x": ([L, D], F32),
        "x_own": ([OWN, D], F32),
        "w_hy": ([D, 1536], F32),
        "w_qk": ([D, 1024], F32),
        "w_vo": ([D, 1024], F32),
        "w_gi": ([D, 64], F32),
        "w_gf": ([D, 64], F32),
        "w_mg": ([D, 4096], F32),
        "cwb": ([128, 20, 4], F32),
        "f_w1": ([33, 64], F32),
        "f_b1": ([64, 1], F32),
        "f_f1": ([64, 1], F32),
        "f_w2": ([64, 64], F32),
        "f_b2": ([64, 1], F32),
        "f_f2": ([64, 1], F32),
        "f_w3": ([64, 2048], F32),
        "hb_bc": ([2, 128, CH], F32),
        "gbias": ([64, 2], F32),
        "w_ba": ([1024, D], F32),
        "w_bb": ([1024, D], F32),
        "w_out": ([D, D], F32),
        "w_gu": ([D, 2 * FF], F32),
        "w_dn": ([FF, D], F32),
        "n1_bc": ([128, D], F32),
        "n2_bc": ([128, D], F32),
        "nf_bc": ([128, D], F32),
        "ident": ([128, 128], BF16),
        "identf": ([128, 128], F32),
        "FA64": ([64, 66], BF16),
        "MB": ([128, 33, 3, 128], BF16),
        "IB": ([128, 33, 3, 128], BF16),
        "Hpad": ([66, 4, 128], BF16),
        "zzT": ([33, NFFT], F32),
        "negtau": ([64, 128], F32),
        "absd_bc": ([64, CH], F32),
        "maskf": ([128, 128], F32),
        "maskb": ([128, 128], F32),
        "resetmask": ([64, L], F32),
    }

    class LazyIn(dict):
        def __missing__(self, name):
            shape, dt = SHAPES[name]
            ap = nc.dram_tensor(name, list(shape), dt, kind="ExternalInput").ap()
            self[name] = ap
            return ap
    I = LazyIn()
    k.I = I
    out = nc.dram_tensor("out", [OWN, D], F32, kind="ExternalOutput").ap()
    k.out = out

    S = {}
    S["HV"] = dscr("HV", [L, CH], BF16)
    S["HX1"] = dscr("HX1", [L, CH], BF16)
    S["HX2"] = dscr("HX2", [L, CH], BF16)
    S["Z"] = dscr("Z", [L, CH], BF16)
    S["T2"] = dscr("T2", [2, 66, 128, CH], BF16)
    S["T3"] = dscr("T3", [66, 128, CH], BF16)
    S["KSPEC"] = dscr("KSPEC", [2, 128, 33, 2, CH], BF16)
    S["QT"] = dscr("QT", [4, 128, L], BF16)
    S["KT"] = dscr("KT", [4, 128, L], BF16)
    S["KTM"] = dscr("KTM", [L, CH], BF16)
    S["VTM"] = dscr("VTM", [L, CH], BF16)
    S["OTM"] = dscr("OTM", [L, CH], BF16)
    S["IG"] = dscr("IG", [64, L], F32)
    S["FG"] = dscr("FG", [64, L], F32)
    S["GAT"] = dscr("GAT", [16, 128, OWN], BF16)
    S["GBT"] = dscr("GBT", [16, 128, OWN], BF16)
    if "CCIN" in dbg_in:
        S["CCIN"] = nc.dram_tensor("CCIN_int", [L, 1024], BF16).ap()
        k.ccin_ext = nc.dram_tensor("CCIN", [L, 1024], BF16, kind="ExternalInput").ap()
    else:
        S["CCIN"] = dscr("CCIN", [L, 1024], BF16)
    S["CCOUT"] = dscr("CCOUT", [2 * L, 1024], BF16)
    S["YOWN"] = dscr("YOWN", [2, 2048, 1024], BF16)
    S["X1"] = dscr("X1", [OWN, D], F32)
    S["HN2T"] = dscr("HN2T", [128, 16, OWN], BF16)
    S["X2"] = dscr("X2", [OWN, D], F32)
    S["RHO"] = dscr("RHO", [64, 32], F32)
    k.S = S

    with ExitStack() as es:
        def set_psum(pes, tag, n32, n16):
            psf = [pes.enter_context(nc.psum_tensor(f"psf{tag}{i}", [128, 512], F32)) for i in range(n32)]
            psb = [pes.enter_context(nc.psum_tensor(f"psb{tag}{i}", [128, 1024], BF16)) for i in range(n16)]
            k.psf = Ring(psf)
            k.psb = Ring(psb)
            k.ps8 = lambda: k.psf
        ident = es.enter_context(nc.sbuf_tensor("sb_ident", [128, 128], BF16))
        identf = es.enter_context(nc.sbuf_tensor("sb_identf", [128, 128], F32))
        epst = es.enter_context(nc.sbuf_tensor("epst", [128, 1], F32))
        k.ident, k.identf, k.epst = ident, identf, epst
        t0 = P.dma("sp", ident[:], I["ident"], sem="c0")
        t1 = P.dma("sp", identf[:], I["identf"], sem="c0")
        t2 = P.op("dve", lambda e: e.memset(epst[:], EPS))
        k.c_tok = [t0, t1, t2]
        P.barrier()

        def phase_XB2(kk):
            if "X" in phases:
                phase_X(kk)
            if "B2" in phases:
                phase_B2(kk)
        plan = [("F", phase_F, 8, 0), ("A", phase_AB1, 6, 2), ("H", phase_H, 8, 0), ("M", phase_M, 8, 0),
                ("XB2", phase_XB2, 6, 2), ("C", phase_C, 6, 2), ("D", phase_D, 8, 0)]
        for name, fn, n32, n16 in plan:
            if name in phases or (name == "A" and "B1" in phases) or (name == "XB2" and ("X" in phases or "B2" in phases)):
                with ExitStack() as pes:
                    set_psum(pes, name, n32, n16)
                    fn(k)
                    P.barrier()
        P.run()
    return nc, set(I.keys())


def load_w_bf16(k, dst, src_cols, deps, sem):
    return k.P.dma("pool", dst, src_cols.rearrange("(kc p) n -> p kc n", p=128), deps=deps, sem=sem)


def rmsnorm_tile(k, x_sb, x_tok, nw_bc, xn_bf, junk, ss, rms, rstd, out_free):
    P = k.P
    t1 = P.op("act", lambda e: e.activation(out=junk[:], in_=x_sb[:], func=AF.Square, accum_out=ss[:]),
              deps=[x_tok, out_free])
    t2 = P.op("act", lambda e: e.activation(out=rms[:], in_=ss[:], func=AF.Sqrt, scale=1.0 / D, bias=k.epst[:]),
              deps=[t1])
    t3 = P.op("dve", lambda e: e.reciprocal(out=rstd[:], in_=rms[:]), deps=[t2])
    t4 = P.op("dve", lambda e: e.scalar_tensor_tensor(out=xn_bf[:], in0=x_sb[:], scalar=rstd[:, 0:1], in1=nw_bc[:],
                                                       op0=ALU.mult, op1=ALU.mult), deps=[t3, out_free])
    return t4


def transpose_to_T(k, xn_bf, xn_tok, dstT, tok0, dst_free=None):
    P = k.P
    toks = []
    last_pe = None
    for half in range(2):
        idx, pb, fr = k.psb.next()
        for j in range(8):
            kc = half * 8 + j
            last_pe = P.op("pe", (lambda e, pb=pb, j=j, kc=kc: e.transpose(out=pb[:, j * 128:(j + 1) * 128],
                                                                             in_=xn_bf[:, kc * 128:(kc + 1) * 128],
                                                                             identity=k.ident[:])),
                           deps=[xn_tok] + fr + k.c_tok)
        eng = "act" if half == 0 else "dve"
        if eng == "act":
            t = P.op("act", (lambda e, pb=pb, half=half: e.activation(
                out=dstT[:, half * 8:(half + 1) * 8, tok0:tok0 + 128],
                in_=pb[:, :].rearrange("p (k t) -> p k t", t=128), func=AF.Copy)), deps=[last_pe, dst_free])
        else:
            t = P.op("dve", (lambda e, pb=pb, half=half: e.tensor_copy(
                out=dstT[:, half * 8:(half + 1) * 8, tok0:tok0 + 128],
                in_=pb[:, :].rearrange("p (k t) -> p k t", t=128))), deps=[last_pe, dst_free])
        k.psb.release(idx, [t])
        toks.append(t)
    return toks, last_pe


def phase_F(k):
    nc, P, I, S = k.nc, k.P, k.I, k.S
    with ExitStack() as es:
        sb = lambda n, s, d: es.enter_context(nc.sbuf_tensor(n, s, d))
        zzT = sb("sb_zzT", [33, NFFT], F32)
        w1 = sb("fw1", [33, 64], F32)
        w2 = sb("fw2", [64, 64], F32)
        prm = sb("fprm", [64, 4], F32)
        sc = sb("fsc", [64, 4], F32)
        w3 = sb("fw3", [64, 2048], BF16)
        h1 = sb("fh1", [64, 512], F32)
        tmp = sb("ftmp", [64, 512], F32)
        tmp2 = sb("ftmp2", [64, 512], F32)
        h2T = sb("fh2T", [64, NFFT], BF16)
        negtau = sb("fnegtau", [64, 128], F32)
        absd = sb("fabsd", [64, CH], F32)
        FA = sb("fFA", [64, 66], BF16)
        win = [sb(f"fwin{i}", [64, CH], F32) for i in range(2)]
        kern = [sb(f"fkern{i}", [64, CH], BF16) for i in range(4)]
        S2 = [sb(f"fS2{i}", [66, 2, 16, CH], BF16) for i in range(2)]
        ld = []
        ld.append(P.dma("sp", zzT[:], I["zzT"], sem="fl"))
        ld.append(P.dma("sp", w1[:], I["f_w1"], sem="fl"))
        ld.append(P.dma("sp", w2[:], I["f_w2"], sem="fl"))
        ld.append(P.dma("sp", prm[:, 0:1], I["f_b1"], sem="fl"))
        ld.append(P.dma("sp", prm[:, 1:2], I["f_f1"], sem="fl"))
        ld.append(P.dma("sp", prm[:, 2:3], I["f_b2"], sem="fl"))
        ld.append(P.dma("sp", prm[:, 3:4], I["f_f2"], sem="fl"))
        ld.append(P.dma("sp", negtau[:], I["negtau"], sem="fl"))
        ld.append(P.dma("sp", absd[:], I["absd_bc"], sem="fl"))
        ld.append(P.dma("sp", FA[:], I["FA64"], sem="fl"))
        ldw3 = P.dma("pool", w3[:], I["f_w3"], sem="fl3")
        ldall = [ld[-1]]
        tsc = None
        for li in range(2):
            t_ = P.op("dve", (lambda e, li=li: e.tensor_scalar(out=sc[:, 2 * li:2 * li + 1], in0=prm[:, 2 * li + 1:2 * li + 2],
                                                               scalar1=1.0 / 3.0, scalar2=None, op0=ALU.mult)), deps=ldall)
            tsc = P.op("dve", (lambda e, li=li: e.tensor_tensor(out=sc[:, 2 * li + 1:2 * li + 2], in0=sc[:, 2 * li:2 * li + 1],
                                                                in1=prm[:, 2 * li:2 * li + 1], op=ALU.mult)), deps=[t_])
        prev = tsc
        for blk in range(16):
            cs = slice(blk * 512, (blk + 1) * 512)
            idx, ps, fr = k.psf.next()
            t = P.op("pe", (lambda e, ps=ps, cs=cs: e.matmul(ps[0:64, :], lhsT=w1[:], rhs=zzT[:, cs], start=True, stop=True)),
                     deps=ldall + fr)
            t = P.op("act", (lambda e, ps=ps: e.activation(out=tmp[:], in_=ps[0:64, :], func=AF.Sin, scale=sc[:, 0:1], bias=sc[:, 1:2])),
                     deps=[t, prev])
            k.psf.release(idx, [t])
            t = P.op("dve", lambda e: e.tensor_tensor(out=tmp2[:], in0=tmp[:], in1=tmp[:], op=ALU.mult), deps=[t])
            t = P.op("dve", lambda e: e.tensor_scalar(out=tmp2[:], in0=tmp2[:], scalar1=-4.0, scalar2=3.0, op0=ALU.mult, op1=ALU.add), deps=[t])
            t = P.op("dve", lambda e: e.tensor_tensor(out=h1[:], in0=tmp[:], in1=tmp2[:], op=ALU.mult), deps=[t])
            idx, ps, fr = k.psf.next()
            t = P.op("pe", (lambda e, ps=ps: e.matmul(ps[0:64, :], lhsT=w2[:], rhs=h1[:], start=True, stop=True)), deps=[t] + fr)
            t = P.op("act", (lambda e, ps=ps: e.activation(out=tmp[:], in_=ps[0:64, :], func=AF.Sin, scale=sc[:, 2:3], bias=sc[:, 3:4])),
                     deps=[t])
            k.psf.release(idx, [t])
            t = P.op("dve", lambda e: e.tensor_tensor(out=tmp2[:], in0=tmp[:], in1=tmp[:], op=ALU.mult), deps=[t])
            t = P.op("dve", lambda e: e.tensor_scalar(out=tmp2[:], in0=tmp2[:], scalar1=-4.0, scalar2=3.0, op0=ALU.mult, op1=ALU.add), deps=[t])
            prev = P.op("dve", (lambda e, cs=cs: e.tensor_tensor(out=h2T[:, cs], in0=tmp[:], in1=tmp2[:], op=ALU.mult)), deps=[t])
        h2_ready = prev
        kern6 = kern + [sb(f"fkern{i}", [64, CH], BF16) for i in range(4, 8)]
        winR = Ring(win)
        kernR = Ring(kern6)
        S2R = Ring(S2)
        R8 = k.ps8()
        h2v = h2T[:, :]
        items = [(bp, o) for bp in range(128) for o in range(2)]
        st1 = {}
        wstate = {}
        s2state = {}
        SK = 3

        def stage1(i):
            bp, o = items[i]
            if o == 0:
                wi, wt, wfree = winR.next()
                tw = P.op("act", (lambda e, wt=wt, bp=bp: e.activation(out=wt[:], in_=absd[:], func=AF.Exp, scale=negtau[:, bp:bp + 1])),
                          deps=ldall + wfree)
                wstate[bp] = (wi, wt, tw, [])
            wi, wt, tw, wusers = wstate[bp]
            idx, ps, fr = R8.next()
            lo = bass.AP(h2v.tensor, h2v.offset + bp, [[h2v.ap[0][0], 64], [128, 32]])
            hi = bass.AP(h2v.tensor, h2v.offset + L + bp, [[h2v.ap[0][0], 64], [128, 32]])
            P.op("pe", (lambda e, ps=ps, lo=lo, o=o: e.matmul(ps[0:32, :], lhsT=lo, rhs=w3[:, (2 * o) * CH:(2 * o + 1) * CH],
                                                           start=True, stop=True)), deps=[h2_ready, ldw3] + fr, sem=None)
            t = P.op("pe", (lambda e, ps=ps, hi=hi, o=o: e.matmul(ps[32:64, :], lhsT=hi, rhs=w3[:, (2 * o + 1) * CH:(2 * o + 2) * CH],
                                                               start=True, stop=True)))
            ki, kt, kfree = kernR.next()
            tk = P.op("dve", (lambda e, ps=ps, kt=kt, wt=wt: e.tensor_tensor(out=kt[:], in0=ps[0:64, :], in1=wt[:], op=ALU.mult)),
                      deps=[t, tw] + kfree)
            R8.release(idx, [tk])
            wusers.append(tk)
            if o == 1:
                winR.release(wi, wusers)
            st1[i] = (ki, kt, tk)

        def stage2(i):
            bp, o = items[i]
            ki, kt, tk = st1.pop(i)
            if bp % 16 == 0 and o == 0:
                s2i, s2b, s2f = S2R.next()
                s2state["cur"] = (s2i, s2b, s2f, [])
            s2i, s2b, s2f, evs = s2state["cur"]
            idx2, ps2, fr2 = R8.next()
            ta = P.op("pe", (lambda e, ps2=ps2, kt=kt: e.matmul(ps2[0:66, :], lhsT=FA[:], rhs=kt[:], start=True, stop=True)),
                      deps=[tk] + fr2)
            kernR.release(ki, [ta])
            if o == 0:
                te = P.op("act", (lambda e, ps2=ps2, s2b=s2b, o=o, bp=bp: e.activation(out=s2b[:, o, bp % 16, :], in_=ps2[0:66, :], func=AF.Copy)),
                          deps=[ta] + s2f)
            else:
                te = P.op("dve", (lambda e, ps2=ps2, s2b=s2b, o=o, bp=bp: e.tensor_copy(out=s2b[:, o, bp % 16, :], in_=ps2[0:66, :])),
                          deps=[ta] + s2f)
            R8.release(idx2, [te])
            evs.append(te)
            if bp % 16 == 15 and o == 1:
                b0 = bp - 15
                dts = []
                for oo in range(2):
                    dts.append(P.dma("sp", S["T2"][oo, :, b0:b0 + 16, :], s2b[:, oo, :, :], deps=evs[-2:], sem=f"fs2_{s2i}"))
                S2R.release(s2i, [dts[-1]])

        for i in range(len(items) + SK):
            if i < len(items):
                stage1(i)
            if i >= SK:
                stage2(i - SK)
    P.barrier()
    with ExitStack() as es:
        sb = lambda n, s, d: es.enter_context(nc.sbuf_tensor(n, s, d))
        MB = sb("fMB", [128, 33, 3, 128], BF16)
        hb = sb("fhb", [128, 2, CH], F32)
        A2 = [sb(f"fA{i}", [128, 66, CH], BF16) for i in range(2)]
        KS = [sb(f"fKS{i}", [128, 2, CH], BF16) for i in range(4)]
        tmb = P.dma("sp", MB[:], I["MB"], sem="fl")
        thb = P.dma("sp", hb[:], I["hb_bc"].rearrange("o p c -> p o c"), sem="fl")
        KSR = Ring(KS)
        R8 = k.ps8()
        a_rdy = []
        for o in range(2):
            lds = []
            for g in range(6):
                lds.append(P.dma("pool" if o == 1 else "sp", A2[o][:, g * 11:(g + 1) * 11, :],
                                 S["T2"][o, g * 11:(g + 1) * 11, :, :].rearrange("r b c -> b r c"), sem=f"fA{o}"))
            a_rdy.append([lds[-1]])
        for o in range(2):
            A_sb = A2[o]
            a_ready = a_rdy[o]
            for p in range(33):
                ksi, ks, ksfree = KSR.next()
                i1, pre, f1 = R8.next()
                P.op("pe", (lambda e, pre=pre, p=p, A_sb=A_sb: e.matmul(pre[:], lhsT=MB[:, p, 0, :], rhs=A_sb[:, p, :], start=True, stop=False)),
                     deps=a_ready + [tmb] + f1, sem=None)
                t_re = P.op("pe", (lambda e, pre=pre, p=p, A_sb=A_sb: e.matmul(pre[:], lhsT=MB[:, p, 1, :], rhs=A_sb[:, 33 + p, :], start=False, stop=True)))
                i2, pim, f2 = R8.next()
                P.op("pe", (lambda e, pim=pim, p=p, A_sb=A_sb: e.matmul(pim[:], lhsT=MB[:, p, 0, :], rhs=A_sb[:, 33 + p, :], start=True, stop=False)),
                     deps=f2, sem=None)
                t_im = P.op("pe", (lambda e, pim=pim, p=p, A_sb=A_sb: e.matmul(pim[:], lhsT=MB[:, p, 2, :], rhs=A_sb[:, p, :], start=False, stop=True)))
                e1 = P.op("dve", (lambda e, pre=pre, ks=ks, o=o: e.tensor_tensor(out=ks[:, 0, :], in0=pre[:], in1=hb[:, o, :], op=ALU.add)),
                          deps=[t_re, thb] + ksfree)
                R8.release(i1, [e1])
                e2 = P.op("act", (lambda e, pim=pim, ks=ks: e.activation(out=ks[:, 1, :], in_=pim[:], func=AF.Copy)), deps=[t_im] + ksfree)
                R8.release(i2, [e2])
                d = P.dma("sp", S["KSPEC"][o, :, p, :, :], ks[:], deps=[e1, e2], sem=f"fks{ksi}")
                KSR.release(ksi, [d])


def phase_H(k):
    nc, P, I, S = k.nc, k.P, k.I, k.S
    with ExitStack() as es:
        sb = lambda n, s, d: es.enter_context(nc.sbuf_tensor(n, s, d))
        FA = sb("hFA", [64, 66], BF16)
        Hpad = sb("hHpad", [66, 4, 128], BF16)
        MB = sb("hMB", [128, 33, 3, 128], BF16)
        IB = sb("hIB", [128, 33, 3, 128], BF16)
        tl = [P.dma("sp", FA[:], I["FA64"], sem="hl"), P.dma("sp", Hpad[:], I["Hpad"], sem="hl"),
              P.dma("pool", MB[:], I["MB"], sem="hlp"), P.dma("pool", IB[:], I["IB"], sem="hlp")]
        tl = [tl[1], tl[3]]
        k.h_tabs = (MB, IB)
        for o, (src, gate, dst, dst_cols) in enumerate([(S["HV"], S["HX1"], S["Z"], CH), (S["Z"], S["HX2"], S["CCIN"], 1024)]):
            _h_conv(k, o, src, gate, dst, FA, Hpad, tl)


def _h_conv(k, o, src, gate, dst, FA, Hpad, tl):
    nc, P, I, S = k.nc, k.P, k.I, k.S
    if True:
        if True:
            with ExitStack() as es2:
                sb2 = lambda n, s, d: es2.enter_context(nc.sbuf_tensor(n, s, d))
                Xc = [sb2(f"hXc{o}_{i}", [32, 16, CH], BF16) for i in range(3)]
                S2 = [sb2(f"hS2{o}_{i}", [66, 16, CH], BF16) for i in range(2)]
                XR, SR = Ring(Xc), Ring(S2)
                srcv = src.rearrange("(a b) c -> a b c", b=128)
                xinfo = {}

                def xload(bcx):
                    xi_x, xbx, xfreex = XR.next()
                    txx = P.dma("sp", xbx[:], srcv[:, bcx * 16:(bcx + 1) * 16, :], deps=xfreex, sem=f"hx{xi_x}")
                    xinfo[bcx] = (xi_x, xbx, txx)
                xload(0)
                for bc in range(8):
                    if bc + 1 < 8:
                        xload(bc + 1)
                    xi, xb, tx = xinfo.pop(bc)
                    si, sbuf, sfree = SR.next()
                    evs = []
                    mms = []
                    for j in range(16):
                        idx, ps, fr = k.psf.next()
                        t = P.op("pe", (lambda e, ps=ps, xb=xb, j=j: e.matmul(ps[0:66, :], lhsT=FA[0:32, :], rhs=xb[:, j, :], start=True, stop=True)),
                                 deps=[tx] + tl + fr)
                        mms.append(t)
                        eng = "act" if j % 2 == 0 else "dve"
                        if eng == "act":
                            te = P.op("act", (lambda e, ps=ps, sbuf=sbuf, j=j: e.activation(out=sbuf[:, j, :], in_=ps[0:66, :], func=AF.Copy)),
                                      deps=[t] + sfree)
                        else:
                            te = P.op("dve", (lambda e, ps=ps, sbuf=sbuf, j=j: e.tensor_copy(out=sbuf[:, j, :], in_=ps[0:66, :])),
                                      deps=[t] + sfree)
                        k.psf.release(idx, [te])
                        evs.append(te)
                    XR.release(xi, [mms[-1]])
                    d = P.dma("sp", S["T2"][0, :, bc * 16:(bc + 1) * 16, :], sbuf[:], deps=evs[-2:], sem=f"hs{si}")
                    SR.release(si, [d])
            P.barrier()
            with ExitStack() as es2:
                sb2 = lambda n, s, d: es2.enter_context(nc.sbuf_tensor(n, s, d))
                MB, IB = k.h_tabs
                A_sb = sb2(f"hA{o}", [128, 66, CH], BF16)
                KK = [sb2(f"hK{o}_{i}", [128, 2, CH], BF16) for i in range(6)]
                Yb = [sb2(f"hY{o}_{i}", [128, 2, CH], BF16) for i in range(5)]
                tt_ = [sb2(f"hT{o}_{i}", [128, 4, CH], F32) for i in range(3)]
                Bst = [sb2(f"hB{o}_{i}", [128, 2, CH], BF16) for i in range(3)]
                tm = tl
                lds = []
                for g in range(6):
                    lds.append(P.dma("sp" if g % 2 == 0 else "pool", A_sb[:, g * 11:(g + 1) * 11, :],
                                     S["T2"][0, g * 11:(g + 1) * 11, :, :].rearrange("r b c -> b r c"), sem=f"hA{g % 2}"))
                a_ready = lds[-2:]
                KR, YR, TR, BR = Ring(KK), Ring(Yb), Ring(tt_), Ring(Bst)
                R8 = k.ps8()
                T3v = S["T3"].rearrange("(ri p) b c -> b ri p c", ri=2)
                yst = {}
                SK = 2

                kinfo = {}
                PF = 3

                def kload(p):
                    ki, kk, kfree = KR.next()
                    tk = P.dma("sp", kk[:], S["KSPEC"][o, :, p, :, :], deps=kfree, sem=f"hk{ki}")
                    kinfo[p] = (ki, kk, tk)

                def hb_stage1(p):
                    if p + PF < 33:
                        kload(p + PF)
                    ki, kk, tk = kinfo.pop(p)
                    i1, pre, f1 = R8.next()
                    P.op("pe", (lambda e, pre=pre, p=p: e.matmul(pre[:], lhsT=MB[:, p, 0, :], rhs=A_sb[:, p, :], start=True, stop=False)),
                         deps=a_ready + tm + f1, sem=None)
                    t_re = P.op("pe", (lambda e, pre=pre, p=p: e.matmul(pre[:], lhsT=MB[:, p, 1, :], rhs=A_sb[:, 33 + p, :], start=False, stop=True)))
                    i2, pim, f2 = R8.next()
                    P.op("pe", (lambda e, pim=pim, p=p: e.matmul(pim[:], lhsT=MB[:, p, 0, :], rhs=A_sb[:, 33 + p, :], start=True, stop=False)),
                         deps=f2, sem=None)
                    t_im = P.op("pe", (lambda e, pim=pim, p=p: e.matmul(pim[:], lhsT=MB[:, p, 2, :], rhs=A_sb[:, p, :], start=False, stop=True)))
                    ti, tb_, tfree = TR.next()
                    yi, yb, yfree = YR.next()
                    m1 = P.op("dve", (lambda e, pre=pre, kk=kk, tb_=tb_: e.tensor_tensor(out=tb_[:, 0, :], in0=pre[:], in1=kk[:, 0, :], op=ALU.mult)),
                              deps=[t_re, tk] + tfree)
                    m2 = P.op("dve", (lambda e, pim=pim, kk=kk, tb_=tb_: e.tensor_tensor(out=tb_[:, 1, :], in0=pim[:], in1=kk[:, 1, :], op=ALU.mult)),
                              deps=[t_im])
                    m3 = P.op("dve", (lambda e, pre=pre, kk=kk, tb_=tb_: e.tensor_tensor(out=tb_[:, 2, :], in0=pre[:], in1=kk[:, 1, :], op=ALU.mult)))
                    m4 = P.op("dve", (lambda e, pim=pim, kk=kk, tb_=tb_: e.tensor_tensor(out=tb_[:, 3, :], in0=pim[:], in1=kk[:, 0, :], op=ALU.mult)))
                    R8.release(i1, [m3])
                    R8.release(i2, [m4])
                    KR.release(ki, [m4])
                    y1 = P.op("dve", (lambda e, tb_=tb_, yb=yb: e.tensor_tensor(out=yb[:, 0, :], in0=tb_[:, 0, :], in1=tb_[:, 1, :], op=ALU.subtract)),
                              deps=[m2] + yfree)
                    y2 = P.op("pool", (lambda e, tb_=tb_, yb=yb: e.tensor_tensor(out=yb[:, 1, :], in0=tb_[:, 2, :], in1=tb_[:, 3, :], op=ALU.add)),
                              deps=[m4] + yfree)
                    TR.release(ti, [y1, y2])
                    yst[p] = (yi, yb, [y1, y2])

                def hb_stage2(p):
                    yi, yb, y2 = yst.pop(p)
                    j1, qre, g1 = R8.next()
                    P.op("pe", (lambda e, qre=qre, p=p, yb=yb: e.matmul(qre[:], lhsT=IB[:, p, 0, :], rhs=yb[:, 0, :], start=True, stop=False)),
                         deps=y2 + g1, sem=None)
                    u_re = P.op("pe", (lambda e, qre=qre, p=p, yb=yb: e.matmul(qre[:], lhsT=IB[:, p, 2, :], rhs=yb[:, 1, :], start=False, stop=True)))
                    j2, qim, g2 = R8.next()
                    P.op("pe", (lambda e, qim=qim, p=p, yb=yb: e.matmul(qim[:], lhsT=IB[:, p, 1, :], rhs=yb[:, 0, :], start=True, stop=False)),
                         deps=g2, sem=None)
                    u_im = P.op("pe", (lambda e, qim=qim, p=p, yb=yb: e.matmul(qim[:], lhsT=IB[:, p, 0, :], rhs=yb[:, 1, :], start=False, stop=True)))
                    YR.release(yi, [u_im])
                    bi_, bb, bfree = BR.next()
                    e1 = P.op("act", (lambda e, qre=qre, bb=bb: e.activation(out=bb[:, 0, :], in_=qre[:], func=AF.Copy)), deps=[u_re] + bfree)
                    e2 = P.op("act", (lambda e, qim=qim, bb=bb: e.activation(out=bb[:, 1, :], in_=qim[:], func=AF.Copy)), deps=[u_im])
                    R8.release(j1, [e1])
                    R8.release(j2, [e2])
                    d = P.dma("sp", T3v[:, :, p, :], bb[:], deps=[e2], sem=f"hb{bi_}")
                    BR.release(bi_, [d])

                for p0 in range(PF):
                    kload(p0)
                for i in range(33 + SK):
                    if i < 33:
                        hb_stage1(i)
                    if i >= SK:
                        hb_stage2(i - SK)
            P.barrier()
            with ExitStack() as es2:
                sb2 = lambda n, s, d: es2.enter_context(nc.sbuf_tensor(n, s, d))
                G_sb = sb2(f"hG{o}", [128, 32, CH], BF16)
                Z_sb = sb2(f"hZ{o}", [128, 32, CH], BF16)
                Bc = [sb2(f"hBc{o}_{i}", [66, 16, CH], BF16) for i in range(3)]
                gv = gate.rearrange("(a bg bi) c -> bi a bg c", a=32, bg=32, bi=4)
                tgc = []
                for bc in range(8):
                    tgl = None
                    for bi in range(4):
                        tgl = P.dma("pool", G_sb[32 * bi:32 * bi + 32, bc * 4:(bc + 1) * 4, :], gv[bi][:, bc * 4:(bc + 1) * 4, :], sem=f"hg{bc}")
                    tgc.append(tgl)
                dv = dst[:, 0:CH].rearrange("(a bg bi) c -> bi a bg c", a=32, bg=32, bi=4)
                BR = Ring(Bc)
                zt = []
                binfo = {}

                def bload(bcx):
                    bi_x, bbx, bfreex = BR.next()
                    tbx = P.dma("sp", bbx[:], S["T3"][:, bcx * 16:(bcx + 1) * 16, :], deps=bfreex, sem=f"hc{bi_x}")
                    binfo[bcx] = (bi_x, bbx, tbx)
                bload(0)
                for bc in range(8):
                    tg = [tgc[bc]]
                    if bc + 1 < 8:
                        bload(bc + 1)
                    bi_, bb, tb_ = binfo.pop(bc)
                    last = None
                    for g in range(4):
                        bg = bc * 4 + g
                        idx, ps, fr = k.psf.next()
                        for bi in range(4):
                            last = P.op("pe", (lambda e, ps=ps, bb=bb, bi=bi, g=g: e.matmul(ps[:], lhsT=Hpad[:, bi, :], rhs=bb[:, g * 4 + bi, :],
                                                                                         start=(bi == 0), stop=(bi == 3))),
                                        deps=[tb_] + tl + fr, sem=("pe" if bi == 3 else None))
                        tz = P.op("dve", (lambda e, ps=ps, bg=bg: e.tensor_tensor(out=Z_sb[:, bg, :], in0=ps[:], in1=G_sb[:, bg, :], op=ALU.mult)),
                                  deps=[last] + tg)
                        k.psf.release(idx, [tz])
                        zt.append(tz)
                    BR.release(bi_, [last])
                    for bi in range(4):
                        P.dma("sp", dv[bi][:, bc * 4:(bc + 1) * 4, :], Z_sb[32 * bi:32 * bi + 32, bc * 4:(bc + 1) * 4, :], deps=[zt[-1]], sem="hz")
            P.barrier()


def norm_stage(k, src, tiles, nw_name, hnT, tag, excl=()):
    nc, P, I = k.nc, k.P, k.I
    with ExitStack() as es:
        sb = lambda n, s, d: es.enter_context(nc.sbuf_tensor(n, s, d))
        nw = sb(f"nw{tag}", [128, D], F32)
        xt = [sb(f"nx{tag}{i}", [128, D], F32) for i in range(4)]
        xn = [sb(f"nxn{tag}{i}", [128, D], BF16) for i in range(3)]
        junk = sb(f"nj{tag}", [128, D], BF16)
        st = [sb(f"nst{tag}{i}", [128, 4], F32) for i in range(4)]
        tnw = P.dma("sp", nw[:], I[nw_name], sem=f"nw{tag}")
        XR, NR = Ring(xt), Ring(xn)
        pend = []

        def stage1(slot, row0):
            xi, xb, xfree = XR.next()
            tx = P.dma("sp", xb[:], src[row0:row0 + 128, :], deps=xfree, sem=f"nx{tag}{xi}")
            ni, nb_, nfree = NR.next()
            stt = st[xi]
            t1 = P.op("act", (lambda e, xb=xb, stt=stt: e.activation(out=junk[:], in_=xb[:], func=AF.Square, accum_out=stt[:, 0:1])),
                      deps=[tx])
            t2 = P.op("act", (lambda e, stt=stt: e.activation(out=stt[:, 1:2], in_=stt[:, 0:1], func=AF.Sqrt, scale=1.0 / D, bias=k.epst[:])),
                      deps=[t1])
            t3 = P.op("dve", (lambda e, stt=stt: e.reciprocal(out=stt[:, 2:3], in_=stt[:, 1:2])), deps=[t2])
            t4 = P.op("dve", (lambda e, xb=xb, nb_=nb_, stt=stt: e.scalar_tensor_tensor(out=nb_[:], in0=xb[:], scalar=stt[:, 2:3], in1=nw[:],
                                                                                     op0=ALU.mult, op1=ALU.mult)),
                      deps=[t3, tnw] + nfree)
            XR.release(xi, [t4])
            pend.append((slot, ni, nb_, t4))

        def stage2():
            slot, ni, nb_, t4 = pend.pop(0)
            toks, last_pe = transpose_to_T(k, nb_, t4, hnT, slot * 128)
            NR.release(ni, [last_pe])

        for (slot, row0) in tiles:
            stage1(slot, row0)
            if len(pend) > 1:
                stage2()
        while pend:
            stage2()
    P.barrier(exclude=excl)


def phase_AB1(k):
    for ph in range(2):
        _ab1_pass(k, ph)


def _ab1_pass(k, ph):
    nc, P, I, S = k.nc, k.P, k.I, k.S
    if True:
        T0 = ph * OWN
        with ExitStack() as es:
            sb = lambda n, s, d: es.enter_context(nc.sbuf_tensor(n, s, d))
            hnT = sb(f"hnT{ph}", [128, 16, 18 * 128], BF16)
            tiles = [(s_, T0 + (s_ - 1) * 128) for s_ in range(1, 17)]
            tiles.append((17, OWN) if ph == 0 else (0, OWN - 128))
            norm_stage(k, I["x"], tiles, "n1_bc", hnT, f"a{ph}")
            with ExitStack() as es2:
                sb2 = lambda n, s, d: es2.enter_context(nc.sbuf_tensor(n, s, d))
                Wb = [sb2(f"bW{ph}{i}", [128, 16, 512], BF16) for i in range(2)]
                Wg = [sb2(f"bWg{ph}{i}", [128, 16, 64], BF16) for i in range(2)]
                praw = [sb2(f"bP{ph}{i}", [128, OWN + 2], F32) for i in range(2)]
                tmpf = [sb2(f"bT{ph}{i}", [128, OWN], F32) for i in range(2)]
                cvo = [sb2(f"bC{ph}{i}", [128, OWN], BF16) for i in range(3)]
                TM = [sb2(f"bTM{ph}{i}", [128, 16, 512], BF16) for i in range(1)]
                vst = [sb2(f"bV{ph}{i}", [128, 4, 512], BF16) for i in range(2)]
                gst = [sb2(f"bG{ph}{i}", [64, OWN], F32) for i in range(2)]
                cwb = sb2(f"bcwb{ph}", [128, 20, 4], F32)
                tcw = P.dma("sp", cwb[:], I["cwb"], sem="bcw")
                padcol = 0 if ph == 0 else OWN + 1
                tpad = None
                for i in range(2):
                    tpad = P.op("dve", (lambda e, i=i: e.memset(praw[i][:, padcol:padcol + 1], 0.0)))
                WR, PR, TR_, CR, TMR, VR, GR, WGR = Ring(Wb), Ring(praw), Ring(tmpf), Ring(cvo), Ring(TM), Ring(vst), Ring(gst), Ring(Wg)
                groups = [("hy", 0, S["HV"]), ("hy", 1, S["HX1"]), ("hy", 2, S["HX2"]), ("q", 0, S["QT"]), ("k", 1, S["KT"])]
                for gi_, (kind, gsub, dstd) in enumerate(groups):
                    wsrc = I["w_hy"][:, gsub * 512:(gsub + 1) * 512] if kind == "hy" else I["w_qk"][:, gsub * 512:(gsub + 1) * 512]
                    wi, wb, wfree = WR.next()
                    tw = load_w_bf16(k, wb[:], wsrc, wfree, f"bw{wi}")
                    wusers = []
                    need_tm = kind in ("hy", "k")
                    if need_tm:
                        tmi, tmb, tmfree = TMR.next()
                        tm_evs = []
                    pending = []
                    for m in range(4):
                        tix = gi_ * 4 + m
                        pi_, pr, pfree = PR.next()
                        evs = []
                        for tb in range(5):
                            idx, ps, fr = k.psf.next()
                            if tb < 4:
                                c0, n = 128 + tb * 512, 512
                                o0 = 1 + tb * 512
                            else:
                                c0, n = (128 + OWN, 1) if ph == 0 else (127, 1)
                                o0 = OWN + 1 if ph == 0 else 0
                            last = None
                            for kc in range(16):
                                last = P.op("pe", (lambda e, ps=ps, wb=wb, kc=kc, m=m, c0=c0, n=n: e.matmul(
                                    ps[:, 0:n], lhsT=wb[:, kc, m * 128:(m + 1) * 128], rhs=hnT[:, kc, c0:c0 + n],
                                    start=(kc == 0), stop=(kc == 15))), deps=[tw] + fr, sem=("pe" if kc == 15 else None))
                            wusers.append(last)
                            te = P.op("act", (lambda e, ps=ps, pr=pr, o0=o0, n=n: e.activation(out=pr[:, o0:o0 + n], in_=ps[:, 0:n], func=AF.Copy)),
                                      deps=[last] + pfree + [tpad])
                            k.psf.release(idx, [te])
                            evs.append(te)
                        while pending:
                            pending.pop(0)()
                        ti_, tf, tfree = TR_.next()
                        ci, cb, cfree = CR.next()
                        a1 = P.op("act", (lambda e, pr=pr, tf=tf, tix=tix: e.activation(out=tf[:], in_=pr[:, 1:OWN + 1], func=AF.Identity,
                                                                                       scale=cwb[:, tix, 1:2], bias=cwb[:, tix, 3:4])),
                                  deps=[evs[-1], tcw] + tfree)
                        d1 = P.op("dve", (lambda e, pr=pr, tf=tf, tix=tix: e.scalar_tensor_tensor(out=tf[:], in0=pr[:, 0:OWN], scalar=cwb[:, tix, 0:1],
                                                                                                 in1=tf[:], op0=ALU.mult, op1=ALU.add)), deps=[a1])
                        if kind == "hy":
                            d2 = P.op("dve", (lambda e, pr=pr, tf=tf, cb=cb, tix=tix: e.scalar_tensor_tensor(
                                out=cb[:], in0=pr[:, 2:OWN + 2], scalar=cwb[:, tix, 2:3], in1=tf[:], op0=ALU.mult, op1=ALU.add)),
                                deps=[d1] + cfree)
                            PR.release(pi_, [d2])
                            TR_.release(ti_, [d2])
                            cready = d2
                        else:
                            d2 = P.op("dve", (lambda e, pr=pr, tf=tf, tix=tix: e.scalar_tensor_tensor(
                                out=tf[:], in0=pr[:, 2:OWN + 2], scalar=cwb[:, tix, 2:3], in1=tf[:], op0=ALU.mult, op1=ALU.add)),
                                deps=[d1])
                            PR.release(pi_, [d2])
                            a2 = P.op("act", (lambda e, tf=tf, cb=cb: e.activation(out=cb[:], in_=tf[:], func=AF.Silu)), deps=[d2] + cfree)
                            TR_.release(ti_, [a2])
                            cready = a2
                        cusers = []
                        if kind in ("q", "k"):
                            dq = P.dma("sp", dstd[m, :, T0:T0 + OWN], cb[:], deps=[cready], sem=f"bq{ci}")
                            cusers.append(dq)
                        if need_tm:
                            def make_tr(m=m, cb=cb, ci=ci, cready=cready, cusers=cusers):
                                def do_tr():
                                    lastpe = None
                                    for half in range(2):
                                        bidx, pb, bfr = k.psb.next()
                                        for j in range(8):
                                            tt = half * 8 + j
                                            lastpe = P.op("pe", (lambda e, pb=pb, cb=cb, j=j, tt=tt: e.transpose(
                                                out=pb[:, j * 128:(j + 1) * 128], in_=cb[:, tt * 128:(tt + 1) * 128], identity=k.ident[:])),
                                                deps=[cready] + bfr, sem=("pe" if j == 7 else None))
                                        dstv = tmb[:, half * 8:(half + 1) * 8, m * 128:(m + 1) * 128]
                                        srcv = pb[:, :].rearrange("p (t c) -> p t c", c=128)
                                        if half == 0:
                                            tev = P.op("act", (lambda e, dstv=dstv, srcv=srcv: e.activation(out=dstv, in_=srcv, func=AF.Copy)),
                                                       deps=[lastpe] + tmfree)
                                        else:
                                            tev = P.op("dve", (lambda e, dstv=dstv, srcv=srcv: e.tensor_copy(out=dstv, in_=srcv)),
                                                       deps=[lastpe] + tmfree)
                                        k.psb.release(bidx, [tev])
                                        tm_evs.append(tev)
                                    CR.release(ci, cusers + [lastpe])
                                return do_tr
                            pending.append(make_tr())
                        else:
                            CR.release(ci, cusers)
                    while pending:
                        pending.pop(0)()
                    WR.release(wi, [wusers[-1]])
                    if need_tm:
                        dsttm = S["KTM"] if kind == "k" else dstd
                        dtm = P.dma("sp", dsttm[T0:T0 + OWN, :].rearrange("(tt p) c -> p tt c", p=128), tmb[:], deps=tm_evs[-2:], sem=f"btm{tmi}")
                        TMR.release(tmi, [dtm])
                for gsub, dstd in ((0, S["VTM"]), (1, S["OTM"])):
                    wi, wb, wfree = WR.next()
                    tw = load_w_bf16(k, wb[:], I["w_vo"][:, gsub * 512:(gsub + 1) * 512], wfree, f"bw{wi}")
                    last = None
                    for tt in range(16):
                        if tt % 4 == 0:
                            vi, vb, vfree = VR.next()
                            vevs = []
                        idx, ps, fr = k.psf.next()
                        for kc in range(16):
                            last = P.op("pe", (lambda e, ps=ps, wb=wb, kc=kc, tt=tt: e.matmul(
                                ps[:], lhsT=hnT[:, kc, 128 + tt * 128:128 + (tt + 1) * 128], rhs=wb[:, kc, :],
                                start=(kc == 0), stop=(kc == 15))), deps=[tw] + fr, sem=("pe" if kc == 15 else None))
                        fn_ = AF.Copy if gsub == 0 else AF.Sigmoid
                        te = P.op("act", (lambda e, ps=ps, vb=vb, tt=tt, fn_=fn_: e.activation(out=vb[:, tt % 4, :], in_=ps[:], func=fn_)),
                                  deps=[last] + vfree)
                        k.psf.release(idx, [te])
                        vevs.append(te)
                        if tt % 4 == 3:
                            r0 = T0 + (tt - 3) * 128
                            dv = P.dma("sp", dstd[r0:r0 + 512, :].rearrange("(j p) c -> p j c", p=128), vb[:], deps=[vevs[-1]], sem=f"bv{vi}")
                            VR.release(vi, [dv])
                    WR.release(wi, [last])
                for gname, dstd in (("w_gi", S["IG"]), ("w_gf", S["FG"])):
                    wi, wb, wfree = WGR.next()
                    tw = load_w_bf16(k, wb[:], I[gname], wfree, f"bwg{wi}")
                    gi2, gb_, gfree = GR.next()
                    gev = None
                    last = None
                    for tb in range(4):
                        idx, ps, fr = k.psf.next()
                        for kc in range(16):
                            last = P.op("pe", (lambda e, ps=ps, wb=wb, kc=kc, tb=tb: e.matmul(
                                ps[0:64, :], lhsT=wb[:, kc, :], rhs=hnT[:, kc, 128 + tb * 512:128 + (tb + 1) * 512],
                                start=(kc == 0), stop=(kc == 15))), deps=[tw] + fr, sem=("pe" if kc == 15 else None))
                        gev = P.op("act", (lambda e, ps=ps, gb_=gb_, tb=tb: e.activation(out=gb_[:, tb * 512:(tb + 1) * 512], in_=ps[0:64, :], func=AF.Copy)),
                                   deps=[last] + gfree)
                        k.psf.release(idx, [gev])
                    WGR.release(wi, [last])
                    dg = P.dma("sp", dstd[:, T0:T0 + OWN], gb_[:], deps=[gev], sem=f"bg{gi2}")
                    GR.release(gi2, [dg])
            P.barrier()


def phase_B2(k):
    nc, P, I, S = k.nc, k.P, k.I, k.S
    with ExitStack() as es:
        sb = lambda n, s, d: es.enter_context(nc.sbuf_tensor(n, s, d))
        hnT = sb("hnTo", [128, 16, OWN], BF16)
        norm_stage(k, I["x_own"], [(s_, s_ * 128) for s_ in range(16)], "n1_bc", hnT, "b2", excl=("cc",))
        with ExitStack() as es2:
            sb2 = lambda n, s, d: es2.enter_context(nc.sbuf_tensor(n, s, d))
            Wb = [sb2(f"mW{i}", [128, 16, 512], BF16) for i in range(2)]
            gst = [sb2(f"mG{i}", [128, OWN], BF16) for i in range(3)]
            WR, GR = Ring(Wb), Ring(gst)
            for g in range(8):
                wi, wb, wfree = WR.next()
                tw = load_w_bf16(k, wb[:], I["w_mg"][:, g * 512:(g + 1) * 512], wfree, f"mw{wi}")
                last = None
                for m in range(4):
                    gi2, gb_, gfree = GR.next()
                    ev = None
                    for tb in range(4):
                        idx, ps, fr = k.psf.next()
                        for kc in range(16):
                            last = P.op("pe", (lambda e, ps=ps, wb=wb, kc=kc, m=m, tb=tb: e.matmul(
                                ps[:], lhsT=wb[:, kc, m * 128:(m + 1) * 128], rhs=hnT[:, kc, tb * 512:(tb + 1) * 512],
                                start=(kc == 0), stop=(kc == 15))), deps=[tw] + fr, sem=("pe" if kc == 15 else None))
                        ev = P.op("act", (lambda e, ps=ps, gb_=gb_, tb=tb: e.activation(out=gb_[:, tb * 512:(tb + 1) * 512], in_=ps[:], func=AF.Sigmoid)),
                                  deps=[last] + gfree)
                        k.psf.release(idx, [ev])
                    dt_ = (g % 4) * 4 + m
                    dstd = S["GAT"] if g < 4 else S["GBT"]
                    dg = P.dma("sp", dstd[dt_, :, :], gb_[:], deps=[ev], sem=f"mg{gi2}")
                    GR.release(gi2, [dg])
                WR.release(wi, [last])
    P.barrier()


def rev(ap2, n=None):
    (ps, pn), (fs, fn) = ap2.ap
    return bass.AP(ap2.tensor, ap2.offset + (fn - 1) * fs, [[ps, pn], [-fs, fn]])


def phase_M(k):
    nc, P, I, S = k.nc, k.P, k.I, k.S
    with ExitStack() as es:
        sb = lambda n, s, d: es.enter_context(nc.sbuf_tensor(n, s, d))
        WT = sb("mWT", [128, 32, 64], F32)
        FT = sb("mFT", [128, 32, 64], F32)
        RB = sb("mRB", [128, 64 * 32], F32)
        maskf = sb("mmaskf", [128, 128], F32)
        maskb = sb("mmaskb", [128, 128], F32)
        tmk = [P.dma("sp", maskf[:], I["maskf"], sem="mk"), P.dma("sp", maskb[:], I["maskb"], sem="mk")]
        tmk = [tmk[-1]]
        with ExitStack() as es2:
            sb2 = lambda n, s, d: es2.enter_context(nc.sbuf_tensor(n, s, d))
            IG = sb2("pIG", [64, L], F32)
            FG = sb2("pFG", [64, L], F32)
            RM = sb2("pRM", [64, L], F32)
            Lc = sb2("pLc", [64, L], F32)
            G = sb2("pG", [64, L], F32)
            TA = sb2("pTA", [64, L], F32)
            Wp = sb2("pWp", [64, L], F32)
            FL = sb2("pFL", [64, L], F32)
            gb = sb2("pgb", [64, 2], F32)
            cst = sb2("pcst", [64, 4], F32)
            sm = sb2("psm", [64, 8, 32], F32)
            ld = [P.dma("sp", IG[:], S["IG"], sem="pl"), P.dma("sp", FG[:], S["FG"], sem="pl"),
                  P.dma("sp", RM[:], I["resetmask"], sem="pl"), P.dma("sp", gb[:], I["gbias"], sem="pl")]
            ld = [ld[-1]]
            c1 = P.op("dve", lambda e: e.tensor_scalar(out=cst[:, 0:1], in0=gb[:, 1:2], scalar1=-1.0, scalar2=None, op0=ALU.mult), deps=ld)
            c2 = P.op("dve", lambda e: e.memset(cst[:, 1:2], 1.0))
            c3 = P.op("dve", lambda e: e.memset(cst[:, 2:3], float(np.log(1.0 / 16.0))))
            c4 = P.op("dve", lambda e: e.memset(sm[:, 4, :], 0.0))
            a1 = P.op("act", lambda e: e.activation(out=TA[:], in_=FG[:], func=AF.Exp, scale=-1.0, bias=cst[:, 0:1]), deps=[c1] + ld)
            a2 = P.op("act", lambda e: e.activation(out=FG[:], in_=TA[:], func=AF.Ln, scale=1.0, bias=cst[:, 1:2]), deps=[a1, c2])
            s1 = P.op("dve", lambda e: e.tensor_tensor_scan(out=Lc[0:32, :], data0=RM[0:32, :], data1=FG[0:32, :], initial=0.0,
                                                            op0=ALU.mult, op1=ALU.add), deps=[a2])
            s2 = P.op("dve", lambda e: e.tensor_tensor_scan(out=rev(Lc[32:64, :]), data0=rev(RM[32:64, :]), data1=rev(FG[32:64, :]), initial=0.0,
                                                            op0=ALU.mult, op1=ALU.add), deps=[a2])
            g1 = P.op("dve", lambda e: e.scalar_tensor_tensor(out=G[:], in0=IG[:], scalar=gb[:, 0:1], in1=Lc[:], op0=ALU.add, op1=ALU.add), deps=[s2])
            g2 = P.op("dve", lambda e: e.tensor_reduce(out=sm[:, 0, :], in_=G[:, :].rearrange("p (c j) -> p c j", j=128), axis=AX.X, op=ALU.max), deps=[g1])
            lcv = Lc[:, :].rearrange("p (c j) -> p c j", j=128)
            n1 = P.op("dve", lambda e: e.tensor_copy(out=sm[0:32, 6, :], in_=lcv[0:32, :, 127]), deps=[g2])
            n2 = P.op("dve", lambda e: e.tensor_copy(out=sm[32:64, 6, :], in_=lcv[32:64, :, 0]), deps=[n1])
            n3 = P.op("dve", lambda e: e.tensor_scalar(out=sm[:, 1, :], in0=sm[:, 6, :], scalar1=-1.0, scalar2=None, op0=ALU.mult), deps=[n2])
            m1 = P.op("dve", lambda e: e.tensor_tensor_scan(out=sm[0:32, 2, :], data0=sm[0:32, 0, :], data1=sm[0:32, 1, :], initial=0.0,
                                                            op0=ALU.max, op1=ALU.add), deps=[n3])
            m2 = P.op("dve", lambda e: e.tensor_tensor_scan(out=rev(sm[32:64, 2, :]), data0=rev(sm[32:64, 0, :]), data1=rev(sm[32:64, 1, :]), initial=0.0,
                                                            op0=ALU.max, op1=ALU.add), deps=[m1])
            m3 = P.op("dve", lambda e: e.tensor_tensor(out=sm[:, 3, :], in0=sm[:, 2, :], in1=sm[:, 6, :], op=ALU.add), deps=[m2])
            m4 = P.op("dve", lambda e: e.tensor_copy(out=sm[0:32, 4, 1:32], in_=sm[0:32, 2, 0:31]), deps=[m3, c4])
            m5 = P.op("dve", lambda e: e.tensor_copy(out=sm[32:64, 4, 0:31], in_=sm[32:64, 2, 1:32]), deps=[m4])
            m6 = P.op("dve", lambda e: e.tensor_tensor(out=sm[:, 5, :], in0=sm[:, 4, :], in1=sm[:, 3, :], op=ALU.subtract), deps=[m5])
            m7 = P.op("act", lambda e: e.activation(out=sm[:, 5, :], in_=sm[:, 5, :], func=AF.Exp), deps=[m6])
            dr = P.dma("sp", S["RHO"], sm[:, 5, :], deps=[m7], sem="prho")
            rsrc = bass.AP(S["RHO"].tensor, S["RHO"].offset, [[0, 128], [1, 64 * 32]])
            dr2 = P.dma("sp", RB[:], rsrc, deps=[dr], sem="prho")
            gcb = bass.AP(sm[:, 3, :].tensor, sm[:, 3, :].offset, [list(sm[:, 3, :].ap[0]), [1, 32], [0, 128]])
            w1 = P.op("dve", lambda e: e.tensor_tensor(out=TA[:, :].rearrange("p (c j) -> p c j", j=128), in0=G[:, :].rearrange("p (c j) -> p c j", j=128),
                                                       in1=gcb, op=ALU.subtract), deps=[m3, a2])
            w2 = P.op("act", lambda e: e.activation(out=Wp[:], in_=TA[:], func=AF.Exp, bias=cst[:, 2:3], scale=1.0), deps=[w1, c3])
            w3 = P.op("dve", lambda e: e.tensor_tensor(out=G[:, :].rearrange("p (c j) -> p c j", j=128), in0=lcv, in1=gcb, op=ALU.subtract), deps=[w1])
            w4 = P.op("act", lambda e: e.activation(out=FL[:], in_=G[:], func=AF.Exp), deps=[w3])
            for (src, dstT, tsrc) in ((Wp, WT, w2), (FL, FT, w4)):
                for blk in range(4):
                    idx, ps, fr = k.psf.next()
                    last = None
                    for j in range(8):
                        c = blk * 8 + j
                        last = P.op("pe", (lambda e, ps=ps, src=src, j=j, c=c: e.transpose(out=ps[:, j * 64:(j + 1) * 64], in_=src[:, c * 128:(c + 1) * 128],
                                                                                          identity=k.identf[0:64, 0:64])),
                                    deps=[tsrc] + fr + k.c_tok, sem=("pe" if j == 7 else None))
                    te = P.op("dve", (lambda e, ps=ps, dstT=dstT, blk=blk: e.tensor_copy(out=dstT[:, blk * 8:(blk + 1) * 8, :],
                                                                                       in_=ps[:, :].rearrange("p (c r) -> p c r", r=64))), deps=[last])
                    k.psf.release(idx, [te])
            prep_done = dr2
        P.barrier()
        for hl in range(2):
            _m_head(k, hl, WT, FT, RB, maskf, maskb)
    P.barrier()


def _m_head(k, hl, WT, FT, RB, maskf, maskb):
    nc, P, I, S = k.nc, k.P, k.I, k.S
    with ExitStack() as es:
        sb = lambda n, s, d: es.enter_context(nc.sbuf_tensor(n, s, d))
        qT = sb(f"hq{hl}", [128, 2, L], BF16)
        kT = sb(f"hk{hl}", [128, 2, L], BF16)
        KM = sb(f"hkm{hl}", [128, 32, 256], BF16)
        VA = sb(f"hva{hl}", [128, 32, 257], BF16)
        OH = sb(f"hoh{hl}", [128, 32, 256], BF16)
        HF = sb(f"hhf{hl}", [128, 32, 256], F32)
        YB = sb(f"hyb{hl}", [128, 32, 256], BF16)
        Cst = [sb(f"hC{hl}{d}", [128, 2, 257], F32) for d in range(2)]
        Cs = [[sb(f"hCs{hl}{d}{i}", [128, 2, 257], BF16) for i in range(2)] for d in range(2)]
        SKA = 2
        dCs = [sb(f"hdC{hl}{i}", [128, 2, 257], F32) for i in range(2 * (SKA + 1) + 1)]
        kt_ = [sb(f"hkt{hl}{i}", [128, 256], BF16) for i in range(4)]
        St = [sb(f"hSt{hl}{i}", [128, 128], BF16) for i in range(2 * (SKA + 1) + 1)]
        dn = [sb(f"hdn{hl}{i}", [128, 2], F32) for i in range(4)]
        ht = [sb(f"hht{hl}{i}", [128, 256], F32) for i in range(3)]
        lds = []
        for dt in range(2):
            lds.append(P.dma("sp", qT[:, dt, :], S["QT"][hl * 2 + dt], sem="hl0"))
            lds.append(P.dma("sp", kT[:, dt, :], S["KT"][hl * 2 + dt], sem="hl0"))
        hc = slice(hl * 256, (hl + 1) * 256)
        lds.append(P.dma("sp", KM[:], S["KTM"][:, hc].rearrange("(c s) d -> s c d", s=128), sem="hl0"))
        lds.append(P.dma("sp", VA[:, :, 0:256], S["VTM"][:, hc].rearrange("(c s) d -> s c d", s=128), sem="hl0"))
        lds.append(P.dma("sp", OH[:], S["OTM"][:, hc].rearrange("(c s) d -> s c d", s=128), sem="hl0"))
        lds = [lds[-1]]
        tone = P.op("dve", lambda e: e.memset(VA[:, :, 256:257], 1.0))
        ready = lds + [tone]
        KR, SR, DR, HR, DCR = Ring(kt_), Ring(St), Ring(dn), Ring(ht), Ring(dCs)
        CR = [Ring(Cs[0]), Ring(Cs[1])]
        R8 = k.ps8()
        hf_tok = {}
        ylast = [None]
        stA = {}
        st = []
        for dr_ in range(2):
            tz = P.op("dve", (lambda e, dr_=dr_: e.memset(Cst[dr_][:], 0.0)))
            st.append({"row": 32 * dr_ + hl, "mask": maskf if dr_ == 0 else maskb,
                       "order": list(range(32)) if dr_ == 0 else list(range(31, -1, -1)),
                       "c_upd": [tz], "cs_cur": None})

        def stageA(dr_, oi):
            sd = st[dr_]
            c = sd["order"][oi]
            row, mask = sd["row"], sd["mask"]
            csl = slice(c * 128, (c + 1) * 128)
            wcol = WT[:, c, row:row + 1]
            i1, pS, f1 = R8.next()
            P.op("pe", (lambda e, pS=pS, csl=csl: e.matmul(pS[:, 0:128], lhsT=kT[:, 0, csl], rhs=qT[:, 0, csl], start=True, stop=False)),
                 deps=ready + f1, sem=None)
            tS = P.op("pe", (lambda e, pS=pS, csl=csl: e.matmul(pS[:, 0:128], lhsT=kT[:, 1, csl], rhs=qT[:, 1, csl], start=False, stop=True)))
            si, stb, sfree = SR.next()
            t2 = P.op("dve", (lambda e, pS=pS, stb=stb, wcol=wcol, mask=mask: e.scalar_tensor_tensor(
                out=stb[:], in0=pS[:, 0:128], scalar=wcol, in1=mask[:], op0=ALU.mult, op1=ALU.mult)), deps=[tS] + sfree)
            R8.release(i1, [t2])
            dinfo = None
            if oi < 31:
                ki, kb, kfree = KR.next()
                t8 = P.op("act", (lambda e, kb=kb, c=c, wcol=wcol: e.activation(out=kb[:], in_=KM[:, c, :], func=AF.Copy, scale=wcol)),
                          deps=ready + kfree)
                di_, dcb, dcfree = DCR.next()
                last9 = None
                tdc = None
                for kt in range(2):
                    i3, pC, f3 = R8.next()
                    last9 = P.op("pe", (lambda e, pC=pC, kb=kb, kt=kt, c=c: e.matmul(pC[:, 0:257], lhsT=kb[:, kt * 128:(kt + 1) * 128], rhs=VA[:, c, :],
                                                                                  start=True, stop=True)), deps=[t8] + f3)
                    if kt == 0:
                        tdc0 = P.op("act", (lambda e, pC=pC, dcb=dcb, kt=kt: e.activation(out=dcb[:, kt, :], in_=pC[:, 0:257], func=AF.Copy)),
                                    deps=[last9] + dcfree)
                        R8.release(i3, [tdc0])
                    else:
                        tdc1 = P.op("dve", (lambda e, pC=pC, dcb=dcb, kt=kt: e.tensor_copy(out=dcb[:, kt, :], in_=pC[:, 0:257])),
                                    deps=[last9] + dcfree)
                        R8.release(i3, [tdc1])
                        tdc = [tdc0, tdc1]
                KR.release(ki, [last9])
                dinfo = (di_, dcb, tdc)
            stA[(dr_, oi)] = (si, stb, t2, dinfo)

        def stageB(dr_, oi):
            sd = st[dr_]
            c = sd["order"][oi]
            row = sd["row"]
            csl = slice(c * 128, (c + 1) * 128)
            fcol = FT[:, c, row:row + 1]
            rho_c = RB[:, row * 32 + c:row * 32 + c + 1]
            si, stb, t2, dinfo = stA.pop((dr_, oi))
            i2, pN, f2 = R8.next()
            first = (oi == 0)
            tN = P.op("pe", (lambda e, pN=pN, stb=stb, c=c, first=first: e.matmul(pN[:, 0:257], lhsT=stb[:], rhs=VA[:, c, :], start=True, stop=first)),
                      deps=[t2] + f2, sem=("pe" if first else None))
            if not first:
                cs_idx, cbuf, ctoks = sd["cs_cur"]
                P.op("pe", (lambda e, pN=pN, cbuf=cbuf, csl=csl: e.matmul(pN[:, 0:257], lhsT=qT[:, 0, csl], rhs=cbuf[:, 0, :], start=False, stop=False)),
                     deps=ctoks, sem=None)
                tN = P.op("pe", (lambda e, pN=pN, cbuf=cbuf, csl=csl: e.matmul(pN[:, 0:257], lhsT=qT[:, 1, csl], rhs=cbuf[:, 1, :], start=False, stop=True)))
                CR[dr_].release(cs_idx, [tN])
            SR.release(si, [tN])
            di, dnb, dfree = DR.next()
            t4a = P.op("act", (lambda e, pN=pN, dnb=dnb: e.activation(out=dnb[:, 0:1], in_=pN[:, 256:257], func=AF.Abs)), deps=[tN] + dfree)
            t4 = P.op("dve", (lambda e, dnb=dnb, fcol=fcol: e.tensor_tensor(out=dnb[:, 0:1], in0=dnb[:, 0:1], in1=fcol, op=ALU.max)), deps=[t4a])
            t5 = P.op("dve", (lambda e, dnb=dnb: e.reciprocal(out=dnb[:, 1:2], in_=dnb[:, 0:1])), deps=[t4])
            if c not in hf_tok:
                t6 = P.op("act", (lambda e, pN=pN, dnb=dnb, c=c: e.activation(out=HF[:, c, :], in_=pN[:, 0:256], func=AF.Copy, scale=dnb[:, 1:2])),
                          deps=[t5])
                R8.release(i2, [t6])
                DR.release(di, [t6])
                hf_tok[c] = t6
            else:
                hi_, hb_, hfree = HR.next()
                t6 = P.op("dve", (lambda e, pN=pN, dnb=dnb, c=c, hb_=hb_: e.scalar_tensor_tensor(
                    out=hb_[:], in0=pN[:, 0:256], scalar=dnb[:, 1:2], in1=HF[:, c, :], op0=ALU.mult, op1=ALU.add)), deps=[t5, hf_tok[c]] + hfree)
                R8.release(i2, [t6])
                DR.release(di, [t6])
                t7 = P.op("pool", (lambda e, hb_=hb_, c=c: e.tensor_tensor(out=YB[:, c, :], in0=hb_[:], in1=OH[:, c, :], op=ALU.mult)),
                          deps=[t6] + ready)
                HR.release(hi_, [t7])
                ylast[0] = t7
            if oi < 31:
                cn = sd["order"][oi + 1]
                rho_n = RB[:, row * 32 + cn:row * 32 + cn + 1]
                di_, dcb, tdc = dinfo
                cs_idx, cbn, cfree = CR[dr_].next()
                Cd = Cst[dr_]
                ups = []
                tu = P.op("dve", (lambda e, Cd=Cd, dcb=dcb, rho_c=rho_c: e.scalar_tensor_tensor(
                    out=Cd[:, :, :], in0=Cd[:, :, :], scalar=rho_c, in1=dcb[:, :, :], op0=ALU.mult, op1=ALU.add)), deps=tdc + sd["c_upd"])
                ts = P.op("act", (lambda e, cbn=cbn, Cd=Cd, rho_n=rho_n: e.activation(out=cbn[:, :, :], in_=Cd[:, :, :], func=AF.Copy, scale=rho_n)),
                          deps=[tu] + cfree)
                ups.append(ts)
                DCR.release(di_, [tu])
                sd["c_upd"] = [ups[-1]]
                sd["cs_cur"] = (cs_idx, cbn, [ups[-1]])

        for i in range(32 + SKA):
            if i < 32:
                for dr_ in range(2):
                    stageA(dr_, i)
            if i >= SKA:
                for dr_ in range(2):
                    stageB(dr_, i - SKA)
        ccv = S["CCIN"][:, 512 + hl * 256:512 + (hl + 1) * 256].rearrange("(c s) d -> s c d", s=128)
        P.dma("sp", ccv, YB[:], deps=[ylast[0]], sem="hyb")
    P.barrier()


def phase_X(k):
    P, S = k.P, k.S
    if hasattr(k, "ccin_ext"):
        P.dma("sp", S["CCIN"], k.ccin_ext, sem="xcp")
        P.barrier()
    rg = [[0, 1], [2, 3], [4, 5], [6, 7]]
    for j in range(4):
        src = S["CCIN"][j * 1024:(j + 1) * 1024, :]
        dst = S["CCOUT"][j * 2048:(j + 1) * 2048, :]
        P.op("pool", (lambda e, src=src, dst=dst: e.collective_compute("AllGather", ALU.bypass, replica_groups=rg,
                                                                       ins=[src.opt()], outs=[dst.opt()])), sem="cc", inc=1)


def phase_C(k):
    nc, P, I, S = k.nc, k.P, k.I, k.S
    with ExitStack() as es:
        sb = lambda n, s, d: es.enter_context(nc.sbuf_tensor(n, s, d))
        mixedT = sb("cmix", [128, 16, OWN], BF16)
        with ExitStack() as es1:
            sb1 = lambda n, s, d: es1.enter_context(nc.sbuf_tensor(n, s, d))
            yT = sb1("cyT", [128, 16, OWN], BF16)
            with ExitStack() as es2:
                sb2 = lambda n, s, d: es2.enter_context(nc.sbuf_tensor(n, s, d))
                Yt = [sb2(f"cY{i}", [128, 2, 1024], BF16) for i in range(3)]
                YR = Ring(Yt)
                def cpy(e, jj):
                    par = e.partition_id() % 2
                    return e.dma_start(out=S["YOWN"][jj], in_=S["CCOUT"][bass.ds(par * 4096 + jj * 2048, 2048), :])
                P.op("sp", (lambda e: cpy(e, 0)), sem="cyo", inc=16)
                tcp = P.op("sp", (lambda e: cpy(e, 1)), sem="cyo", inc=16)
                for tt in range(16):
                    yi, yb, yfree = YR.next()
                    jj, t8 = tt // 8, tt % 8
                    srcv = S["YOWN"][jj].rearrange("(r t) c -> t r c", r=2)[t8 * 128:(t8 + 1) * 128, :, :]
                    ty = P.dma("sp", yb[:], srcv, deps=[tcp] + yfree, sem=f"cy{yi}")
                    lastpe = None
                    for half in range(2):
                        bidx, pb, bfr = k.psb.next()
                        for j in range(8):
                            r, ct = j // 4, j % 4
                            c0 = half * 512 + ct * 128
                            lastpe = P.op("pe", (lambda e, pb=pb, yb=yb, j=j, r=r, c0=c0: e.transpose(
                                out=pb[:, j * 128:(j + 1) * 128], in_=yb[:, r, c0:c0 + 128], identity=k.ident[:])),
                                deps=[ty] + bfr + k.c_tok, sem=("pe" if j == 7 else None))
                        dstv = yT[:, half * 8:(half + 1) * 8, tt * 128:(tt + 1) * 128]
                        srcv = pb[:, :].rearrange("p (k t) -> p k t", t=128)
                        if half == 0:
                            tev = P.op("act", (lambda e, dstv=dstv, srcv=srcv: e.activation(out=dstv, in_=srcv, func=AF.Copy)), deps=[lastpe])
                        else:
                            tev = P.op("dve", (lambda e, dstv=dstv, srcv=srcv: e.tensor_copy(out=dstv, in_=srcv)), deps=[lastpe])
                        k.psb.release(bidx, [tev])
                    YR.release(yi, [lastpe])
            P.barrier()
            with ExitStack() as es2:
                sb2 = lambda n, s, d: es2.enter_context(nc.sbuf_tensor(n, s, d))
                Wa = [sb2(f"cWa{i}", [128, 8, 512], BF16) for i in range(2)]
                Wb = [sb2(f"cWb{i}", [128, 8, 512], BF16) for i in range(2)]
                GA = [sb2(f"cGA{i}", [128, OWN], BF16) for i in range(2)]
                GB = [sb2(f"cGB{i}", [128, OWN], BF16) for i in range(2)]
                t1s = [sb2(f"ct1{i}", [128, 512], F32) for i in range(2)]
                t2s = [sb2(f"ct2{i}", [128, 512], F32) for i in range(2)]
                WAR, WBR, GAR, GBR, T1R, T2R = Ring(Wa), Ring(Wb), Ring(GA), Ring(GB), Ring(t1s), Ring(t2s)
                for g in range(4):
                    wai, wa, wafree = WAR.next()
                    twa = load_w_bf16(k, wa[:], I["w_ba"][:, g * 512:(g + 1) * 512], wafree, f"cwa{wai}")
                    wbi, wb, wbfree = WBR.next()
                    twb = load_w_bf16(k, wb[:], I["w_bb"][:, g * 512:(g + 1) * 512], wbfree, f"cwb{wbi}")
                    lastb = None
                    for m in range(4):
                        dt_ = g * 4 + m
                        gai, ga, gafree = GAR.next()
                        tga = P.dma("sp", ga[:], S["GAT"][dt_], deps=gafree, sem=f"cga{gai}")
                        gbi, gb, gbfree = GBR.next()
                        tgb = P.dma("sp", gb[:], S["GBT"][dt_], deps=gbfree, sem=f"cgb{gbi}")
                        ua = ub = None
                        for tb in range(4):
                            ts_ = slice(tb * 512, (tb + 1) * 512)
                            ia, pa, fa = k.psf.next()
                            la = None
                            for ct in range(8):
                                la = P.op("pe", (lambda e, pa=pa, wa=wa, ct=ct, m=m, ts_=ts_: e.matmul(
                                    pa[:], lhsT=wa[:, ct, m * 128:(m + 1) * 128], rhs=yT[:, ct, ts_], start=(ct == 0), stop=(ct == 7))),
                                    deps=[twa] + fa, sem=("pe" if ct == 7 else None))
                            ib, pb_, fb = k.psf.next()
                            for ct in range(8):
                                lastb = P.op("pe", (lambda e, pb_=pb_, wb=wb, ct=ct, m=m, ts_=ts_: e.matmul(
                                    pb_[:], lhsT=wb[:, ct, m * 128:(m + 1) * 128], rhs=yT[:, 8 + ct, ts_], start=(ct == 0), stop=(ct == 7))),
                                    deps=[twb] + fb, sem=("pe" if ct == 7 else None))
                            i1, t1, f1 = T1R.next()
                            i2, t2, f2 = T2R.next()
                            ua = P.op("dve", (lambda e, pa=pa, ga=ga, t1=t1, ts_=ts_: e.tensor_tensor(out=t1[:], in0=pa[:], in1=ga[:, ts_], op=ALU.mult)),
                                      deps=[la, tga] + f1)
                            k.psf.release(ia, [ua])
                            ub = P.op("dve", (lambda e, pb_=pb_, gb=gb, t2=t2, ts_=ts_: e.tensor_tensor(out=t2[:], in0=pb_[:], in1=gb[:, ts_], op=ALU.mult)),
                                      deps=[lastb, tgb] + f2)
                            k.psf.release(ib, [ub])
                            um = P.op("pool", (lambda e, t1=t1, t2=t2, dt_=dt_, ts_=ts_: e.tensor_tensor(out=mixedT[:, dt_, ts_], in0=t1[:], in1=t2[:], op=ALU.add)),
                                      deps=[ub])
                            T1R.release(i1, [um])
                            T2R.release(i2, [um])
                        GAR.release(gai, [ua])
                        GBR.release(gbi, [ub])
                    WAR.release(wai, [lastb])
                    WBR.release(wbi, [lastb])
            P.barrier()
        with ExitStack() as es2:
            sb2 = lambda n, s, d: es2.enter_context(nc.sbuf_tensor(n, s, d))
            Wo = [sb2(f"cWo{i}", [128, 16, 512], BF16) for i in range(2)]
            xs = [sb2(f"cxs{i}", [128, 512], F32) for i in range(3)]
            x1s = [sb2(f"cx1{i}", [128, 512], F32) for i in range(3)]
            WR, XR, OR_ = Ring(Wo), Ring(xs), Ring(x1s)
            for nb in range(4):
                cs = slice(nb * 512, (nb + 1) * 512)
                wi, wo, wfree = WR.next()
                tw = load_w_bf16(k, wo[:], I["w_out"][:, cs], wfree, f"cwo{wi}")
                last = None
                for tt in range(16):
                    rs = slice(tt * 128, (tt + 1) * 128)
                    xi, xb, xfree = XR.next()
                    tx = P.dma("sp", xb[:], I["x_own"][rs, cs], deps=xfree, sem=f"cx{xi}")
                    idx, ps, fr = k.psf.next()
                    for dt_ in range(16):
                        last = P.op("pe", (lambda e, ps=ps, wo=wo, dt_=dt_, rs=rs: e.matmul(ps[:], lhsT=mixedT[:, dt_, rs], rhs=wo[:, dt_, :],
                                                                                         start=(dt_ == 0), stop=(dt_ == 15))),
                                    deps=[tw] + fr, sem=("pe" if dt_ == 15 else None))
                    oi, ob, ofree = OR_.next()
                    ta = P.op("dve", (lambda e, ps=ps, xb=xb, ob=ob: e.tensor_tensor(out=ob[:], in0=ps[:], in1=xb[:], op=ALU.add)), deps=[last, tx] + ofree)
                    k.psf.release(idx, [ta])
                    XR.release(xi, [ta])
                    do = P.dma("sp", S["X1"][rs, cs], ob[:], deps=[ta], sem=f"co{oi}")
                    OR_.release(oi, [do])
                WR.release(wi, [last])
        P.barrier()
    with ExitStack() as es:
        hn2T = es.enter_context(nc.sbuf_tensor("chn2T", [128, 16, OWN], BF16))
        norm_stage(k, S["X1"], [(s_, s_ * 128) for s_ in range(16)], "n2_bc", hn2T, "c4")
        P.dma("sp", S["HN2T"], hn2T[:], sem="chn")
        P.barrier()


def phase_D(k):
    nc, P, I, S = k.nc, k.P, k.I, k.S
    for TB in range(2):
        _d_block(k, TB)
    with ExitStack() as es:
        sb = lambda n, s, d: es.enter_context(nc.sbuf_tensor(n, s, d))
        nw = sb("dnw", [128, D], F32)
        xt = [sb(f"dx{i}", [128, D], F32) for i in range(4)]
        ot = [sb(f"do{i}", [128, D], F32) for i in range(3)]
        junk = sb("dj", [128, D], BF16)
        st = [sb(f"dst{i}", [128, 4], F32) for i in range(4)]
        tnw = P.dma("sp", nw[:], I["nf_bc"], sem="dnw")
        XR, OR_ = Ring(xt), Ring(ot)
        for tt in range(16):
            rs = slice(tt * 128, (tt + 1) * 128)
            xi, xb, xfree = XR.next()
            tx = P.dma("sp", xb[:], S["X2"][rs, :], deps=xfree, sem=f"dx{xi}")
            stt = st[xi]
            t1 = P.op("act", (lambda e, xb=xb, stt=stt: e.activation(out=junk[:], in_=xb[:], func=AF.Square, accum_out=stt[:, 0:1])), deps=[tx])
            t2 = P.op("act", (lambda e, stt=stt: e.activation(out=stt[:, 1:2], in_=stt[:, 0:1], func=AF.Sqrt, scale=1.0 / D, bias=k.epst[:])), deps=[t1])
            t3 = P.op("dve", (lambda e, stt=stt: e.reciprocal(out=stt[:, 2:3], in_=stt[:, 1:2])), deps=[t2])
            oi, ob, ofree = OR_.next()
            t4 = P.op("dve", (lambda e, xb=xb, ob=ob, stt=stt: e.scalar_tensor_tensor(out=ob[:], in0=xb[:], scalar=stt[:, 2:3], in1=nw[:],
                                                                                   op0=ALU.mult, op1=ALU.mult)), deps=[t3, tnw] + ofree)
            XR.release(xi, [t4])
            do = P.dma("sp", k.out[rs, :], ob[:], deps=[t4], sem=f"dout{oi}")
            OR_.release(oi, [do])
    P.barrier()


def _d_block(k, TB):
    nc, P, I, S = k.nc, k.P, k.I, k.S
    NT = 1024
    T0 = TB * NT
    with ExitStack() as es:
        sb = lambda n, s, d: es.enter_context(nc.sbuf_tensor(n, s, d))
        actT = sb(f"dact{TB}", [128, NFT, NT], BF16)
        with ExitStack() as es2:
            sb2 = lambda n, s, d: es2.enter_context(nc.sbuf_tensor(n, s, d))
            hn2 = sb2(f"dhn{TB}", [128, 16, NT], BF16)
            Wg = [sb2(f"dWg{TB}{i}", [128, 16, 256], BF16) for i in range(2)]
            Wu = [sb2(f"dWu{TB}{i}", [128, 16, 256], BF16) for i in range(2)]
            sg = [sb2(f"dsg{TB}{i}", [128, 512], F32) for i in range(3)]
            th = P.dma("sp", hn2[:], S["HN2T"][:, :, T0:T0 + NT], sem="dhn")
            WGR, WUR, SGR = Ring(Wg), Ring(Wu), Ring(sg)
            for fg in range(22):
                gi_, wg, gfree = WGR.next()
                tg = load_w_bf16(k, wg[:], I["w_gu"][:, fg * 256:(fg + 1) * 256], gfree, f"dwg{gi_}")
                ui_, wu, ufree = WUR.next()
                tu = load_w_bf16(k, wu[:], I["w_gu"][:, FF + fg * 256:FF + (fg + 1) * 256], ufree, f"dwu{ui_}")
                lastg = lastu = None
                for m in range(2):
                    ft = fg * 2 + m
                    for tb in range(2):
                        ts_ = slice(tb * 512, (tb + 1) * 512)
                        ig_, pg, fgr = k.psf.next()
                        for kc in range(16):
                            lastg = P.op("pe", (lambda e, pg=pg, wg=wg, kc=kc, m=m, ts_=ts_: e.matmul(
                                pg[:], lhsT=wg[:, kc, m * 128:(m + 1) * 128], rhs=hn2[:, kc, ts_], start=(kc == 0), stop=(kc == 15))),
                                deps=[tg, th] + fgr, sem=("pe" if kc == 15 else None))
                        iu_, pu, fur = k.psf.next()
                        for kc in range(16):
                            lastu = P.op("pe", (lambda e, pu=pu, wu=wu, kc=kc, m=m, ts_=ts_: e.matmul(
                                pu[:], lhsT=wu[:, kc, m * 128:(m + 1) * 128], rhs=hn2[:, kc, ts_], start=(kc == 0), stop=(kc == 15))),
                                deps=[tu] + fur, sem=("pe" if kc == 15 else None))
                        si, sgb, sfree = SGR.next()
                        a1 = P.op("act", (lambda e, pg=pg, sgb=sgb: e.activation(out=sgb[:], in_=pg[:], func=AF.Silu)), deps=[lastg] + sfree)
                        k.psf.release(ig_, [a1])
                        d1 = P.op("dve", (lambda e, pu=pu, sgb=sgb, ft=ft, ts_=ts_: e.tensor_tensor(out=actT[:, ft, ts_], in0=pu[:], in1=sgb[:], op=ALU.mult)),
                                  deps=[lastu, a1])
                        k.psf.release(iu_, [d1])
                        SGR.release(si, [d1])
                WGR.release(gi_, [lastg])
                WUR.release(ui_, [lastu])
        P.barrier()
        with ExitStack() as es2:
            sb2 = lambda n, s, d: es2.enter_context(nc.sbuf_tensor(n, s, d))
            Wd = [sb2(f"dWd{TB}{i}", [128, NFT, 512], BF16) for i in range(2)]
            xs = [sb2(f"dxs{TB}{i}", [128, 512], F32) for i in range(3)]
            os_ = [sb2(f"dos{TB}{i}", [128, 512], F32) for i in range(3)]
            WR, XR, OR_ = Ring(Wd), Ring(xs), Ring(os_)
            for nb in range(4):
                cs = slice(nb * 512, (nb + 1) * 512)
                wi, wd, wfree = WR.next()
                tws = []
                for q4 in range(4):
                    fsl = slice(q4 * 11, (q4 + 1) * 11)
                    tws.append(P.dma("pool", wd[:, fsl, :], I["w_dn"][q4 * 11 * 128:(q4 + 1) * 11 * 128, cs].rearrange("(ft p) n -> p ft n", p=128),
                                     deps=wfree, sem=f"dwd{wi}"))
                tw = tws[-1]
                last = None
                for tt in range(NT // 128):
                    rs = slice(T0 + tt * 128, T0 + (tt + 1) * 128)
                    xi, xb, xfree = XR.next()
                    tx = P.dma("sp", xb[:], S["X1"][rs, cs], deps=xfree, sem=f"dxs{xi}")
                    idx, ps, fr = k.psf.next()
                    for ft in range(NFT):
                        last = P.op("pe", (lambda e, ps=ps, wd=wd, ft=ft, tt=tt: e.matmul(ps[:], lhsT=actT[:, ft, tt * 128:(tt + 1) * 128], rhs=wd[:, ft, :],
                                                                                       start=(ft == 0), stop=(ft == NFT - 1))),
                                    deps=[tw] + fr, sem=("pe" if ft == NFT - 1 else None))
                    oi, ob, ofree = OR_.next()
                    ta = P.op("dve", (lambda e, ps=ps, xb=xb, ob=ob: e.tensor_tensor(out=ob[:], in0=ps[:], in1=xb[:], op=ALU.add)), deps=[last, tx] + ofree)
                    k.psf.release(idx, [ta])
                    XR.release(xi, [ta])
                    do = P.dma("sp", S["X2"][rs, cs], ob[:], deps=[ta], sem=f"dos{oi}")
                    OR_.release(oi, [do])
                WR.release(wi, [last])
        P.barrier()


def make_in_maps(inputs, names=None):
    c = host_consts()
    f32 = lambda a: np.ascontiguousarray(np.asarray(a, dtype=np.float32))
    x = np.asarray(inputs["x"], np.float32)
    w_in = np.asarray(inputs["w_in"], np.float32)[0]
    conv_w = np.asarray(inputs["conv_w"], np.float32)[0]
    conv_b = np.asarray(inputs["conv_b"], np.float32)[0]
    w3 = np.asarray(inputs["filt_w3"], np.float32)[0]
    hbias = np.asarray(inputs["hyena_bias"], np.float32)[0]
    gb = np.asarray(inputs["mlstm_gate_bias"], np.float32)[0]
    shared = {
        "w_mg": f32(w_in[:, 7184:11280]),
        "f_w1": f32(inputs["filt_w1"][0]), "f_b1": f32(inputs["filt_b1"][0][:, None]), "f_f1": f32(inputs["filt_freq1"][0][:, None]),
        "f_w2": f32(inputs["filt_w2"][0]), "f_b2": f32(inputs["filt_b2"][0][:, None]), "f_f2": f32(inputs["filt_freq2"][0][:, None]),
        "w_ba": f32(inputs["w_branch_a"][0]), "w_bb": f32(inputs["w_branch_b"][0]), "w_out": f32(inputs["w_out"][0]),
        "w_gu": f32(inputs["w_gate_up"][0]), "w_dn": f32(inputs["w_down"][0]),
        "n1_bc": f32(np.broadcast_to(np.asarray(inputs["norm1_w"], np.float32)[0], (128, D))),
        "n2_bc": f32(np.broadcast_to(np.asarray(inputs["norm2_w"], np.float32)[0], (128, D))),
        "nf_bc": f32(np.broadcast_to(np.asarray(inputs["norm_f_w"], np.float32), (128, D))),
        "ident": c["ident"], "identf": c["identf"], "FA64": c["FA64"], "MB": c["MB"], "IB": c["IB"], "Hpad": c["Hpad"],
        "zzT": c["zzT"], "negtau": c["negtau"], "maskf": c["maskf"], "maskb": c["maskb"], "resetmask": c["resetmask"],
    }
    per_half = []
    for h in range(2):
        hs = slice(h * 512, (h + 1) * 512)
        m = {}
        m["w_hy"] = f32(np.concatenate([w_in[:, 0:1024][:, hs], w_in[:, 1024:2048][:, hs], w_in[:, 2048:3072][:, hs]], 1))
        m["w_qk"] = f32(np.concatenate([w_in[:, 3072:4096][:, hs], w_in[:, 4096:5120][:, hs]], 1))
        m["w_vo"] = f32(np.concatenate([w_in[:, 5120:6144][:, hs], w_in[:, 6144:7168][:, hs]], 1))
        wgi = np.zeros((D, 64), np.float32)
        wgf = np.zeros((D, 64), np.float32)
        gbias = np.zeros((64, 2), np.float32)
        for hl in range(2):
            head = 2 * h + hl
            wgi[:, hl] = w_in[:, 7168 + 0 * 4 + head]
            wgf[:, hl] = w_in[:, 7168 + 1 * 4 + head]
            wgi[:, 32 + hl] = w_in[:, 7168 + 2 * 4 + head]
            wgf[:, 32 + hl] = w_in[:, 7168 + 3 * 4 + head]
            gbias[hl, 0] = gb[0, head]
            gbias[hl, 1] = gb[1, head]
            gbias[32 + hl, 0] = gb[2, head]
            gbias[32 + hl, 1] = gb[3, head]
        m["w_gi"], m["w_gf"], m["gbias"] = wgi, wgf, gbias
        cwb = np.zeros((128, 20, 4), np.float32)
        for tile in range(20):
            if tile < 12:
                col0 = (tile // 4) * 1024 + h * 512 + (tile % 4) * 128
            else:
                t2 = tile - 12
                col0 = 3072 + (t2 // 4) * 1024 + h * 512 + (t2 % 4) * 128
            cwb[:, tile, 0:3] = conv_w[:, col0:col0 + 128].T
            cwb[:, tile, 3] = conv_b[col0:col0 + 128]
        m["cwb"] = cwb
        m["f_w3"] = f32(np.concatenate([w3[:, o * 2048 + d_ * 1024 + h * 512: o * 2048 + d_ * 1024 + (h + 1) * 512]
                                        for o in range(2) for d_ in range(2)], 1))
        m["hb_bc"] = f32(np.broadcast_to(hbias[:, None, hs], (2, 128, 512)))
        m["absd_bc"] = f32(np.broadcast_to(c["absdelta"][hs], (64, 512)))
        per_half.append(m)
    maps = []
    for core in range(8):
        b, h = core // 2, core % 2
        m = dict(shared)
        m.update(per_half[h])
        m["x"] = f32(x[b])
        m["x_own"] = f32(x[b, h * OWN:(h + 1) * OWN])
        if names is not None:
            m = {kk: v for kk, v in m.items() if kk in names}
        maps.append(m)
    return maps


_NC_CACHE = {}


def kernel(**inputs):
    if "nc" not in _NC_CACHE:
        _NC_CACHE["nc"] = build()
    nc, names = _NC_CACHE["nc"]
    maps = make_in_maps(inputs, names)
    res = run_bass_kernel_spmd(nc, maps, core_ids=list(range(8)))
    out = np.zeros((4, L, D), np.float32)
    for core in range(8):
        b, h = core // 2, core % 2
        out[b, h * OWN:(h + 1) * OWN] = res.results[core]["out"]
    return out
```

```python
import numpy as np
import ml_dtypes
from contextlib import ExitStack
import concourse.bass as bass
import concourse.mybir as mybir
from concourse.bass_utils import run_bass_kernel_spmd

F32 = mybir.dt.float32
BF16 = mybir.dt.bfloat16
AF = mybir.ActivationFunctionType
ALU = mybir.AluOpType
AX = mybir.AxisListType

D = 2048
L = 4096
OWN = 2048
CH = 512
FF = 5632
NFT = FF // 128
EPS = 1e-6
NFFT = 8192


class Prog:
    def __init__(self, nc):
        self.nc = nc
        self.q = {k: [] for k in ("pe", "act", "dve", "pool", "sp")}
        self.sems = {}
        self.cnt = {}
        self.waited = {k: {} for k in self.q}
        self._ctx = []
        self.alias = {}
        self.next_phys = 0
        self.next_phys_s = 0

    ENG = ("pe", "act", "dve", "pool")

    def sem(self, name, eng=None):
        if name not in self.ENG and not name.startswith("g#") and not name.startswith("s#"):
            if name not in self.alias:
                if eng == "pool":
                    self.alias[name] = "s#%d" % self.next_phys_s
                    self.next_phys_s += 1
                else:
                    self.alias[name] = "g#%d" % self.next_phys
                    self.next_phys += 1
            name = self.alias[name]
        if name not in self.sems:
            cm = self.nc.semaphore(name)
            s = cm.__enter__()
            self._ctx.append(cm)
            self.sems[name] = s
            self.cnt[name] = 0
        return self.sems[name]

    def op(self, eng, fn, deps=(), sem="auto", inc=1):
        if sem == "auto":
            sem = eng
        waits = []
        for d in deps:
            if d is None:
                continue
            if isinstance(d, list):
                dl = d
            else:
                dl = [d]
            for dd in dl:
                if dd is None:
                    continue
                sn, v = dd
                if self.waited[eng].get(sn, 0) >= v:
                    continue
                self.waited[eng][sn] = v
                waits.append((self.sems[sn], v))
        tok = None
        s = None
        if sem is not None:
            s = self.sem(sem, eng)
            if sem not in self.ENG:
                sem = self.alias.get(sem, sem)
            self.cnt[sem] += inc
            tok = (sem, self.cnt[sem])
        self.q[eng].append((waits, fn, s, inc))
        return tok

    def dma(self, eng, out, in_, deps=(), sem=None):
        return self.op(eng, lambda e: e.dma_start(out=out, in_=in_), deps=deps, sem=sem, inc=16)

    def barrier(self, exclude=()):
        ex = {self.alias.get(n, n) for n in exclude}
        toks = [(n, c) for n, c in self.cnt.items() if c > 0 and n not in ex]
        if ex:
            for eng in self.q:
                self.op(eng, None, deps=toks, sem=None)
            return
        for eng in self.q:
            self.op(eng, None, deps=toks, sem=None)
        self.alias = {}
        self.next_phys = 0
        self.next_phys_s = 0

    def run(self):
        nc = self.nc
        with nc.Block() as block:
            def mk(name):
                def body(e):
                    for waits, fn, s, inc in self.q[name]:
                        for (ws, v) in waits:
                            e.wait_ge(ws, v)
                        if fn is not None:
                            ins = fn(e)
                            if s is not None:
                                ins.then_inc(s, inc)
                return body
            block.tensor(mk("pe"))
            block.scalar(mk("act"))
            block.vector(mk("dve"))
            block.gpsimd(mk("pool"))
            block.sync(mk("sp"))
        for cm in reversed(self._ctx):
            cm.__exit__(None, None, None)


class Ring:
    def __init__(self, bufs):
        self.bufs = list(bufs)
        self.free = [[] for _ in self.bufs]
        self.i = 0

    def next(self):
        idx = self.i % len(self.bufs)
        self.i += 1
        return idx, self.bufs[idx], self.free[idx]

    def release(self, idx, toks):
        self.free[idx] = [t for t in toks if t is not None]


def T(x):
    return x if isinstance(x, list) else [x]


def bf(x):
    return np.ascontiguousarray(x.astype(np.float32)).astype(ml_dtypes.bfloat16)


_CONSTS = None


def host_consts():
    global _CONSTS
    if _CONSTS is not None:
        return _CONSTS
    c = {}
    c["ident"] = bf(np.eye(128))
    c["identf"] = np.eye(128, dtype=np.float32)
    a = np.arange(64)
    p = np.arange(33)
    ang = 2 * np.pi * np.outer(a, p) / 64.0
    c["FA64"] = bf(np.concatenate([np.cos(ang), -np.sin(ang)], 1))
    b = np.arange(128, dtype=np.float64)
    q = np.arange(128, dtype=np.float64)
    th = 2 * np.pi * b[:, None, None] * (p[None, :, None] + 64 * q[None, None, :]) / float(NFFT)
    c["MB"] = bf(np.stack([np.cos(th), np.sin(th), -np.sin(th)], 2))
    thT = th.transpose(2, 1, 0)
    c["IB"] = bf(np.stack([np.cos(thT), np.sin(thT), -np.sin(thT)], 2))
    wp = np.full(33, 2.0)
    wp[0] = 1.0
    wp[32] = 1.0
    a32 = np.arange(32)
    ang2 = 2 * np.pi * np.outer(p, a32) / 64.0
    Hm = np.concatenate([wp[:, None] * np.cos(ang2), -wp[:, None] * np.sin(ang2)], 0) / float(NFFT)
    Hpad = np.zeros((66, 4, 128))
    for bi in range(4):
        Hpad[:, bi, 32 * bi:32 * bi + 32] = Hm
    c["Hpad"] = bf(Hpad)
    t = np.linspace(0.0, 1.0, L, dtype=np.float32)
    bands = 16
    omega = (2.0 * np.pi * np.arange(L, dtype=np.float32) / L).astype(np.float32)
    freqs = np.linspace(1e-4, bands - 1, bands, dtype=np.float32)
    angz = (omega[:, None] * freqs[None, :]).astype(np.float32)
    z = np.concatenate([t[:, None], np.cos(angz), -np.sin(angz)], -1).astype(np.float32)
    zz = np.zeros((NFFT, 33), np.float32)
    zz[:L] = z
    zz[L + 1:] = z[1:][::-1]
    c["zzT"] = np.ascontiguousarray(zz.T)
    tau = np.zeros(NFFT, np.float32)
    tau[:L] = t
    tau[L + 1:] = t[1:][::-1]
    tau[L] = 1e4
    c["negtau"] = np.ascontiguousarray(-tau.reshape(64, 128))
    max_decay = np.log(1e-2) / 0.3
    min_decay = np.log(1e-2) / 1.5
    deltas = np.abs(np.linspace(min_decay, max_decay, 1024, dtype=np.float32))
    c["absdelta"] = deltas
    s_ = np.arange(128)[:, None]
    j_ = np.arange(128)[None, :]
    c["maskf"] = (s_ <= j_).astype(np.float32)
    c["maskb"] = (s_ >= j_).astype(np.float32)
    rm = np.ones((64, L), np.float32)
    rm[0:32, ::128] = 0.0
    rm[32:64, 127::128] = 0.0
    c["resetmask"] = rm
    _CONSTS = c
    return c


class K:
    pass


def build(phases=None, dbg_in=(), dbg_out=()):
    ALL = ["F", "A", "B1", "B2", "H", "M", "X", "C", "D"]
    if phases is None:
        phases = set(ALL)
    nc = bass.Bass("TRN2", target_bir_lowering=False)
    P = Prog(nc)
    k = K()
    k.nc, k.P = nc, P
    k.cache = {}

    def din(name, shape, dt=F32):
        return nc.dram_tensor(name, list(shape), dt, kind="ExternalInput").ap()

    def dscr(name, shape, dt):
        if name in dbg_in:
            return nc.dram_tensor(name, list(shape), dt, kind="ExternalInput").ap()
        if name in dbg_out:
            return nc.dram_tensor(name, list(shape), dt, kind="ExternalOutput").ap()
        return nc.dram_tensor(name, list(shape), dt).ap()

    SHAPES = {
        "x": ([L, D], F32),
        "x_own": ([OWN, D], F32),
        "w_hy": ([D, 1536], F32),
        "w_qk": ([D, 1024], F32),
        "w_vo": ([D, 1024], F32),
        "w_gi": ([D, 64], F32),
        "w_gf": ([D, 64], F32),
        "w_mg": ([D, 4096], F32),
        "cwb": ([128, 20, 4], F32),
        "f_w1": ([33, 64], F32),
        "f_b1": ([64, 1], F32),
        "f_f1": ([64, 1], F32),
        "f_w2": ([64, 64], F32),
        "f_b2": ([64, 1], F32),
        "f_f2": ([64, 1], F32),
        "f_w3": ([64, 2048], F32),
        "hb_bc": ([2, 128, CH], F32),
        "gbias": ([64, 2], F32),
        "w_ba": ([1024, D], F32),
        "w_bb": ([1024, D], F32),
        "w_out": ([D, D], F32),
        "w_gu": ([D, 2 * FF], F32),
        "w_dn": ([FF, D], F32),
        "n1_bc": ([128, D], F32),
        "n2_bc": ([128, D], F32),
        "nf_bc": ([128, D], F32),
        "ident": ([128, 128], BF16),
        "identf": ([128, 128], F32),
        "FA64": ([64, 66], BF16),
        "MB": ([128, 33, 3, 128], BF16),
        "IB": ([128, 33, 3, 128], BF16),
        "Hpad": ([66, 4, 128], BF16),
        "zzT": ([33, NFFT], F32),
        "negtau": ([64, 128], F32),
        "absd_bc": ([64, CH], F32),
        "maskf": ([128, 128], F32),
        "maskb": ([128, 128], F32),
        "resetmask": ([64, L], F32),
    }

    class LazyIn(dict):
        def __missing__(self, name):
            shape, dt = SHAPES[name]
            ap = nc.dram_tensor(name, list(shape), dt, kind="ExternalInput").ap()
            self[name] = ap
            return ap
    I = LazyIn()
    k.I = I
    out = nc.dram_tensor("out", [OWN, D], F32, kind="ExternalOutput").ap()
    k.out = out

    S = {}
    S["HV"] = dscr("HV", [L, CH], BF16)
    S["HX1"] = dscr("HX1", [L, CH], BF16)
    S["HX2"] = dscr("HX2", [L, CH], BF16)
    S["Z"] = dscr("Z", [L, CH], BF16)
    S["T2"] = dscr("T2", [2, 66, 128, CH], BF16)
    S["T3"] = dscr("T3", [66, 128, CH], BF16)
    S["KSPEC"] = dscr("KSPEC", [2, 128, 33, 2, CH], BF16)
    S["QT"] = dscr("QT", [4, 128, L], BF16)
    S["KT"] = dscr("KT", [4, 128, L], BF16)
    S["KTM"] = dscr("KTM", [L, CH], BF16)
    S["VTM"] = dscr("VTM", [L, CH], BF16)
    S["OTM"] = dscr("OTM", [L, CH], BF16)
    S["IG"] = dscr("IG", [64, L], F32)
    S["FG"] = dscr("FG", [64, L], F32)
    S["GAT"] = dscr("GAT", [16, 128, OWN], BF16)
    S["GBT"] = dscr("GBT", [16, 128, OWN], BF16)
    if "CCIN" in dbg_in:
        S["CCIN"] = nc.dram_tensor("CCIN_int", [L, 1024], BF16).ap()
        k.ccin_ext = nc.dram_tensor("CCIN", [L, 1024], BF16, kind="ExternalInput").ap()
    else:
        S["CCIN"] = dscr("CCIN", [L, 1024], BF16)
    S["CCOUT"] = dscr("CCOUT", [2 * L, 1024], BF16)
    S["YOWN"] = dscr("YOWN", [2, 2048, 1024], BF16)
    S["X1"] = dscr("X1", [OWN, D], F32)
    S["HN2T"] = dscr("HN2T", [128, 16, OWN], BF16)
    S["X2"] = dscr("X2", [OWN, D], F32)
    S["RHO"] = dscr("RHO", [64, 32], F32)
    k.S = S

    with ExitStack() as es:
        def set_psum(pes, tag, n32, n16):
            psf = [pes.enter_context(nc.psum_tensor(f"psf{tag}{i}", [128, 512], F32)) for i in range(n32)]
            psb = [pes.enter_context(nc.psum_tensor(f"psb{tag}{i}", [128, 1024], BF16)) for i in range(n16)]
            k.psf = Ring(psf)
            k.psb = Ring(psb)
            k.ps8 = lambda: k.psf
        ident = es.enter_context(nc.sbuf_tensor("sb_ident", [128, 128], BF16))
        identf = es.enter_context(nc.sbuf_tensor("sb_identf", [128, 128], F32))
        epst = es.enter_context(nc.sbuf_tensor("epst", [128, 1], F32))
        k.ident, k.identf, k.epst = ident, identf, epst
        t0 = P.dma("sp", ident[:], I["ident"], sem="c0")
        t1 = P.dma("sp", identf[:], I["identf"], sem="c0")
        t2 = P.op("dve", lambda e: e.memset(epst[:], EPS))
        k.c_tok = [t0, t1, t2]
        P.barrier()

        def phase_XB2(kk):
            if "X" in phases:
                phase_X(kk)
            if "B2" in phases:
                phase_B2(kk)
        plan = [("F", phase_F, 8, 0), ("A", phase_AB1, 6, 2), ("H", phase_H, 8, 0), ("M", phase_M, 8, 0),
                ("XB2", phase_XB2, 6, 2), ("C", phase_C, 6, 2), ("D", phase_D, 8, 0)]
        for name, fn, n32, n16 in plan:
            if name in phases or (name == "A" and "B1" in phases) or (name == "XB2" and ("X" in phases or "B2" in phases)):
                with ExitStack() as pes:
                    set_psum(pes, name, n32, n16)
                    fn(k)
                    P.barrier()
        P.run()
    return nc, set(I.keys())


def load_w_bf16(k, dst, src_cols, deps, sem):
    return k.P.dma("pool", dst, src_cols.rearrange("(kc p) n -> p kc n", p=128), deps=deps, sem=sem)


def rmsnorm_tile(k, x_sb, x_tok, nw_bc, xn_bf, junk, ss, rms, rstd, out_free):
    P = k.P
    t1 = P.op("act", lambda e: e.activation(out=junk[:], in_=x_sb[:], func=AF.Square, accum_out=ss[:]),
              deps=[x_tok, out_free])
    t2 = P.op("act", lambda e: e.activation(out=rms[:], in_=ss[:], func=AF.Sqrt, scale=1.0 / D, bias=k.epst[:]),
              deps=[t1])
    t3 = P.op("dve", lambda e: e.reciprocal(out=rstd[:], in_=rms[:]), deps=[t2])
    t4 = P.op("dve", lambda e: e.scalar_tensor_tensor(out=xn_bf[:], in0=x_sb[:], scalar=rstd[:, 0:1], in1=nw_bc[:],
                                                       op0=ALU.mult, op1=ALU.mult), deps=[t3, out_free])
    return t4


def transpose_to_T(k, xn_bf, xn_tok, dstT, tok0, dst_free=None):
    P = k.P
    toks = []
    last_pe = None
    for half in range(2):
        idx, pb, fr = k.psb.next()
        for j in range(8):
            kc = half * 8 + j
            last_pe = P.op("pe", (lambda e, pb=pb, j=j, kc=kc: e.transpose(out=pb[:, j * 128:(j + 1) * 128],
                                                                             in_=xn_bf[:, kc * 128:(kc + 1) * 128],
                                                                             identity=k.ident[:])),
                           deps=[xn_tok] + fr + k.c_tok)
        eng = "act" if half == 0 else "dve"
        if eng == "act":
            t = P.op("act", (lambda e, pb=pb, half=half: e.activation(
                out=dstT[:, half * 8:(half + 1) * 8, tok0:tok0 + 128],
                in_=pb[:, :].rearrange("p (k t) -> p k t", t=128), func=AF.Copy)), deps=[last_pe, dst_free])
        else:
            t = P.op("dve", (lambda e, pb=pb, half=half: e.tensor_copy(
                out=dstT[:, half * 8:(half + 1) * 8, tok0:tok0 + 128],
                in_=pb[:, :].rearrange("p (k t) -> p k t", t=128))), deps=[last_pe, dst_free])
        k.psb.release(idx, [t])
        toks.append(t)
    return toks, last_pe


def phase_F(k):
    nc, P, I, S = k.nc, k.P, k.I, k.S
    with ExitStack() as es:
        sb = lambda n, s, d: es.enter_context(nc.sbuf_tensor(n, s, d))
        zzT = sb("sb_zzT", [33, NFFT], F32)
        w1 = sb("fw1", [33, 64], F32)
        w2 = sb("fw2", [64, 64], F32)
        prm = sb("fprm", [64, 4], F32)
        sc = sb("fsc", [64, 4], F32)
        w3 = sb("fw3", [64, 2048], BF16)
        h1 = sb("fh1", [64, 512], F32)
        tmp = sb("ftmp", [64, 512], F32)
        tmp2 = sb("ftmp2", [64, 512], F32)
        h2T = sb("fh2T", [64, NFFT], BF16)
        negtau = sb("fnegtau", [64, 128], F32)
        absd = sb("fabsd", [64, CH], F32)
        FA = sb("fFA", [64, 66], BF16)
        win = [sb(f"fwin{i}", [64, CH], F32) for i in range(2)]
        kern = [sb(f"fkern{i}", [64, CH], BF16) for i in range(4)]
        S2 = [sb(f"fS2{i}", [66, 2, 16, CH], BF16) for i in range(2)]
        ld = []
        ld.append(P.dma("sp", zzT[:], I["zzT"], sem="fl"))
        ld.append(P.dma("sp", w1[:], I["f_w1"], sem="fl"))
        ld.append(P.dma("sp", w2[:], I["f_w2"], sem="fl"))
        ld.append(P.dma("sp", prm[:, 0:1], I["f_b1"], sem="fl"))
        ld.append(P.dma("sp", prm[:, 1:2], I["f_f1"], sem="fl"))
        ld.append(P.dma("sp", prm[:, 2:3], I["f_b2"], sem="fl"))
        ld.append(P.dma("sp", prm[:, 3:4], I["f_f2"], sem="fl"))
        ld.append(P.dma("sp", negtau[:], I["negtau"], sem="fl"))
        ld.append(P.dma("sp", absd[:], I["absd_bc"], sem="fl"))
        ld.append(P.dma("sp", FA[:], I["FA64"], sem="fl"))
        ldw3 = P.dma("pool", w3[:], I["f_w3"], sem="fl3")
        ldall = [ld[-1]]
        tsc = None
        for li in range(2):
            t_ = P.op("dve", (lambda e, li=li: e.tensor_scalar(out=sc[:, 2 * li:2 * li + 1], in0=prm[:, 2 * li + 1:2 * li + 2],
                                                               scalar1=1.0 / 3.0, scalar2=None, op0=ALU.mult)), deps=ldall)
            tsc = P.op("dve", (lambda e, li=li: e.tensor_tensor(out=sc[:, 2 * li + 1:2 * li + 2], in0=sc[:, 2 * li:2 * li + 1],
                                                                in1=prm[:, 2 * li:2 * li + 1], op=ALU.mult)), deps=[t_])
        prev = tsc
        for blk in range(16):
            cs = slice(blk * 512, (blk + 1) * 512)
            idx, ps, fr = k.psf.next()
            t = P.op("pe", (lambda e, ps=ps, cs=cs: e.matmul(ps[0:64, :], lhsT=w1[:], rhs=zzT[:, cs], start=True, stop=True)),
                     deps=ldall + fr)
            t = P.op("act", (lambda e, ps=ps: e.activation(out=tmp[:], in_=ps[0:64, :], func=AF.Sin, scale=sc[:, 0:1], bias=sc[:, 1:2])),
                     deps=[t, prev])
            k.psf.release(idx, [t])
            t = P.op("dve", lambda e: e.tensor_tensor(out=tmp2[:], in0=tmp[:], in1=tmp[:], op=ALU.mult), deps=[t])
            t = P.op("dve", lambda e: e.tensor_scalar(out=tmp2[:], in0=tmp2[:], scalar1=-4.0, scalar2=3.0, op0=ALU.mult, op1=ALU.add), deps=[t])
            t = P.op("dve", lambda e: e.tensor_tensor(out=h1[:], in0=tmp[:], in1=tmp2[:], op=ALU.mult), deps=[t])
            idx, ps, fr = k.psf.next()
            t = P.op("pe", (lambda e, ps=ps: e.matmul(ps[0:64, :], lhsT=w2[:], rhs=h1[:], start=True, stop=True)), deps=[t] + fr)
            t = P.op("act", (lambda e, ps=ps: e.activation(out=tmp[:], in_=ps[0:64, :], func=AF.Sin, scale=sc[:, 2:3], bias=sc[:, 3:4])),
                     deps=[t])
            k.psf.release(idx, [t])
            t = P.op("dve", lambda e: e.tensor_tensor(out=tmp2[:], in0=tmp[:], in1=tmp[:], op=ALU.mult), deps=[t])
            t = P.op("dve", lambda e: e.tensor_scalar(out=tmp2[:], in0=tmp2[:], scalar1=-4.0, scalar2=3.0, op0=ALU.mult, op1=ALU.add), deps=[t])
            prev = P.op("dve", (lambda e, cs=cs: e.tensor_tensor(out=h2T[:, cs], in0=tmp[:], in1=tmp2[:], op=ALU.mult)), deps=[t])
        h2_ready = prev
        kern6 = kern + [sb(f"fkern{i}", [64, CH], BF16) for i in range(4, 8)]
        winR = Ring(win)
        kernR = Ring(kern6)
        S2R = Ring(S2)
        R8 = k.ps8()
        h2v = h2T[:, :]
        items = [(bp, o) for bp in range(128) for o in range(2)]
        st1 = {}
        wstate = {}
        s2state = {}
        SK = 3

        def stage1(i):
            bp, o = items[i]
            if o == 0:
                wi, wt, wfree = winR.next()
                tw = P.op("act", (lambda e, wt=wt, bp=bp: e.activation(out=wt[:], in_=absd[:], func=AF.Exp, scale=negtau[:, bp:bp + 1])),
                          deps=ldall + wfree)
                wstate[bp] = (wi, wt, tw, [])
            wi, wt, tw, wusers = wstate[bp]
            idx, ps, fr = R8.next()
            lo = bass.AP(h2v.tensor, h2v.offset + bp, [[h2v.ap[0][0], 64], [128, 32]])
            hi = bass.AP(h2v.tensor, h2v.offset + L + bp, [[h2v.ap[0][0], 64], [128, 32]])
            P.op("pe", (lambda e, ps=ps, lo=lo, o=o: e.matmul(ps[0:32, :], lhsT=lo, rhs=w3[:, (2 * o) * CH:(2 * o + 1) * CH],
                                                           start=True, stop=True)), deps=[h2_ready, ldw3] + fr, sem=None)
            t = P.op("pe", (lambda e, ps=ps, hi=hi, o=o: e.matmul(ps[32:64, :], lhsT=hi, rhs=w3[:, (2 * o + 1) * CH:(2 * o + 2) * CH],
                                                               start=True, stop=True)))
            ki, kt, kfree = kernR.next()
            tk = P.op("dve", (lambda e, ps=ps, kt=kt, wt=wt: e.tensor_tensor(out=kt[:], in0=ps[0:64, :], in1=wt[:], op=ALU.mult)),
                      deps=[t, tw] + kfree)
            R8.release(idx, [tk])
            wusers.append(tk)
            if o == 1:
                winR.release(wi, wusers)
            st1[i] = (ki, kt, tk)

        def stage2(i):
            bp, o = items[i]
            ki, kt, tk = st1.pop(i)
            if bp % 16 == 0 and o == 0:
                s2i, s2b, s2f = S2R.next()
                s2state["cur"] = (s2i, s2b, s2f, [])
            s2i, s2b, s2f, evs = s2state["cur"]
            idx2, ps2, fr2 = R8.next()
            ta = P.op("pe", (lambda e, ps2=ps2, kt=kt: e.matmul(ps2[0:66, :], lhsT=FA[:], rhs=kt[:], start=True, stop=True)),
                      deps=[tk] + fr2)
            kernR.release(ki, [ta])
            if o == 0:
                te = P.op("act", (lambda e, ps2=ps2, s2b=s2b, o=o, bp=bp: e.activation(out=s2b[:, o, bp % 16, :], in_=ps2[0:66, :], func=AF.Copy)),
                          deps=[ta] + s2f)
            else:
                te = P.op("dve", (lambda e, ps2=ps2, s2b=s2b, o=o, bp=bp: e.tensor_copy(out=s2b[:, o, bp % 16, :], in_=ps2[0:66, :])),
                          deps=[ta] + s2f)
            R8.release(idx2, [te])
            evs.append(te)
            if bp % 16 == 15 and o == 1:
                b0 = bp - 15
                dts = []
                for oo in range(2):
                    dts.append(P.dma("sp", S["T2"][oo, :, b0:b0 + 16, :], s2b[:, oo, :, :], deps=evs[-2:], sem=f"fs2_{s2i}"))
                S2R.release(s2i, [dts[-1]])

        for i in range(len(items) + SK):
            if i < len(items):
                stage1(i)
            if i >= SK:
                stage2(i - SK)
    P.barrier()
    with ExitStack() as es:
        sb = lambda n, s, d: es.enter_context(nc.sbuf_tensor(n, s, d))
        MB = sb("fMB", [128, 33, 3, 128], BF16)
        hb = sb("fhb", [128, 2, CH], F32)
        A2 = [sb(f"fA{i}", [128, 66, CH], BF16) for i in range(2)]
        KS = [sb(f"fKS{i}", [128, 2, CH], BF16) for i in range(4)]
        tmb = P.dma("sp", MB[:], I["MB"], sem="fl")
        thb = P.dma("sp", hb[:], I["hb_bc"].rearrange("o p c -> p o c"), sem="fl")
        KSR = Ring(KS)
        R8 = k.ps8()
        a_rdy = []
        for o in range(2):
            lds = []
            for g in range(6):
                lds.append(P.dma("pool" if o == 1 else "sp", A2[o][:, g * 11:(g + 1) * 11, :],
                                 S["T2"][o, g * 11:(g + 1) * 11, :, :].rearrange("r b c -> b r c"), sem=f"fA{o}"))
            a_rdy.append([lds[-1]])
        for o in range(2):
            A_sb = A2[o]
            a_ready = a_rdy[o]
            for p in range(33):
                ksi, ks, ksfree = KSR.next()
                i1, pre, f1 = R8.next()
                P.op("pe", (lambda e, pre=pre, p=p, A_sb=A_sb: e.matmul(pre[:], lhsT=MB[:, p, 0, :], rhs=A_sb[:, p, :], start=True, stop=False)),
                     deps=a_ready + [tmb] + f1, sem=None)
                t_re = P.op("pe", (lambda e, pre=pre, p=p, A_sb=A_sb: e.matmul(pre[:], lhsT=MB[:, p, 1, :], rhs=A_sb[:, 33 + p, :], start=False, stop=True)))
                i2, pim, f2 = R8.next()
                P.op("pe", (lambda e, pim=pim, p=p, A_sb=A_sb: e.matmul(pim[:], lhsT=MB[:, p, 0, :], rhs=A_sb[:, 33 + p, :], start=True, stop=False)),
                     deps=f2, sem=None)
                t_im = P.op("pe", (lambda e, pim=pim, p=p, A_sb=A_sb: e.matmul(pim[:], lhsT=MB[:, p, 2, :], rhs=A_sb[:, p, :], start=False, stop=True)))
                e1 = P.op("dve", (lambda e, pre=pre, ks=ks, o=o: e.tensor_tensor(out=ks[:, 0, :], in0=pre[:], in1=hb[:, o, :], op=ALU.add)),
                          deps=[t_re, thb] + ksfree)
                R8.release(i1, [e1])
                e2 = P.op("act", (lambda e, pim=pim, ks=ks: e.activation(out=ks[:, 1, :], in_=pim[:], func=AF.Copy)), deps=[t_im] + ksfree)
                R8.release(i2, [e2])
                d = P.dma("sp", S["KSPEC"][o, :, p, :, :], ks[:], deps=[e1, e2], sem=f"fks{ksi}")
                KSR.release(ksi, [d])


def phase_H(k):
    nc, P, I, S = k.nc, k.P, k.I, k.S
    with ExitStack() as es:
        sb = lambda n, s, d: es.enter_context(nc.sbuf_tensor(n, s, d))
        FA = sb("hFA", [64, 66], BF16)
        Hpad = sb("hHpad", [66, 4, 128], BF16)
        MB = sb("hMB", [128, 33, 3, 128], BF16)
        IB = sb("hIB", [128, 33, 3, 128], BF16)
        tl = [P.dma("sp", FA[:], I["FA64"], sem="hl"), P.dma("sp", Hpad[:], I["Hpad"], sem="hl"),
              P.dma("pool", MB[:], I["MB"], sem="hlp"), P.dma("pool", IB[:], I["IB"], sem="hlp")]
        tl = [tl[1], tl[3]]
        k.h_tabs = (MB, IB)
        for o, (src, gate, dst, dst_cols) in enumerate([(S["HV"], S["HX1"], S["Z"], CH), (S["Z"], S["HX2"], S["CCIN"], 1024)]):
            _h_conv(k, o, src, gate, dst, FA, Hpad, tl)


def _h_conv(k, o, src, gate, dst, FA, Hpad, tl):
    nc, P, I, S = k.nc, k.P, k.I, k.S
    if True:
        if True:
            with ExitStack() as es2:
                sb2 = lambda n, s, d: es2.enter_context(nc.sbuf_tensor(n, s, d))
                Xc = [sb2(f"hXc{o}_{i}", [32, 16, CH], BF16) for i in range(3)]
                S2 = [sb2(f"hS2{o}_{i}", [66, 16, CH], BF16) for i in range(2)]
                XR, SR = Ring(Xc), Ring(S2)
                srcv = src.rearrange("(a b) c -> a b c", b=128)
                xinfo = {}

                def xload(bcx):
                    xi_x, xbx, xfreex = XR.next()
                    txx = P.dma("sp", xbx[:], srcv[:, bcx * 16:(bcx + 1) * 16, :], deps=xfreex, sem=f"hx{xi_x}")
                    xinfo[bcx] = (xi_x, xbx, txx)
                xload(0)
                for bc in range(8):
                    if bc + 1 < 8:
                        xload(bc + 1)
                    xi, xb, tx = xinfo.pop(bc)
                    si, sbuf, sfree = SR.next()
                    evs = []
                    mms = []
                    for j in range(16):
                        idx, ps, fr = k.psf.next()
                        t = P.op("pe", (lambda e, ps=ps, xb=xb, j=j: e.matmul(ps[0:66, :], lhsT=FA[0:32, :], rhs=xb[:, j, :], start=True, stop=True)),
                                 deps=[tx] + tl + fr)
                        mms.append(t)
                        eng = "act" if j % 2 == 0 else "dve"
                        if eng == "act":
                            te = P.op("act", (lambda e, ps=ps, sbuf=sbuf, j=j: e.activation(out=sbuf[:, j, :], in_=ps[0:66, :], func=AF.Copy)),
                                      deps=[t] + sfree)
                        else:
                            te = P.op("dve", (lambda e, ps=ps, sbuf=sbuf, j=j: e.tensor_copy(out=sbuf[:, j, :], in_=ps[0:66, :])),
                                      deps=[t] + sfree)
                        k.psf.release(idx, [te])
                        evs.append(te)
                    XR.release(xi, [mms[-1]])
                    d = P.dma("sp", S["T2"][0, :, bc * 16:(bc + 1) * 16, :], sbuf[:], deps=evs[-2:], sem=f"hs{si}")
                    SR.release(si, [d])
            P.barrier()
            with ExitStack() as es2:
                sb2 = lambda n, s, d: es2.enter_context(nc.sbuf_tensor(n, s, d))
                MB, IB = k.h_tabs
                A_sb = sb2(f"hA{o}", [128, 66, CH], BF16)
                KK = [sb2(f"hK{o}_{i}", [128, 2, CH], BF16) for i in range(6)]
                Yb = [sb2(f"hY{o}_{i}", [128, 2, CH], BF16) for i in range(5)]
                tt_ = [sb2(f"hT{o}_{i}", [128, 4, CH], F32) for i in range(3)]
                Bst = [sb2(f"hB{o}_{i}", [128, 2, CH], BF16) for i in range(3)]
                tm = tl
                lds = []
                for g in range(6):
                    lds.append(P.dma("sp" if g % 2 == 0 else "pool", A_sb[:, g * 11:(g + 1) * 11, :],
                                     S["T2"][0, g * 11:(g + 1) * 11, :, :].rearrange("r b c -> b r c"), sem=f"hA{g % 2}"))
                a_ready = lds[-2:]
                KR, YR, TR, BR = Ring(KK), Ring(Yb), Ring(tt_), Ring(Bst)
                R8 = k.ps8()
                T3v = S["T3"].rearrange("(ri p) b c -> b ri p c", ri=2)
                yst = {}
                SK = 2

                kinfo = {}
                PF = 3

                def kload(p):
                    ki, kk, kfree = KR.next()
                    tk = P.dma("sp", kk[:], S["KSPEC"][o, :, p, :, :], deps=kfree, sem=f"hk{ki}")
                    kinfo[p] = (ki, kk, tk)

                def hb_stage1(p):
                    if p + PF < 33:
                        kload(p + PF)
                    ki, kk, tk = kinfo.pop(p)
                    i1, pre, f1 = R8.next()
                    P.op("pe", (lambda e, pre=pre, p=p: e.matmul(pre[:], lhsT=MB[:, p, 0, :], rhs=A_sb[:, p, :], start=True, stop=False)),
                         deps=a_ready + tm + f1, sem=None)
                    t_re = P.op("pe", (lambda e, pre=pre, p=p: e.matmul(pre[:], lhsT=MB[:, p, 1, :], rhs=A_sb[:, 33 + p, :], start=False, stop=True)))
                    i2, pim, f2 = R8.next()
                    P.op("pe", (lambda e, pim=pim, p=p: e.matmul(pim[:], lhsT=MB[:, p, 0, :], rhs=A_sb[:, 33 + p, :], start=True, stop=False)),
                         deps=f2, sem=None)
                    t_im = P.op("pe", (lambda e, pim=pim, p=p: e.matmul(pim[:], lhsT=MB[:, p, 2, :], rhs=A_sb[:, p, :], start=False, stop=True)))
                    ti, tb_, tfree = TR.next()
                    yi, yb, yfree = YR.next()
                    m1 = P.op("dve", (lambda e, pre=pre, kk=kk, tb_=tb_: e.tensor_tensor(out=tb_[:, 0, :], in0=pre[:], in1=kk[:, 0, :], op=ALU.mult)),
                              deps=[t_re, tk] + tfree)
                    m2 = P.op("dve", (lambda e, pim=pim, kk=kk, tb_=tb_: e.tensor_tensor(out=tb_[:, 1, :], in0=pim[:], in1=kk[:, 1, :], op=ALU.mult)),
                              deps=[t_im])
                    m3 = P.op("dve", (lambda e, pre=pre, kk=kk, tb_=tb_: e.tensor_tensor(out=tb_[:, 2, :], in0=pre[:], in1=kk[:, 1, :], op=ALU.mult)))
                    m4 = P.op("dve", (lambda e, pim=pim, kk=kk, tb_=tb_: e.tensor_tensor(out=tb_[:, 3, :], in0=pim[:], in1=kk[:, 0, :], op=ALU.mult)))
                    R8.release(i1, [m3])
                    R8.release(i2, [m4])
                    KR.release(ki, [m4])
                    y1 = P.op("dve", (lambda e, tb_=tb_, yb=yb: e.tensor_tensor(out=yb[:, 0, :], in0=tb_[:, 0, :], in1=tb_[:, 1, :], op=ALU.subtract)),
                              deps=[m2] + yfree)
                    y2 = P.op("pool", (lambda e, tb_=tb_, yb=yb: e.tensor_tensor(out=yb[:, 1, :], in0=tb_[:, 2, :], in1=tb_[:, 3, :], op=ALU.add)),
                              deps=[m4] + yfree)
                    TR.release(ti, [y1, y2])
                    yst[p] = (yi, yb, [y1, y2])

                def hb_stage2(p):
                    yi, yb, y2 = yst.pop(p)
                    j1, qre, g1 = R8.next()
                    P.op("pe", (lambda e, qre=qre, p=p, yb=yb: e.matmul(qre[:], lhsT=IB[:, p, 0, :], rhs=yb[:, 0, :], start=True, stop=False)),
                         deps=y2 + g1, sem=None)
                    u_re = P.op("pe", (lambda e, qre=qre, p=p, yb=yb: e.matmul(qre[:], lhsT=IB[:, p, 2, :], rhs=yb[:, 1, :], start=False, stop=True)))
                    j2, qim, g2 = R8.next()
                    P.op("pe", (lambda e, qim=qim, p=p, yb=yb: e.matmul(qim[:], lhsT=IB[:, p, 1, :], rhs=yb[:, 0, :], start=True, stop=False)),
                         deps=g2, sem=None)
                    u_im = P.op("pe", (lambda e, qim=qim, p=p, yb=yb: e.matmul(qim[:], lhsT=IB[:, p, 0, :], rhs=yb[:, 1, :], start=False, stop=True)))
                    YR.release(yi, [u_im])
                    bi_, bb, bfree = BR.next()
                    e1 = P.op("act", (lambda e, qre=qre, bb=bb: e.activation(out=bb[:, 0, :], in_=qre[:], func=AF.Copy)), deps=[u_re] + bfree)
                    e2 = P.op("act", (lambda e, qim=qim, bb=bb: e.activation(out=bb[:, 1, :], in_=qim[:], func=AF.Copy)), deps=[u_im])
                    R8.release(j1, [e1])
                    R8.release(j2, [e2])
                    d = P.dma("sp", T3v[:, :, p, :], bb[:], deps=[e2], sem=f"hb{bi_}")
                    BR.release(bi_, [d])

                for p0 in range(PF):
                    kload(p0)
                for i in range(33 + SK):
                    if i < 33:
                        hb_stage1(i)
                    if i >= SK:
                        hb_stage2(i - SK)
            P.barrier()
            with ExitStack() as es2:
                sb2 = lambda n, s, d: es2.enter_context(nc.sbuf_tensor(n, s, d))
                G_sb = sb2(f"hG{o}", [128, 32, CH], BF16)
                Z_sb = sb2(f"hZ{o}", [128, 32, CH], BF16)
                Bc = [sb2(f"hBc{o}_{i}", [66, 16, CH], BF16) for i in range(3)]
                gv = gate.rearrange("(a bg bi) c -> bi a bg c", a=32, bg=32, bi=4)
                tgc = []
                for bc in range(8):
                    tgl = None
                    for bi in range(4):
                        tgl = P.dma("pool", G_sb[32 * bi:32 * bi + 32, bc * 4:(bc + 1) * 4, :], gv[bi][:, bc * 4:(bc + 1) * 4, :], sem=f"hg{bc}")
                    tgc.append(tgl)
                dv = dst[:, 0:CH].rearrange("(a bg bi) c -> bi a bg c", a=32, bg=32, bi=4)
                BR = Ring(Bc)
                zt = []
                binfo = {}

                def bload(bcx):
                    bi_x, bbx, bfreex = BR.next()
                    tbx = P.dma("sp", bbx[:], S["T3"][:, bcx * 16:(bcx + 1) * 16, :], deps=bfreex, sem=f"hc{bi_x}")
                    binfo[bcx] = (bi_x, bbx, tbx)
                bload(0)
                for bc in range(8):
                    tg = [tgc[bc]]
                    if bc + 1 < 8:
                        bload(bc + 1)
                    bi_, bb, tb_ = binfo.pop(bc)
                    last = None
                    for g in range(4):
                        bg = bc * 4 + g
                        idx, ps, fr = k.psf.next()
                        for bi in range(4):
                            last = P.op("pe", (lambda e, ps=ps, bb=bb, bi=bi, g=g: e.matmul(ps[:], lhsT=Hpad[:, bi, :], rhs=bb[:, g * 4 + bi, :],
                                                                                         start=(bi == 0), stop=(bi == 3))),
                                        deps=[tb_] + tl + fr, sem=("pe" if bi == 3 else None))
                        tz = P.op("dve", (lambda e, ps=ps, bg=bg: e.tensor_tensor(out=Z_sb[:, bg, :], in0=ps[:], in1=G_sb[:, bg, :], op=ALU.mult)),
                                  deps=[last] + tg)
                        k.psf.release(idx, [tz])
                        zt.append(tz)
                    BR.release(bi_, [last])
                    for bi in range(4):
                        P.dma("sp", dv[bi][:, bc * 4:(bc + 1) * 4, :], Z_sb[32 * bi:32 * bi + 32, bc * 4:(bc + 1) * 4, :], deps=[zt[-1]], sem="hz")
            P.barrier()


def norm_stage(k, src, tiles, nw_name, hnT, tag, excl=()):
    nc, P, I = k.nc, k.P, k.I
    with ExitStack() as es:
        sb = lambda n, s, d: es.enter_context(nc.sbuf_tensor(n, s, d))
        nw = sb(f"nw{tag}", [128, D], F32)
        xt = [sb(f"nx{tag}{i}", [128, D], F32) for i in range(4)]
        xn = [sb(f"nxn{tag}{i}", [128, D], BF16) for i in range(3)]
        junk = sb(f"nj{tag}", [128, D], BF16)
        st = [sb(f"nst{tag}{i}", [128, 4], F32) for i in range(4)]
        tnw = P.dma("sp", nw[:], I[nw_name], sem=f"nw{tag}")
        XR, NR = Ring(xt), Ring(xn)
        pend = []

        def stage1(slot, row0):
            xi, xb, xfree = XR.next()
            tx = P.dma("sp", xb[:], src[row0:row0 + 128, :], deps=xfree, sem=f"nx{tag}{xi}")
            ni, nb_, nfree = NR.next()
            stt = st[xi]
            t1 = P.op("act", (lambda e, xb=xb, stt=stt: e.activation(out=junk[:], in_=xb[:], func=AF.Square, accum_out=stt[:, 0:1])),
                      deps=[tx])
            t2 = P.op("act", (lambda e, stt=stt: e.activation(out=stt[:, 1:2], in_=stt[:, 0:1], func=AF.Sqrt, scale=1.0 / D, bias=k.epst[:])),
                      deps=[t1])
            t3 = P.op("dve", (lambda e, stt=stt: e.reciprocal(out=stt[:, 2:3], in_=stt[:, 1:2])), deps=[t2])
            t4 = P.op("dve", (lambda e, xb=xb, nb_=nb_, stt=stt: e.scalar_tensor_tensor(out=nb_[:], in0=xb[:], scalar=stt[:, 2:3], in1=nw[:],
                                                                                     op0=ALU.mult, op1=ALU.mult)),
                      deps=[t3, tnw] + nfree)
            XR.release(xi, [t4])
            pend.append((slot, ni, nb_, t4))

        def stage2():
            slot, ni, nb_, t4 = pend.pop(0)
            toks, last_pe = transpose_to_T(k, nb_, t4, hnT, slot * 128)
            NR.release(ni, [last_pe])

        for (slot, row0) in tiles:
            stage1(slot, row0)
            if len(pend) > 1:
                stage2()
        while pend:
            stage2()
    P.barrier(exclude=excl)


def phase_AB1(k):
    for ph in range(2):
        _ab1_pass(k, ph)


def _ab1_pass(k, ph):
    nc, P, I, S = k.nc, k.P, k.I, k.S
    if True:
        T0 = ph * OWN
        with ExitStack() as es:
            sb = lambda n, s, d: es.enter_context(nc.sbuf_tensor(n, s, d))
            hnT = sb(f"hnT{ph}", [128, 16, 18 * 128], BF16)
            tiles = [(s_, T0 + (s_ - 1) * 128) for s_ in range(1, 17)]
            tiles.append((17, OWN) if ph == 0 else (0, OWN - 128))
            norm_stage(k, I["x"], tiles, "n1_bc", hnT, f"a{ph}")
            with ExitStack() as es2:
                sb2 = lambda n, s, d: es2.enter_context(nc.sbuf_tensor(n, s, d))
                Wb = [sb2(f"bW{ph}{i}", [128, 16, 512], BF16) for i in range(2)]
                Wg = [sb2(f"bWg{ph}{i}", [128, 16, 64], BF16) for i in range(2)]
                praw = [sb2(f"bP{ph}{i}", [128, OWN + 2], F32) for i in range(2)]
                tmpf = [sb2(f"bT{ph}{i}", [128, OWN], F32) for i in range(2)]
                cvo = [sb2(f"bC{ph}{i}", [128, OWN], BF16) for i in range(3)]
                TM = [sb2(f"bTM{ph}{i}", [128, 16, 512], BF16) for i in range(1)]
                vst = [sb2(f"bV{ph}{i}", [128, 4, 512], BF16) for i in range(2)]
                gst = [sb2(f"bG{ph}{i}", [64, OWN], F32) for i in range(2)]
                cwb = sb2(f"bcwb{ph}", [128, 20, 4], F32)
                tcw = P.dma("sp", cwb[:], I["cwb"], sem="bcw")
                padcol = 0 if ph == 0 else OWN + 1
                tpad = None
                for i in range(2):
                    tpad = P.op("dve", (lambda e, i=i: e.memset(praw[i][:, padcol:padcol + 1], 0.0)))
                WR, PR, TR_, CR, TMR, VR, GR, WGR = Ring(Wb), Ring(praw), Ring(tmpf), Ring(cvo), Ring(TM), Ring(vst), Ring(gst), Ring(Wg)
                groups = [("hy", 0, S["HV"]), ("hy", 1, S["HX1"]), ("hy", 2, S["HX2"]), ("q", 0, S["QT"]), ("k", 1, S["KT"])]
                for gi_, (kind, gsub, dstd) in enumerate(groups):
                    wsrc = I["w_hy"][:, gsub * 512:(gsub + 1) * 512] if kind == "hy" else I["w_qk"][:, gsub * 512:(gsub + 1) * 512]
                    wi, wb, wfree = WR.next()
                    tw = load_w_bf16(k, wb[:], wsrc, wfree, f"bw{wi}")
                    wusers = []
                    need_tm = kind in ("hy", "k")
                    if need_tm:
                        tmi, tmb, tmfree = TMR.next()
                        tm_evs = []
                    pending = []
                    for m in range(4):
                        tix = gi_ * 4 + m
                        pi_, pr, pfree = PR.next()
                        evs = []
                        for tb in range(5):
                            idx, ps, fr = k.psf.next()
                            if tb < 4:
                                c0, n = 128 + tb * 512, 512
                                o0 = 1 + tb * 512
                            else:
                                c0, n = (128 + OWN, 1) if ph == 0 else (127, 1)
                                o0 = OWN + 1 if ph == 0 else 0
                            last = None
                            for kc in range(16):
                                last = P.op("pe", (lambda e, ps=ps, wb=wb, kc=kc, m=m, c0=c0, n=n: e.matmul(
                                    ps[:, 0:n], lhsT=wb[:, kc, m * 128:(m + 1) * 128], rhs=hnT[:, kc, c0:c0 + n],
                                    start=(kc == 0), stop=(kc == 15))), deps=[tw] + fr, sem=("pe" if kc == 15 else None))
                            wusers.append(last)
                            te = P.op("act", (lambda e, ps=ps, pr=pr, o0=o0, n=n: e.activation(out=pr[:, o0:o0 + n], in_=ps[:, 0:n], func=AF.Copy)),
                                      deps=[last] + pfree + [tpad])
                            k.psf.release(idx, [te])
                            evs.append(te)
                        while pending:
                            pending.pop(0)()
                        ti_, tf, tfree = TR_.next()
                        ci, cb, cfree = CR.next()
                        a1 = P.op("act", (lambda e, pr=pr, tf=tf, tix=tix: e.activation(out=tf[:], in_=pr[:, 1:OWN + 1], func=AF.Identity,
                                                                                       scale=cwb[:, tix, 1:2], bias=cwb[:, tix, 3:4])),
                                  deps=[evs[-1], tcw] + tfree)
                        d1 = P.op("dve", (lambda e, pr=pr, tf=tf, tix=tix: e.scalar_tensor_tensor(out=tf[:], in0=pr[:, 0:OWN], scalar=cwb[:, tix, 0:1],
                                                                                                 in1=tf[:], op0=ALU.mult, op1=ALU.add)), deps=[a1])
                        if kind == "hy":
                            d2 = P.op("dve", (lambda e, pr=pr, tf=tf, cb=cb, tix=tix: e.scalar_tensor_tensor(
                                out=cb[:], in0=pr[:, 2:OWN + 2], scalar=cwb[:, tix, 2:3], in1=tf[:], op0=ALU.mult, op1=ALU.add)),
                                deps=[d1] + cfree)
                            PR.release(pi_, [d2])
                            TR_.release(ti_, [d2])
                            cready = d2
                        else:
                            d2 = P.op("dve", (lambda e, pr=pr, tf=tf, tix=tix: e.scalar_tensor_tensor(
                                out=tf[:], in0=pr[:, 2:OWN + 2], scalar=cwb[:, tix, 2:3], in1=tf[:], op0=ALU.mult, op1=ALU.add)),
                                deps=[d1])
                            PR.release(pi_, [d2])
                            a2 = P.op("act", (lambda e, tf=tf, cb=cb: e.activation(out=cb[:], in_=tf[:], func=AF.Silu)), deps=[d2] + cfree)
                            TR_.release(ti_, [a2])
                            cready = a2
                        cusers = []
                        if kind in ("q", "k"):
                            dq = P.dma("sp", dstd[m, :, T0:T0 + OWN], cb[:], deps=[cready], sem=f"bq{ci}")
                            cusers.append(dq)
                        if need_tm:
                            def make_tr(m=m, cb=cb, ci=ci, cready=cready, cusers=cusers):
                                def do_tr():
                                    lastpe = None
                                    for half in range(2):
                                        bidx, pb, bfr = k.psb.next()
                                        for j in range(8):
                                            tt = half * 8 + j
                                            lastpe = P.op("pe", (lambda e, pb=pb, cb=cb, j=j, tt=tt: e.transpose(
                                                out=pb[:, j * 128:(j + 1) * 128], in_=cb[:, tt * 128:(tt + 1) * 128], identity=k.ident[:])),
                                                deps=[cready] + bfr, sem=("pe" if j == 7 else None))
                                        dstv = tmb[:, half * 8:(half + 1) * 8, m * 128:(m + 1) * 128]
                                        srcv = pb[:, :].rearrange("p (t c) -> p t c", c=128)
                                        if half == 0:
                                            tev = P.op("act", (lambda e, dstv=dstv, srcv=srcv: e.activation(out=dstv, in_=srcv, func=AF.Copy)),
                                                       deps=[lastpe] + tmfree)
                                        else:
                                            tev = P.op("dve", (lambda e, dstv=dstv, srcv=srcv: e.tensor_copy(out=dstv, in_=srcv)),
                                                       deps=[lastpe] + tmfree)
                                        k.psb.release(bidx, [tev])
                                        tm_evs.append(tev)
                                    CR.release(ci, cusers + [lastpe])
                                return do_tr
                            pending.append(make_tr())
                        else:
                            CR.release(ci, cusers)
                    while pending:
                        pending.pop(0)()
                    WR.release(wi, [wusers[-1]])
                    if need_tm:
                        dsttm = S["KTM"] if kind == "k" else dstd
                        dtm = P.dma("sp", dsttm[T0:T0 + OWN, :].rearrange("(tt p) c -> p tt c", p=128), tmb[:], deps=tm_evs[-2:], sem=f"btm{tmi}")
                        TMR.release(tmi, [dtm])
                for gsub, dstd in ((0, S["VTM"]), (1, S["OTM"])):
                    wi, wb, wfree = WR.next()
                    tw = load_w_bf16(k, wb[:], I["w_vo"][:, gsub * 512:(gsub + 1) * 512], wfree, f"bw{wi}")
                    last = None
                    for tt in range(16):
                        if tt % 4 == 0:
                            vi, vb, vfree = VR.next()
                            vevs = []
                        idx, ps, fr = k.psf.next()
                        for kc in range(16):
                            last = P.op("pe", (lambda e, ps=ps, wb=wb, kc=kc, tt=tt: e.matmul(
                                ps[:], lhsT=hnT[:, kc, 128 + tt * 128:128 + (tt + 1) * 128], rhs=wb[:, kc, :],
                                start=(kc == 0), stop=(kc == 15))), deps=[tw] + fr, sem=("pe" if kc == 15 else None))
                        fn_ = AF.Copy if gsub == 0 else AF.Sigmoid
                        te = P.op("act", (lambda e, ps=ps, vb=vb, tt=tt, fn_=fn_: e.activation(out=vb[:, tt % 4, :], in_=ps[:], func=fn_)),
                                  deps=[last] + vfree)
                        k.psf.release(idx, [te])
                        vevs.append(te)
                        if tt % 4 == 3:
                            r0 = T0 + (tt - 3) * 128
                            dv = P.dma("sp", dstd[r0:r0 + 512, :].rearrange("(j p) c -> p j c", p=128), vb[:], deps=[vevs[-1]], sem=f"bv{vi}")
                            VR.release(vi, [dv])
                    WR.release(wi, [last])
                for gname, dstd in (("w_gi", S["IG"]), ("w_gf", S["FG"])):
                    wi, wb, wfree = WGR.next()
                    tw = load_w_bf16(k, wb[:], I[gname], wfree, f"bwg{wi}")
                    gi2, gb_, gfree = GR.next()
                    gev = None
                    last = None
                    for tb in range(4):
                        idx, ps, fr = k.psf.next()
                        for kc in range(16):
                            last = P.op("pe", (lambda e, ps=ps, wb=wb, kc=kc, tb=tb: e.matmul(
                                ps[0:64, :], lhsT=wb[:, kc, :], rhs=hnT[:, kc, 128 + tb * 512:128 + (tb + 1) * 512],
                                start=(kc == 0), stop=(kc == 15))), deps=[tw] + fr, sem=("pe" if kc == 15 else None))
                        gev = P.op("act", (lambda e, ps=ps, gb_=gb_, tb=tb: e.activation(out=gb_[:, tb * 512:(tb + 1) * 512], in_=ps[0:64, :], func=AF.Copy)),
                                   deps=[last] + gfree)
                        k.psf.release(idx, [gev])
                    WGR.release(wi, [last])
                    dg = P.dma("sp", dstd[:, T0:T0 + OWN], gb_[:], deps=[gev], sem=f"bg{gi2}")
                    GR.release(gi2, [dg])
            P.barrier()


def phase_B2(k):
    nc, P, I, S = k.nc, k.P, k.I, k.S
    with ExitStack() as es:
        sb = lambda n, s, d: es.enter_context(nc.sbuf_tensor(n, s, d))
        hnT = sb("hnTo", [128, 16, OWN], BF16)
        norm_stage(k, I["x_own"], [(s_, s_ * 128) for s_ in range(16)], "n1_bc", hnT, "b2", excl=("cc",))
        with ExitStack() as es2:
            sb2 = lambda n, s, d: es2.enter_context(nc.sbuf_tensor(n, s, d))
            Wb = [sb2(f"mW{i}", [128, 16, 512], BF16) for i in range(2)]
            gst = [sb2(f"mG{i}", [128, OWN], BF16) for i in range(3)]
            WR, GR = Ring(Wb), Ring(gst)
            for g in range(8):
                wi, wb, wfree = WR.next()
                tw = load_w_bf16(k, wb[:], I["w_mg"][:, g * 512:(g + 1) * 512], wfree, f"mw{wi}")
                last = None
                for m in range(4):
                    gi2, gb_, gfree = GR.next()
                    ev = None
                    for tb in range(4):
                        idx, ps, fr = k.psf.next()
                        for kc in range(16):
                            last = P.op("pe", (lambda e, ps=ps, wb=wb, kc=kc, m=m, tb=tb: e.matmul(
                                ps[:], lhsT=wb[:, kc, m * 128:(m + 1) * 128], rhs=hnT[:, kc, tb * 512:(tb + 1) * 512],
                                start=(kc == 0), stop=(kc == 15))), deps=[tw] + fr, sem=("pe" if kc == 15 else None))
                        ev = P.op("act", (lambda e, ps=ps, gb_=gb_, tb=tb: e.activation(out=gb_[:, tb * 512:(tb + 1) * 512], in_=ps[:], func=AF.Sigmoid)),
                                  deps=[last] + gfree)
                        k.psf.release(idx, [ev])
                    dt_ = (g % 4) * 4 + m
                    dstd = S["GAT"] if g < 4 else S["GBT"]
                    dg = P.dma("sp", dstd[dt_, :, :], gb_[:], deps=[ev], sem=f"mg{gi2}")
                    GR.release(gi2, [dg])
                WR.release(wi, [last])
    P.barrier()


def rev(ap2, n=None):
    (ps, pn), (fs, fn) = ap2.ap
    return bass.AP(ap2.tensor, ap2.offset + (fn - 1) * fs, [[ps, pn], [-fs, fn]])


def phase_M(k):
    nc, P, I, S = k.nc, k.P, k.I, k.S
    with ExitStack() as es:
        sb = lambda n, s, d: es.enter_context(nc.sbuf_tensor(n, s, d))
        WT = sb("mWT", [128, 32, 64], F32)
        FT = sb("mFT", [128, 32, 64], F32)
        RB = sb("mRB", [128, 64 * 32], F32)
        maskf = sb("mmaskf", [128, 128], F32)
        maskb = sb("mmaskb", [128, 128], F32)
        tmk = [P.dma("sp", maskf[:], I["maskf"], sem="mk"), P.dma("sp", maskb[:], I["maskb"], sem="mk")]
        tmk = [tmk[-1]]
        with ExitStack() as es2:
            sb2 = lambda n, s, d: es2.enter_context(nc.sbuf_tensor(n, s, d))
            IG = sb2("pIG", [64, L], F32)
            FG = sb2("pFG", [64, L], F32)
            RM = sb2("pRM", [64, L], F32)
            Lc = sb2("pLc", [64, L], F32)
            G = sb2("pG", [64, L], F32)
            TA = sb2("pTA", [64, L], F32)
            Wp = sb2("pWp", [64, L], F32)
            FL = sb2("pFL", [64, L], F32)
            gb = sb2("pgb", [64, 2], F32)
            cst = sb2("pcst", [64, 4], F32)
            sm = sb2("psm", [64, 8, 32], F32)
            ld = [P.dma("sp", IG[:], S["IG"], sem="pl"), P.dma("sp", FG[:], S["FG"], sem="pl"),
                  P.dma("sp", RM[:], I["resetmask"], sem="pl"), P.dma("sp", gb[:], I["gbias"], sem="pl")]
            ld = [ld[-1]]
            c1 = P.op("dve", lambda e: e.tensor_scalar(out=cst[:, 0:1], in0=gb[:, 1:2], scalar1=-1.0, scalar2=None, op0=ALU.mult), deps=ld)
            c2 = P.op("dve", lambda e: e.memset(cst[:, 1:2], 1.0))
            c3 = P.op("dve", lambda e: e.memset(cst[:, 2:3], float(np.log(1.0 / 16.0))))
            c4 = P.op("dve", lambda e: e.memset(sm[:, 4, :], 0.0))
            a1 = P.op("act", lambda e: e.activation(out=TA[:], in_=FG[:], func=AF.Exp, scale=-1.0, bias=cst[:, 0:1]), deps=[c1] + ld)
            a2 = P.op("act", lambda e: e.activation(out=FG[:], in_=TA[:], func=AF.Ln, scale=1.0, bias=cst[:, 1:2]), deps=[a1, c2])
            s1 = P.op("dve", lambda e: e.tensor_tensor_scan(out=Lc[0:32, :], data0=RM[0:32, :], data1=FG[0:32, :], initial=0.0,
                                                            op0=ALU.mult, op1=ALU.add), deps=[a2])
            s2 = P.op("dve", lambda e: e.tensor_tensor_scan(out=rev(Lc[32:64, :]), data0=rev(RM[32:64, :]), data1=rev(FG[32:64, :]), initial=0.0,
                                                            op0=ALU.mult, op1=ALU.add), deps=[a2])
            g1 = P.op("dve", lambda e: e.scalar_tensor_tensor(out=G[:], in0=IG[:], scalar=gb[:, 0:1], in1=Lc[:], op0=ALU.add, op1=ALU.add), deps=[s2])
            g2 = P.op("dve", lambda e: e.tensor_reduce(out=sm[:, 0, :], in_=G[:, :].rearrange("p (c j) -> p c j", j=128), axis=AX.X, op=ALU.max), deps=[g1])
            lcv = Lc[:, :].rearrange("p (c j) -> p c j", j=128)
            n1 = P.op("dve", lambda e: e.tensor_copy(out=sm[0:32, 6, :], in_=lcv[0:32, :, 127]), deps=[g2])
            n2 = P.op("dve", lambda e: e.tensor_copy(out=sm[32:64, 6, :], in_=lcv[32:64, :, 0]), deps=[n1])
            n3 = P.op("dve", lambda e: e.tensor_scalar(out=sm[:, 1, :], in0=sm[:, 6, :], scalar1=-1.0, scalar2=None, op0=ALU.mult), deps=[n2])
            m1 = P.op("dve", lambda e: e.tensor_tensor_scan(out=sm[0:32, 2, :], data0=sm[0:32, 0, :], data1=sm[0:32, 1, :], initial=0.0,
                                                            op0=ALU.max, op1=ALU.add), deps=[n3])
            m2 = P.op("dve", lambda e: e.tensor_tensor_scan(out=rev(sm[32:64, 2, :]), data0=rev(sm[32:64, 0, :]), data1=rev(sm[32:64, 1, :]), initial=0.0,
                                                            op0=ALU.max, op1=ALU.add), deps=[m1])
            m3 = P.op("dve", lambda e: e.tensor_tensor(out=sm[:, 3, :], in0=sm[:, 2, :], in1=sm[:, 6, :], op=ALU.add), deps=[m2])
            m4 = P.op("dve", lambda e: e.tensor_copy(out=sm[0:32, 4, 1:32], in_=sm[0:32, 2, 0:31]), deps=[m3, c4])
            m5 = P.op("dve", lambda e: e.tensor_copy(out=sm[32:64, 4, 0:31], in_=sm[32:64, 2, 1:32]), deps=[m4])
            m6 = P.op("dve", lambda e: e.tensor_tensor(out=sm[:, 5, :], in0=sm[:, 4, :], in1=sm[:, 3, :], op=ALU.subtract), deps=[m5])
            m7 = P.op("act", lambda e: e.activation(out=sm[:, 5, :], in_=sm[:, 5, :], func=AF.Exp), deps=[m6])
            dr = P.dma("sp", S["RHO"], sm[:, 5, :], deps=[m7], sem="prho")
            rsrc = bass.AP(S["RHO"].tensor, S["RHO"].offset, [[0, 128], [1, 64 * 32]])
            dr2 = P.dma("sp", RB[:], rsrc, deps=[dr], sem="prho")
            gcb = bass.AP(sm[:, 3, :].tensor, sm[:, 3, :].offset, [list(sm[:, 3, :].ap[0]), [1, 32], [0, 128]])
            w1 = P.op("dve", lambda e: e.tensor_tensor(out=TA[:, :].rearrange("p (c j) -> p c j", j=128), in0=G[:, :].rearrange("p (c j) -> p c j", j=128),
                                                       in1=gcb, op=ALU.subtract), deps=[m3, a2])
            w2 = P.op("act", lambda e: e.activation(out=Wp[:], in_=TA[:], func=AF.Exp, bias=cst[:, 2:3], scale=1.0), deps=[w1, c3])
            w3 = P.op("dve", lambda e: e.tensor_tensor(out=G[:, :].rearrange("p (c j) -> p c j", j=128), in0=lcv, in1=gcb, op=ALU.subtract), deps=[w1])
            w4 = P.op("act", lambda e: e.activation(out=FL[:], in_=G[:], func=AF.Exp), deps=[w3])
            for (src, dstT, tsrc) in ((Wp, WT, w2), (FL, FT, w4)):
                for blk in range(4):
                    idx, ps, fr = k.psf.next()
                    last = None
                    for j in range(8):
                        c = blk * 8 + j
                        last = P.op("pe", (lambda e, ps=ps, src=src, j=j, c=c: e.transpose(out=ps[:, j * 64:(j + 1) * 64], in_=src[:, c * 128:(c + 1) * 128],
                                                                                          identity=k.identf[0:64, 0:64])),
                                    deps=[tsrc] + fr + k.c_tok, sem=("pe" if j == 7 else None))
                    te = P.op("dve", (lambda e, ps=ps, dstT=dstT, blk=blk: e.tensor_copy(out=dstT[:, blk * 8:(blk + 1) * 8, :],
                                                                                       in_=ps[:, :].rearrange("p (c r) -> p c r", r=64))), deps=[last])
                    k.psf.release(idx, [te])
            prep_done = dr2
        P.barrier()
        for hl in range(2):
            _m_head(k, hl, WT, FT, RB, maskf, maskb)
    P.barrier()


def _m_head(k, hl, WT, FT, RB, maskf, maskb):
    nc, P, I, S = k.nc, k.P, k.I, k.S
    with ExitStack() as es:
        sb = lambda n, s, d: es.enter_context(nc.sbuf_tensor(n, s, d))
        qT = sb(f"hq{hl}", [128, 2, L], BF16)
        kT = sb(f"hk{hl}", [128, 2, L], BF16)
        KM = sb(f"hkm{hl}", [128, 32, 256], BF16)
        VA = sb(f"hva{hl}", [128, 32, 257], BF16)
        OH = sb(f"hoh{hl}", [128, 32, 256], BF16)
        HF = sb(f"hhf{hl}", [128, 32, 256], F32)
        YB = sb(f"hyb{hl}", [128, 32, 256], BF16)
        Cst = [sb(f"hC{hl}{d}", [128, 2, 257], F32) for d in range(2)]
        Cs = [[sb(f"hCs{hl}{d}{i}", [128, 2, 257], BF16) for i in range(2)] for d in range(2)]
        SKA = 2
        dCs = [sb(f"hdC{hl}{i}", [128, 2, 257], F32) for i in range(2 * (SKA + 1) + 1)]
        kt_ = [sb(f"hkt{hl}{i}", [128, 256], BF16) for i in range(4)]
        St = [sb(f"hSt{hl}{i}", [128, 128], BF16) for i in range(2 * (SKA + 1) + 1)]
        dn = [sb(f"hdn{hl}{i}", [128, 2], F32) for i in range(4)]
        ht = [sb(f"hht{hl}{i}", [128, 256], F32) for i in range(3)]
        lds = []
        for dt in range(2):
            lds.append(P.dma("sp", qT[:, dt, :], S["QT"][hl * 2 + dt], sem="hl0"))
            lds.append(P.dma("sp", kT[:, dt, :], S["KT"][hl * 2 + dt], sem="hl0"))
        hc = slice(hl * 256, (hl + 1) * 256)
        lds.append(P.dma("sp", KM[:], S["KTM"][:, hc].rearrange("(c s) d -> s c d", s=128), sem="hl0"))
        lds.append(P.dma("sp", VA[:, :, 0:256], S["VTM"][:, hc].rearrange("(c s) d -> s c d", s=128), sem="hl0"))
        lds.append(P.dma("sp", OH[:], S["OTM"][:, hc].rearrange("(c s) d -> s c d", s=128), sem="hl0"))
        lds = [lds[-1]]
        tone = P.op("dve", lambda e: e.memset(VA[:, :, 256:257], 1.0))
        ready = lds + [tone]
        KR, SR, DR, HR, DCR = Ring(kt_), Ring(St), Ring(dn), Ring(ht), Ring(dCs)
        CR = [Ring(Cs[0]), Ring(Cs[1])]
        R8 = k.ps8()
        hf_tok = {}
        ylast = [None]
        stA = {}
        st = []
        for dr_ in range(2):
            tz = P.op("dve", (lambda e, dr_=dr_: e.memset(Cst[dr_][:], 0.0)))
            st.append({"row": 32 * dr_ + hl, "mask": maskf if dr_ == 0 else maskb,
                       "order": list(range(32)) if dr_ == 0 else list(range(31, -1, -1)),
                       "c_upd": [tz], "cs_cur": None})

        def stageA(dr_, oi):
            sd = st[dr_]
            c = sd["order"][oi]
            row, mask = sd["row"], sd["mask"]
            csl = slice(c * 128, (c + 1) * 128)
            wcol = WT[:, c, row:row + 1]
            i1, pS, f1 = R8.next()
            P.op("pe", (lambda e, pS=pS, csl=csl: e.matmul(pS[:, 0:128], lhsT=kT[:, 0, csl], rhs=qT[:, 0, csl], start=True, stop=False)),
                 deps=ready + f1, sem=None)
            tS = P.op("pe", (lambda e, pS=pS, csl=csl: e.matmul(pS[:, 0:128], lhsT=kT[:, 1, csl], rhs=qT[:, 1, csl], start=False, stop=True)))
            si, stb, sfree = SR.next()
            t2 = P.op("dve", (lambda e, pS=pS, stb=stb, wcol=wcol, mask=mask: e.scalar_tensor_tensor(
                out=stb[:], in0=pS[:, 0:128], scalar=wcol, in1=mask[:], op0=ALU.mult, op1=ALU.mult)), deps=[tS] + sfree)
            R8.release(i1, [t2])
            dinfo = None
            if oi < 31:
                ki, kb, kfree = KR.next()
                t8 = P.op("act", (lambda e, kb=kb, c=c, wcol=wcol: e.activation(out=kb[:], in_=KM[:, c, :], func=AF.Copy, scale=wcol)),
                          deps=ready + kfree)
                di_, dcb, dcfree = DCR.next()
                last9 = None
                tdc = None
                for kt in range(2):
                    i3, pC, f3 = R8.next()
                    last9 = P.op("pe", (lambda e, pC=pC, kb=kb, kt=kt, c=c: e.matmul(pC[:, 0:257], lhsT=kb[:, kt * 128:(kt + 1) * 128], rhs=VA[:, c, :],
                                                                                  start=True, stop=True)), deps=[t8] + f3)
                    if kt == 0:
                        tdc0 = P.op("act", (lambda e, pC=pC, dcb=dcb, kt=kt: e.activation(out=dcb[:, kt, :], in_=pC[:, 0:257], func=AF.Copy)),
                                    deps=[last9] + dcfree)
                        R8.release(i3, [tdc0])
                    else:
                        tdc1 = P.op("dve", (lambda e, pC=pC, dcb=dcb, kt=kt: e.tensor_copy(out=dcb[:, kt, :], in_=pC[:, 0:257])),
                                    deps=[last9] + dcfree)
                        R8.release(i3, [tdc1])
                        tdc = [tdc0, tdc1]
                KR.release(ki, [last9])
                dinfo = (di_, dcb, tdc)
            stA[(dr_, oi)] = (si, stb, t2, dinfo)

        def stageB(dr_, oi):
            sd = st[dr_]
            c = sd["order"][oi]
            row = sd["row"]
            csl = slice(c * 128, (c + 1) * 128)
            fcol = FT[:, c, row:row + 1]
            rho_c = RB[:, row * 32 + c:row * 32 + c + 1]
            si, stb, t2, dinfo = stA.pop((dr_, oi))
            i2, pN, f2 = R8.next()
            first = (oi == 0)
            tN = P.op("pe", (lambda e, pN=pN, stb=stb, c=c, first=first: e.matmul(pN[:, 0:257], lhsT=stb[:], rhs=VA[:, c, :], start=True, stop=first)),
                      deps=[t2] + f2, sem=("pe" if first else None))
            if not first:
                cs_idx, cbuf, ctoks = sd["cs_cur"]
                P.op("pe", (lambda e, pN=pN, cbuf=cbuf, csl=csl: e.matmul(pN[:, 0:257], lhsT=qT[:, 0, csl], rhs=cbuf[:, 0, :], start=False, stop=False)),
                     deps=ctoks, sem=None)
                tN = P.op("pe", (lambda e, pN=pN, cbuf=cbuf, csl=csl: e.matmul(pN[:, 0:257], lhsT=qT[:, 1, csl], rhs=cbuf[:, 1, :], start=False, stop=True)))
                CR[dr_].release(cs_idx, [tN])
            SR.release(si, [tN])
            di, dnb, dfree = DR.next()
            t4a = P.op("act", (lambda e, pN=pN, dnb=dnb: e.activation(out=dnb[:, 0:1], in_=pN[:, 256:257], func=AF.Abs)), deps=[tN] + dfree)
            t4 = P.op("dve", (lambda e, dnb=dnb, fcol=fcol: e.tensor_tensor(out=dnb[:, 0:1], in0=dnb[:, 0:1], in1=fcol, op=ALU.max)), deps=[t4a])
            t5 = P.op("dve", (lambda e, dnb=dnb: e.reciprocal(out=dnb[:, 1:2], in_=dnb[:, 0:1])), deps=[t4])
            if c not in hf_tok:
                t6 = P.op("act", (lambda e, pN=pN, dnb=dnb, c=c: e.activation(out=HF[:, c, :], in_=pN[:, 0:256], func=AF.Copy, scale=dnb[:, 1:2])),
                          deps=[t5])
                R8.release(i2, [t6])
                DR.release(di, [t6])
                hf_tok[c] = t6
            else:
                hi_, hb_, hfree = HR.next()
                t6 = P.op("dve", (lambda e, pN=pN, dnb=dnb, c=c, hb_=hb_: e.scalar_tensor_tensor(
                    out=hb_[:], in0=pN[:, 0:256], scalar=dnb[:, 1:2], in1=HF[:, c, :], op0=ALU.mult, op1=ALU.add)), deps=[t5, hf_tok[c]] + hfree)
                R8.release(i2, [t6])
                DR.release(di, [t6])
                t7 = P.op("pool", (lambda e, hb_=hb_, c=c: e.tensor_tensor(out=YB[:, c, :], in0=hb_[:], in1=OH[:, c, :], op=ALU.mult)),
                          deps=[t6] + ready)
                HR.release(hi_, [t7])
                ylast[0] = t7
            if oi < 31:
                cn = sd["order"][oi + 1]
                rho_n = RB[:, row * 32 + cn:row * 32 + cn + 1]
                di_, dcb, tdc = dinfo
                cs_idx, cbn, cfree = CR[dr_].next()
                Cd = Cst[dr_]
                ups = []
                tu = P.op("dve", (lambda e, Cd=Cd, dcb=dcb, rho_c=rho_c: e.scalar_tensor_tensor(
                    out=Cd[:, :, :], in0=Cd[:, :, :], scalar=rho_c, in1=dcb[:, :, :], op0=ALU.mult, op1=ALU.add)), deps=tdc + sd["c_upd"])
                ts = P.op("act", (lambda e, cbn=cbn, Cd=Cd, rho_n=rho_n: e.activation(out=cbn[:, :, :], in_=Cd[:, :, :], func=AF.Copy, scale=rho_n)),
                          deps=[tu] + cfree)
                ups.append(ts)
                DCR.release(di_, [tu])
                sd["c_upd"] = [ups[-1]]
                sd["cs_cur"] = (cs_idx, cbn, [ups[-1]])

        for i in range(32 + SKA):
            if i < 32:
                for dr_ in range(2):
                    stageA(dr_, i)
            if i >= SKA:
                for dr_ in range(2):
                    stageB(dr_, i - SKA)
        ccv = S["CCIN"][:, 512 + hl * 256:512 + (hl + 1) * 256].rearrange("(c s) d -> s c d", s=128)
        P.dma("sp", ccv, YB[:], deps=[ylast[0]], sem="hyb")
    P.barrier()


def phase_X(k):
    P, S = k.P, k.S
    if hasattr(k, "ccin_ext"):
        P.dma("sp", S["CCIN"], k.ccin_ext, sem="xcp")
        P.barrier()
    rg = [[0, 1], [2, 3], [4, 5], [6, 7]]
    for j in range(4):
        src = S["CCIN"][j * 1024:(j + 1) * 1024, :]
        dst = S["CCOUT"][j * 2048:(j + 1) * 2048, :]
        P.op("pool", (lambda e, src=src, dst=dst: e.collective_compute("AllGather", ALU.bypass, replica_groups=rg,
                                                                       ins=[src.opt()], outs=[dst.opt()])), sem="cc", inc=1)


def phase_C(k):
    nc, P, I, S = k.nc, k.P, k.I, k.S
    with ExitStack() as es:
        sb = lambda n, s, d: es.enter_context(nc.sbuf_tensor(n, s, d))
        mixedT = sb("cmix", [128, 16, OWN], BF16)
        with ExitStack() as es1:
            sb1 = lambda n, s, d: es1.enter_context(nc.sbuf_tensor(n, s, d))
            yT = sb1("cyT", [128, 16, OWN], BF16)
            with ExitStack() as es2:
                sb2 = lambda n, s, d: es2.enter_context(nc.sbuf_tensor(n, s, d))
                Yt = [sb2(f"cY{i}", [128, 2, 1024], BF16) for i in range(3)]
                YR = Ring(Yt)
                def cpy(e, jj):
                    par = e.partition_id() % 2
                    return e.dma_start(out=S["YOWN"][jj], in_=S["CCOUT"][bass.ds(par * 4096 + jj * 2048, 2048), :])
                P.op("sp", (lambda e: cpy(e, 0)), sem="cyo", inc=16)
                tcp = P.op("sp", (lambda e: cpy(e, 1)), sem="cyo", inc=16)
                for tt in range(16):
                    yi, yb, yfree = YR.next()
                    jj, t8 = tt // 8, tt % 8
                    srcv = S["YOWN"][jj].rearrange("(r t) c -> t r c", r=2)[t8 * 128:(t8 + 1) * 128, :, :]
                    ty = P.dma("sp", yb[:], srcv, deps=[tcp] + yfree, sem=f"cy{yi}")
                    lastpe = None
                    for half in range(2):
                        bidx, pb, bfr = k.psb.next()
                        for j in range(8):
                            r, ct = j // 4, j % 4
                            c0 = half * 512 + ct * 128
                            lastpe = P.op("pe", (lambda e, pb=pb, yb=yb, j=j, r=r, c0=c0: e.transpose(
                                out=pb[:, j * 128:(j + 1) * 128], in_=yb[:, r, c0:c0 + 128], identity=k.ident[:])),
                                deps=[ty] + bfr + k.c_tok, sem=("pe" if j == 7 else None))
                        dstv = yT[:, half * 8:(half + 1) * 8, tt * 128:(tt + 1) * 128]
                        srcv = pb[:, :].rearrange("p (k t) -> p k t", t=128)
                        if half == 0:
                            tev = P.op("act", (lambda e, dstv=dstv, srcv=srcv: e.activation(out=dstv, in_=srcv, func=AF.Copy)), deps=[lastpe])
                        else:
                            tev = P.op("dve", (lambda e, dstv=dstv, srcv=srcv: e.tensor_copy(out=dstv, in_=srcv)), deps=[lastpe])
                        k.psb.release(bidx, [tev])
                    YR.release(yi, [lastpe])
            P.barrier()
            with ExitStack() as es2:
                sb2 = lambda n, s, d: es2.enter_context(nc.sbuf_tensor(n, s, d))
                Wa = [sb2(f"cWa{i}", [128, 8, 512], BF16) for i in range(2)]
                Wb = [sb2(f"cWb{i}", [128, 8, 512], BF16) for i in range(2)]
                GA = [sb2(f"cGA{i}", [128, OWN], BF16) for i in range(2)]
                GB = [sb2(f"cGB{i}", [128, OWN], BF16) for i in range(2)]
                t1s = [sb2(f"ct1{i}", [128, 512], F32) for i in range(2)]
                t2s = [sb2(f"ct2{i}", [128, 512], F32) for i in range(2)]
                WAR, WBR, GAR, GBR, T1R, T2R = Ring(Wa), Ring(Wb), Ring(GA), Ring(GB), Ring(t1s), Ring(t2s)
                for g in range(4):
                    wai, wa, wafree = WAR.next()
                    twa = load_w_bf16(k, wa[:], I["w_ba"][:, g * 512:(g + 1) * 512], wafree, f"cwa{wai}")
                    wbi, wb, wbfree = WBR.next()
                    twb = load_w_bf16(k, wb[:], I["w_bb"][:, g * 512:(g + 1) * 512], wbfree, f"cwb{wbi}")
                    lastb = None
                    for m in range(4):
                        dt_ = g * 4 + m
                        gai, ga, gafree = GAR.next()
                        tga = P.dma("sp", ga[:], S["GAT"][dt_], deps=gafree, sem=f"cga{gai}")
                        gbi, gb, gbfree = GBR.next()
                        tgb = P.dma("sp", gb[:], S["GBT"][dt_], deps=gbfree, sem=f"cgb{gbi}")
                        ua = ub = None
                        for tb in range(4):
                            ts_ = slice(tb * 512, (tb + 1) * 512)
                            ia, pa, fa = k.psf.next()
                            la = None
                            for ct in range(8):
                                la = P.op("pe", (lambda e, pa=pa, wa=wa, ct=ct, m=m, ts_=ts_: e.matmul(
                                    pa[:], lhsT=wa[:, ct, m * 128:(m + 1) * 128], rhs=yT[:, ct, ts_], start=(ct == 0), stop=(ct == 7))),
                                    deps=[twa] + fa, sem=("pe" if ct == 7 else None))
                            ib, pb_, fb = k.psf.next()
                            for ct in range(8):
                                lastb = P.op("pe", (lambda e, pb_=pb_, wb=wb, ct=ct, m=m, ts_=ts_: e.matmul(
                                    pb_[:], lhsT=wb[:, ct, m * 128:(m + 1) * 128], rhs=yT[:, 8 + ct, ts_], start=(ct == 0), stop=(ct == 7))),
                                    deps=[twb] + fb, sem=("pe" if ct == 7 else None))
                            i1, t1, f1 = T1R.next()
                            i2, t2, f2 = T2R.next()
                            ua = P.op("dve", (lambda e, pa=pa, ga=ga, t1=t1, ts_=ts_: e.tensor_tensor(out=t1[:], in0=pa[:], in1=ga[:, ts_], op=ALU.mult)),
                                      deps=[la, tga] + f1)
                            k.psf.release(ia, [ua])
                            ub = P.op("dve", (lambda e, pb_=pb_, gb=gb, t2=t2, ts_=ts_: e.tensor_tensor(out=t2[:], in0=pb_[:], in1=gb[:, ts_], op=ALU.mult)),
                                      deps=[lastb, tgb] + f2)
                            k.psf.release(ib, [ub])
                            um = P.op("dve", (lambda e, t1=t1, t2=t2, dt_=dt_, ts_=ts_: e.tensor_tensor(out=mixedT[:, dt_, ts_], in0=t1[:], in1=t2[:], op=ALU.add)),
                                      deps=[ub])
                            T1R.release(i1, [um])
                            T2R.release(i2, [um])
                        GAR.release(gai, [ua])
                        GBR.release(gbi, [ub])
                    WAR.release(wai, [lastb])
                    WBR.release(wbi, [lastb])
            P.barrier()
        with ExitStack() as es2:
            sb2 = lambda n, s, d: es2.enter_context(nc.sbuf_tensor(n, s, d))
            Wo = [sb2(f"cWo{i}", [128, 16, 512], BF16) for i in range(2)]
            xs = [sb2(f"cxs{i}", [128, 512], F32) for i in range(3)]
            x1s = [sb2(f"cx1{i}", [128, 512], F32) for i in range(3)]
            WR, XR, OR_ = Ring(Wo), Ring(xs), Ring(x1s)
            for nb in range(4):
                cs = slice(nb * 512, (nb + 1) * 512)
                wi, wo, wfree = WR.next()
                tw = load_w_bf16(k, wo[:], I["w_out"][:, cs], wfree, f"cwo{wi}")
                last = None
                for tt in range(16):
                    rs = slice(tt * 128, (tt + 1) * 128)
                    xi, xb, xfree = XR.next()
                    tx = P.dma("sp", xb[:], I["x_own"][rs, cs], deps=xfree, sem=f"cx{xi}")
                    idx, ps, fr = k.psf.next()
                    for dt_ in range(16):
                        last = P.op("pe", (lambda e, ps=ps, wo=wo, dt_=dt_, rs=rs: e.matmul(ps[:], lhsT=mixedT[:, dt_, rs], rhs=wo[:, dt_, :],
                                                                                         start=(dt_ == 0), stop=(dt_ == 15))),
                                    deps=[tw] + fr, sem=("pe" if dt_ == 15 else None))
                    oi, ob, ofree = OR_.next()
                    ta = P.op("dve", (lambda e, ps=ps, xb=xb, ob=ob: e.tensor_tensor(out=ob[:], in0=ps[:], in1=xb[:], op=ALU.add)), deps=[last, tx] + ofree)
                    k.psf.release(idx, [ta])
                    XR.release(xi, [ta])
                    do = P.dma("sp", S["X1"][rs, cs], ob[:], deps=[ta], sem=f"co{oi}")
                    OR_.release(oi, [do])
                WR.release(wi, [last])
        P.barrier()
    with ExitStack() as es:
        hn2T = es.enter_context(nc.sbuf_tensor("chn2T", [128, 16, OWN], BF16))
        norm_stage(k, S["X1"], [(s_, s_ * 128) for s_ in range(16)], "n2_bc", hn2T, "c4")
        P.dma("sp", S["HN2T"], hn2T[:], sem="chn")
        P.barrier()


def phase_D(k):
    nc, P, I, S = k.nc, k.P, k.I, k.S
    for TB in range(2):
        _d_block(k, TB)
    with ExitStack() as es:
        sb = lambda n, s, d: es.enter_context(nc.sbuf_tensor(n, s, d))
        nw = sb("dnw", [128, D], F32)
        xt = [sb(f"dx{i}", [128, D], F32) for i in range(4)]
        ot = [sb(f"do{i}", [128, D], F32) for i in range(3)]
        junk = sb("dj", [128, D], BF16)
        st = [sb(f"dst{i}", [128, 4], F32) for i in range(4)]
        tnw = P.dma("sp", nw[:], I["nf_bc"], sem="dnw")
        XR, OR_ = Ring(xt), Ring(ot)
        for tt in range(16):
            rs = slice(tt * 128, (tt + 1) * 128)
            xi, xb, xfree = XR.next()
            tx = P.dma("sp", xb[:], S["X2"][rs, :], deps=xfree, sem=f"dx{xi}")
            stt = st[xi]
            t1 = P.op("act", (lambda e, xb=xb, stt=stt: e.activation(out=junk[:], in_=xb[:], func=AF.Square, accum_out=stt[:, 0:1])), deps=[tx])
            t2 = P.op("act", (lambda e, stt=stt: e.activation(out=stt[:, 1:2], in_=stt[:, 0:1], func=AF.Sqrt, scale=1.0 / D, bias=k.epst[:])), deps=[t1])
            t3 = P.op("dve", (lambda e, stt=stt: e.reciprocal(out=stt[:, 2:3], in_=stt[:, 1:2])), deps=[t2])
            oi, ob, ofree = OR_.next()
            t4 = P.op("dve", (lambda e, xb=xb, ob=ob, stt=stt: e.scalar_tensor_tensor(out=ob[:], in0=xb[:], scalar=stt[:, 2:3], in1=nw[:],
                                                                                   op0=ALU.mult, op1=ALU.mult)), deps=[t3, tnw] + ofree)
            XR.release(xi, [t4])
            do = P.dma("sp", k.out[rs, :], ob[:], deps=[t4], sem=f"dout{oi}")
            OR_.release(oi, [do])
    P.barrier()


def _d_block(k, TB):
    nc, P, I, S = k.nc, k.P, k.I, k.S
    NT = 1024
    T0 = TB * NT
    with ExitStack() as es:
        sb = lambda n, s, d: es.enter_context(nc.sbuf_tensor(n, s, d))
        actT = sb(f"dact{TB}", [128, NFT, NT], BF16)
        with ExitStack() as es2:
            sb2 = lambda n, s, d: es2.enter_context(nc.sbuf_tensor(n, s, d))
            hn2 = sb2(f"dhn{TB}", [128, 16, NT], BF16)
            Wg = [sb2(f"dWg{TB}{i}", [128, 16, 256], BF16) for i in range(2)]
            Wu = [sb2(f"dWu{TB}{i}", [128, 16, 256], BF16) for i in range(2)]
            sg = [sb2(f"dsg{TB}{i}", [128, 512], F32) for i in range(3)]
            th = P.dma("sp", hn2[:], S["HN2T"][:, :, T0:T0 + NT], sem="dhn")
            WGR, WUR, SGR = Ring(Wg), Ring(Wu), Ring(sg)
            for fg in range(22):
                gi_, wg, gfree = WGR.next()
                tg = load_w_bf16(k, wg[:], I["w_gu"][:, fg * 256:(fg + 1) * 256], gfree, f"dwg{gi_}")
                ui_, wu, ufree = WUR.next()
                tu = load_w_bf16(k, wu[:], I["w_gu"][:, FF + fg * 256:FF + (fg + 1) * 256], ufree, f"dwu{ui_}")
                lastg = lastu = None
                for m in range(2):
                    ft = fg * 2 + m
                    for tb in range(2):
                        ts_ = slice(tb * 512, (tb + 1) * 512)
                        ig_, pg, fgr = k.psf.next()
                        for kc in range(16):
                            lastg = P.op("pe", (lambda e, pg=pg, wg=wg, kc=kc, m=m, ts_=ts_: e.matmul(
                                pg[:], lhsT=wg[:, kc, m * 128:(m + 1) * 128], rhs=hn2[:, kc, ts_], start=(kc == 0), stop=(kc == 15))),
                                deps=[tg, th] + fgr, sem=("pe" if kc == 15 else None))
                        iu_, pu, fur = k.psf.next()
                        for kc in range(16):
                            lastu = P.op("pe", (lambda e, pu=pu, wu=wu, kc=kc, m=m, ts_=ts_: e.matmul(
                                pu[:], lhsT=wu[:, kc, m * 128:(m + 1) * 128], rhs=hn2[:, kc, ts_], start=(kc == 0), stop=(kc == 15))),
                                deps=[tu] + fur, sem=("pe" if kc == 15 else None))
                        si, sgb, sfree = SGR.next()
                        a1 = P.op("act", (lambda e, pg=pg, sgb=sgb: e.activation(out=sgb[:], in_=pg[:], func=AF.Silu)), deps=[lastg] + sfree)
                        k.psf.release(ig_, [a1])
                        d1 = P.op("dve", (lambda e, pu=pu, sgb=sgb, ft=ft, ts_=ts_: e.tensor_tensor(out=actT[:, ft, ts_], in0=pu[:], in1=sgb[:], op=ALU.mult)),
                                  deps=[lastu, a1])
                        k.psf.release(iu_, [d1])
                        SGR.release(si, [d1])
                WGR.release(gi_, [lastg])
                WUR.release(ui_, [lastu])
        P.barrier()
        with ExitStack() as es2:
            sb2 = lambda n, s, d: es2.enter_context(nc.sbuf_tensor(n, s, d))
            Wd = [sb2(f"dWd{TB}{i}", [128, NFT, 512], BF16) for i in range(2)]
            xs = [sb2(f"dxs{TB}{i}", [128, 512], F32) for i in range(3)]
            os_ = [sb2(f"dos{TB}{i}", [128, 512], F32) for i in range(3)]
            WR, XR, OR_ = Ring(Wd), Ring(xs), Ring(os_)
            for nb in range(4):
                cs = slice(nb * 512, (nb + 1) * 512)
                wi, wd, wfree = WR.next()
                tws = []
                for q4 in range(4):
                    fsl = slice(q4 * 11, (q4 + 1) * 11)
                    tws.append(P.dma("pool", wd[:, fsl, :], I["w_dn"][q4 * 11 * 128:(q4 + 1) * 11 * 128, cs].rearrange("(ft p) n -> p ft n", p=128),
                                     deps=wfree, sem=f"dwd{wi}"))
                tw = tws[-1]
                last = None
                for tt in range(NT // 128):
                    rs = slice(T0 + tt * 128, T0 + (tt + 1) * 128)
                    xi, xb, xfree = XR.next()
                    tx = P.dma("sp", xb[:], S["X1"][rs, cs], deps=xfree, sem=f"dxs{xi}")
                    idx, ps, fr = k.psf.next()
                    for ft in range(NFT):
                        last = P.op("pe", (lambda e, ps=ps, wd=wd, ft=ft, tt=tt: e.matmul(ps[:], lhsT=actT[:, ft, tt * 128:(tt + 1) * 128], rhs=wd[:, ft, :],
                                                                                       start=(ft == 0), stop=(ft == NFT - 1))),
                                    deps=[tw] + fr, sem=("pe" if ft == NFT - 1 else None))
                    oi, ob, ofree = OR_.next()
                    ta = P.op("dve", (lambda e, ps=ps, xb=xb, ob=ob: e.tensor_tensor(out=ob[:], in0=ps[:], in1=xb[:], op=ALU.add)), deps=[last, tx] + ofree)
                    k.psf.release(idx, [ta])
                    XR.release(xi, [ta])
                    do = P.dma("sp", S["X2"][rs, cs], ob[:], deps=[ta], sem=f"dos{oi}")
                    OR_.release(oi, [do])
                WR.release(wi, [last])
        P.barrier()


def make_in_maps(inputs, names=None):
    c = host_consts()
    f32 = lambda a: np.ascontiguousarray(np.asarray(a, dtype=np.float32))
    x = np.asarray(inputs["x"], np.float32)
    w_in = np.asarray(inputs["w_in"], np.float32)[0]
    conv_w = np.asarray(inputs["conv_w"], np.float32)[0]
    conv_b = np.asarray(inputs["conv_b"], np.float32)[0]
    w3 = np.asarray(inputs["filt_w3"], np.float32)[0]
    hbias = np.asarray(inputs["hyena_bias"], np.float32)[0]
    gb = np.asarray(inputs["mlstm_gate_bias"], np.float32)[0]
    shared = {
        "w_mg": f32(w_in[:, 7184:11280]),
        "f_w1": f32(inputs["filt_w1"][0]), "f_b1": f32(inputs["filt_b1"][0][:, None]), "f_f1": f32(inputs["filt_freq1"][0][:, None]),
        "f_w2": f32(inputs["filt_w2"][0]), "f_b2": f32(inputs["filt_b2"][0][:, None]), "f_f2": f32(inputs["filt_freq2"][0][:, None]),
        "w_ba": f32(inputs["w_branch_a"][0]), "w_bb": f32(inputs["w_branch_b"][0]), "w_out": f32(inputs["w_out"][0]),
        "w_gu": f32(inputs["w_gate_up"][0]), "w_dn": f32(inputs["w_down"][0]),
        "n1_bc": f32(np.broadcast_to(np.asarray(inputs["norm1_w"], np.float32)[0], (128, D))),
        "n2_bc": f32(np.broadcast_to(np.asarray(inputs["norm2_w"], np.float32)[0], (128, D))),
        "nf_bc": f32(np.broadcast_to(np.asarray(inputs["norm_f_w"], np.float32), (128, D))),
        "ident": c["ident"], "identf": c["identf"], "FA64": c["FA64"], "MB": c["MB"], "IB": c["IB"], "Hpad": c["Hpad"],
        "zzT": c["zzT"], "negtau": c["negtau"], "maskf": c["maskf"], "maskb": c["maskb"], "resetmask": c["resetmask"],
    }
    per_half = []
    for h in range(2):
        hs = slice(h * 512, (h + 1) * 512)
        m = {}
        m["w_hy"] = f32(np.concatenate([w_in[:, 0:1024][:, hs], w_in[:, 1024:2048][:, hs], w_in[:, 2048:3072][:, hs]], 1))
        m["w_qk"] = f32(np.concatenate([w_in[:, 3072:4096][:, hs], w_in[:, 4096:5120][:, hs]], 1))
        m["w_vo"] = f32(np.concatenate([w_in[:, 5120:6144][:, hs], w_in[:, 6144:7168][:, hs]], 1))
        wgi = np.zeros((D, 64), np.float32)
        wgf = np.zeros((D, 64), np.float32)
        gbias = np.zeros((64, 2), np.float32)
        for hl in range(2):
            head = 2 * h + hl
            wgi[:, hl] = w_in[:, 7168 + 0 * 4 + head]
            wgf[:, hl] = w_in[:, 7168 + 1 * 4 + head]
            wgi[:, 32 + hl] = w_in[:, 7168 + 2 * 4 + head]
            wgf[:, 32 + hl] = w_in[:, 7168 + 3 * 4 + head]
            gbias[hl, 0] = gb[0, head]
            gbias[hl, 1] = gb[1, head]
            gbias[32 + hl, 0] = gb[2, head]
            gbias[32 + hl, 1] = gb[3, head]
        m["w_gi"], m["w_gf"], m["gbias"] = wgi, wgf, gbias
        cwb = np.zeros((128, 20, 4), np.float32)
        for tile in range(20):
            if tile < 12:
                col0 = (tile // 4) * 1024 + h * 512 + (tile % 4) * 128
            else:
                t2 = tile - 12
                col0 = 3072 + (t2 // 4) * 1024 + h * 512 + (t2 % 4) * 128
            cwb[:, tile, 0:3] = conv_w[:, col0:col0 + 128].T
            cwb[:, tile, 3] = conv_b[col0:col0 + 128]
        m["cwb"] = cwb
        m["f_w3"] = f32(np.concatenate([w3[:, o * 2048 + d_ * 1024 + h * 512: o * 2048 + d_ * 1024 + (h + 1) * 512]
                                        for o in range(2) for d_ in range(2)], 1))
        m["hb_bc"] = f32(np.broadcast_to(hbias[:, None, hs], (2, 128, 512)))
        m["absd_bc"] = f32(np.broadcast_to(c["absdelta"][hs], (64, 512)))
        per_half.append(m)
    maps = []
    for core in range(8):
        b, h = core // 2, core % 2
        m = dict(shared)
        m.update(per_half[h])
        m["x"] = f32(x[b])
        m["x_own"] = f32(x[b, h * OWN:(h + 1) * OWN])
        if names is not None:
            m = {kk: v for kk, v in m.items() if kk in names}
        maps.append(m)
    return maps


_NC_CACHE = {}


def kernel(**inputs):
    if "nc" not in _NC_CACHE:
        _NC_CACHE["nc"] = build()
    nc, names = _NC_CACHE["nc"]
    maps = make_in_maps(inputs, names)
    res = run_bass_kernel_spmd(nc, maps, core_ids=list(range(8)))
    out = np.zeros((4, L, D), np.float32)
    for core in range(8):
        b, h = core // 2, core % 2
        out[b, h * OWN:(h + 1) * OWN] = res.results[core]["out"]
    return out
```
